# Optimizing a Trainium2 kernel written in Bass

```python
import math
import jax, jax.numpy as jnp
from jax import lax
import numpy as np

D_MODEL = 2048
BATCH = 8
SEQ = 2048
DEPTH = 4
DEC_BATCH = 16
DEC_SEQ = 2048
PAST_LEN = 128

D_MIX = D_MODEL
POOL_DIM = D_MIX // 4
POOL_WINDOWS = (2, 4, 8, 16)
POOL_GROUP = POOL_DIM // len(POOL_WINDOWS)
MLA_HEADS = 8
QK_NOPE = 128
QK_ROPE = 64
QK_HEAD = QK_NOPE + QK_ROPE
V_HEAD = 128
MLA_DIM = MLA_HEADS * V_HEAD
Q_LORA = D_MODEL // 4
KV_LORA = D_MODEL // 8
ROPE_THETA = 10000.0
Q_BLOCK = 128
HY_DIM = D_MIX - POOL_DIM - MLA_DIM
HY_ORDER = 2
N_BANDS = 8
N_POS_FEAT = 1 + 2 * N_BANDS
FILT_HIDDEN = 64
HY_FAST_DECAY = 0.3
HY_SLOW_DECAY = 1.5
HY_DECAY_TARGET = 1e-2
D_FF = 5632
NORM_EPS = 1e-6
D_IN = POOL_DIM + Q_LORA + KV_LORA + QK_ROPE + (HY_ORDER + 1) * HY_DIM

kernel_name = 'hybrid_pool_mla_hyena_encoder'


def _rms_norm(x, g):
    xf = x.astype(jnp.float32)
    y = xf * lax.rsqrt(jnp.mean(xf * xf, axis=-1, keepdims=True) + NORM_EPS)
    return (y * g.astype(jnp.float32)).astype(x.dtype)


def _dwconv3(x, w, b):
    xp = jnp.pad(x, ((0, 0), (1, 1), (0, 0)))
    return xp[:, :-2] * w[0] + xp[:, 1:-1] * w[1] + xp[:, 2:] * w[2] + b


def _pool_mixer(u, pool_w, pool_scale):
    B, L, _ = u.shape
    uf = u.astype(jnp.float32)
    csum = jnp.concatenate([jnp.zeros((B, 1, POOL_DIM), jnp.float32), jnp.cumsum(uf, axis=1)], axis=1)
    pos = jnp.arange(L)
    outs = []
    for g, w in enumerate(POOL_WINDOWS):
        half = w // 2
        lo = jnp.maximum(pos - half, 0)
        hi = jnp.minimum(pos + half, L)
        sl = slice(g * POOL_GROUP, (g + 1) * POOL_GROUP)
        cg = csum[..., sl]
        mean = (cg[:, hi] - cg[:, lo]) / (hi - lo).astype(jnp.float32)[:, None]
        pooled = (mean - uf[..., sl]).astype(u.dtype)
        outs.append(pooled @ pool_w[g])
    return jnp.concatenate(outs, axis=-1) * pool_scale


def _rope(x, L):
    freqs = ROPE_THETA ** (-jnp.arange(0, QK_ROPE, 2, dtype=jnp.float32) / QK_ROPE)
    ang = jnp.arange(L, dtype=jnp.float32)[:, None] * freqs[None, :]
    cos = jnp.cos(ang)[None, :, None, :]
    sin = jnp.sin(ang)[None, :, None, :]
    xf = x.astype(jnp.float32)
    x1, x2 = xf[..., :QK_ROPE // 2], xf[..., QK_ROPE // 2:]
    return jnp.concatenate([x1 * cos - x2 * sin, x1 * sin + x2 * cos], axis=-1).astype(x.dtype)


def _mla(c_q, c_kv, k_pe, q_norm_g, w_uq, kv_norm_g, w_ukv, qh_g, kh_g):
    B, L, _ = c_q.shape
    q = (_rms_norm(c_q, q_norm_g) @ w_uq).reshape(B, L, MLA_HEADS, QK_HEAD)
    kv = (_rms_norm(c_kv, kv_norm_g) @ w_ukv).reshape(B, L, MLA_HEADS, QK_NOPE + V_HEAD)
    k_nope, v = kv[..., :QK_NOPE], kv[..., QK_NOPE:]
    k = jnp.concatenate([k_nope, jnp.broadcast_to(k_pe[:, :, None, :], (B, L, MLA_HEADS, QK_ROPE))], axis=-1)
    q = _rms_norm(q, qh_g)
    k = _rms_norm(k, kh_g)
    q = jnp.concatenate([q[..., :QK_NOPE], _rope(q[..., QK_NOPE:], L)], axis=-1)
    k = jnp.concatenate([k[..., :QK_NOPE], _rope(k[..., QK_NOPE:], L)], axis=-1)
    scale = QK_HEAD ** -0.5
    nblk = L // Q_BLOCK
    qb = q.reshape(B, nblk, Q_BLOCK, MLA_HEADS, QK_HEAD).transpose(1, 0, 2, 3, 4)

    def attend(q_blk):
        s = jnp.einsum('bqhd,bkhd->bhqk', q_blk, k, preferred_element_type=jnp.float32) * scale
        p = jax.nn.softmax(s, axis=-1).astype(v.dtype)
        return jnp.einsum('bhqk,bkhd->bqhd', p, v)

    o = lax.map(attend, qb)
    return o.transpose(1, 0, 2, 3, 4).reshape(B, L, MLA_DIM)


def _hyena_filter(L, w1, b1, freq, w2, b2, w3, b3):
    f32 = jnp.float32
    t = jnp.linspace(0.0, 1.0, L, dtype=f32)[:, None]
    w = 2.0 * math.pi * jnp.arange(L, dtype=f32)[:, None] / L
    bands = jnp.linspace(1e-4, N_BANDS - 1, N_BANDS, dtype=f32)[None, :]
    z = jnp.concatenate([t, jnp.cos(bands * w), -jnp.sin(bands * w)], axis=-1)
    fr = freq.astype(f32)
    h = jnp.sin(fr * (z @ w1.astype(f32) + b1.astype(f32)))
    h = jnp.sin(fr * (h @ w2.astype(f32) + b2.astype(f32)))
    h = (h @ w3.astype(f32) + b3.astype(f32)).reshape(L, 2, HY_DIM)
    max_decay = math.log(HY_DECAY_TARGET) / HY_FAST_DECAY
    min_decay = math.log(HY_DECAY_TARGET) / HY_SLOW_DECAY
    deltas = jnp.linspace(min_decay, max_decay, HY_DIM, dtype=f32)
    h = h * jnp.exp(-t * jnp.abs(deltas))[:, None, :]
    return jnp.concatenate([h[:, 0], jnp.zeros((1, HY_DIM), f32), h[:0:-1, 1]], axis=0)


def _hyena(u, conv_w, conv_b, bias, w1, b1, freq, w2, b2, w3, b3):
    B, L, _ = u.shape
    u = _dwconv3(u, conv_w, conv_b)
    x0, x1, v = u[..., :HY_DIM], u[..., HY_DIM:2 * HY_DIM], u[..., 2 * HY_DIM:]
    z = (v * x1).astype(jnp.float32)
    kern = _hyena_filter(L, w1, b1, freq, w2, b2, w3, b3)
    K = jnp.fft.rfft(kern, n=2 * L, axis=0)
    Z = jnp.fft.rfft(z, n=2 * L, axis=1)
    y = jnp.fft.irfft(Z * K[None], n=2 * L, axis=1)[:, :L] + z * bias.astype(jnp.float32)
    return (y * x0.astype(jnp.float32)).astype(u.dtype)


def _layer(x, attn_norm_g, w_in, pool_w, pool_scale, mla_q_norm_g, mla_w_uq, mla_kv_norm_g, mla_w_ukv,
           mla_q_head_norm_g, mla_k_head_norm_g, hy_conv_w, hy_conv_b, hy_filt_w1, hy_filt_b1, hy_filt_freq,
           hy_filt_w2, hy_filt_b2, hy_filt_w3, hy_filt_b3, hy_bias, grp_norm_g, w_out, ffn_norm_g, ffn_w_up,
           ffn_conv_w, ffn_conv_b, ffn_w_down):
    h = _rms_norm(x, attn_norm_g)
    proj = h @ w_in
    o0 = POOL_DIM
    o1 = o0 + Q_LORA
    o2 = o1 + KV_LORA
    o3 = o2 + QK_ROPE
    y_a = _pool_mixer(proj[..., :o0], pool_w, pool_scale)
    y_b = _mla(proj[..., o0:o1], proj[..., o1:o2], proj[..., o2:o3], mla_q_norm_g, mla_w_uq,
               mla_kv_norm_g, mla_w_ukv, mla_q_head_norm_g, mla_k_head_norm_g)
    y_c = _hyena(proj[..., o3:], hy_conv_w, hy_conv_b, hy_bias, hy_filt_w1, hy_filt_b1, hy_filt_freq,
                 hy_filt_w2, hy_filt_b2, hy_filt_w3, hy_filt_b3)
    mixed = jnp.concatenate([
        _rms_norm(y_a, grp_norm_g[:POOL_DIM]),
        _rms_norm(y_b, grp_norm_g[POOL_DIM:POOL_DIM + MLA_DIM]),
        _rms_norm(y_c, grp_norm_g[POOL_DIM + MLA_DIM:]),
    ], axis=-1)
    x = x + (mixed @ w_out).astype(x.dtype)
    h = _rms_norm(x, ffn_norm_g)
    up = h @ ffn_w_up
    gate = _dwconv3(up[..., :D_FF], ffn_conv_w, ffn_conv_b)
    act = jax.nn.silu(gate) * up[..., D_FF:]
    return x + (act @ ffn_w_down).astype(x.dtype)


def setup_inputs(seed: int = 0) -> dict:
    key = jax.random.key(seed)
    k = jax.random.split(key, 32)

    def nrm(i, shape, scale):
        return jax.random.normal(k[i], shape, jnp.float32) * scale

    def gain(i, shape):
        return 1.0 + nrm(i, shape, 0.02)

    L_ = DEPTH
    return {
        'x_prompt': nrm(0, (BATCH, SEQ, D_MODEL), 1.0),
        'x_sample': nrm(1, (DEC_BATCH, DEC_SEQ, D_MODEL), 1.0),
        'attn_norm_g': gain(2, (L_, D_MODEL)),
        'w_in': nrm(3, (L_, D_MODEL, D_IN), D_MODEL ** -0.5),
        'pool_w': nrm(4, (L_, len(POOL_WINDOWS), POOL_GROUP, POOL_GROUP), POOL_GROUP ** -0.5),
        'pool_scale': 1.0 + nrm(5, (L_, POOL_DIM), 0.1),
        'mla_q_norm_g': gain(6, (L_, Q_LORA)),
        'mla_w_uq': nrm(7, (L_, Q_LORA, MLA_HEADS * QK_HEAD), Q_LORA ** -0.5),
        'mla_kv_norm_g': gain(8, (L_, KV_LORA)),
        'mla_w_ukv': nrm(9, (L_, KV_LORA, MLA_HEADS * (QK_NOPE + V_HEAD)), KV_LORA ** -0.5),
        'mla_q_head_norm_g': gain(10, (L_, QK_HEAD)),
        'mla_k_head_norm_g': gain(11, (L_, QK_HEAD)),
        'hy_conv_w': nrm(12, (L_, 3, (HY_ORDER + 1) * HY_DIM), 3 ** -0.5),
        'hy_conv_b': nrm(13, (L_, (HY_ORDER + 1) * HY_DIM), 0.02),
        'hy_filt_w1': nrm(14, (L_, N_POS_FEAT, FILT_HIDDEN), N_POS_FEAT ** -0.5),
        'hy_filt_b1': nrm(15, (L_, FILT_HIDDEN), 0.1),
        'hy_filt_freq': 1.0 + nrm(16, (L_, FILT_HIDDEN), 0.1),
        'hy_filt_w2': nrm(17, (L_, FILT_HIDDEN, FILT_HIDDEN), FILT_HIDDEN ** -0.5),
        'hy_filt_b2': nrm(18, (L_, FILT_HIDDEN), 0.1),
        'hy_filt_w3': nrm(19, (L_, FILT_HIDDEN, 2 * HY_DIM), 0.1 * FILT_HIDDEN ** -0.5),
        'hy_filt_b3': nrm(20, (L_, 2 * HY_DIM), 0.01),
        'hy_bias': nrm(21, (L_, HY_DIM), 0.5),
        'grp_norm_g': gain(22, (L_, D_MIX)),
        'w_out': nrm(23, (L_, D_MIX, D_MODEL), D_MIX ** -0.5),
        'ffn_norm_g': gain(24, (L_, D_MODEL)),
        'ffn_w_up': nrm(25, (L_, D_MODEL, 2 * D_FF), D_MODEL ** -0.5),
        'ffn_conv_w': nrm(26, (L_, 3, D_FF), 3 ** -0.5),
        'ffn_conv_b': nrm(27, (L_, D_FF), 0.02),
        'ffn_w_down': nrm(28, (L_, D_FF, D_MODEL), D_FF ** -0.5),
    }


def reference(x_prompt, x_sample, attn_norm_g, w_in, pool_w, pool_scale, mla_q_norm_g, mla_w_uq,
              mla_kv_norm_g, mla_w_ukv, mla_q_head_norm_g, mla_k_head_norm_g, hy_conv_w, hy_conv_b,
              hy_filt_w1, hy_filt_b1, hy_filt_freq, hy_filt_w2, hy_filt_b2, hy_filt_w3, hy_filt_b3, hy_bias,
              grp_norm_g, w_out, ffn_norm_g, ffn_w_up, ffn_conv_w, ffn_conv_b, ffn_w_down):
    layer_params = (attn_norm_g, w_in, pool_w, pool_scale, mla_q_norm_g, mla_w_uq, mla_kv_norm_g, mla_w_ukv,
                    mla_q_head_norm_g, mla_k_head_norm_g, hy_conv_w, hy_conv_b, hy_filt_w1, hy_filt_b1,
                    hy_filt_freq, hy_filt_w2, hy_filt_b2, hy_filt_w3, hy_filt_b3, hy_bias, grp_norm_g, w_out,
                    ffn_norm_g, ffn_w_up, ffn_conv_w, ffn_conv_b, ffn_w_down)

    def run(x):
        for l in range(DEPTH):
            x = _layer(x, *[p[l] for p in layer_params])
        return x

    y_prompt = run(x_prompt)
    y_sample = run(x_sample)
    return (y_prompt, y_sample)
```

```python
import math
from contextlib import ExitStack
import numpy as np
import ml_dtypes
import concourse.bass as bass
import concourse.mybir as mybir
from concourse.bass_utils import run_bass_kernel_spmd

F32 = mybir.dt.float32
BF16 = mybir.dt.bfloat16
AF = mybir.ActivationFunctionType
ALU = mybir.AluOpType

D = 2048
DC = 16
POOL_WINDOWS = (2, 4, 8, 16)
H = 8
NPF = 17
FH = 64
HY = 512
EPS = 1e-6
TT = 512
N_CORES = 8


class Sem:
    def __init__(self, handle):
        self.h = handle
        self.cnt = 0


class Eng:
    def __init__(self, name, e, sem):
        self.name = name
        self.e = e
        self.sem = sem
        self.seen = {}


class Buf:
    def __init__(self, name, t=None):
        self.name = name
        self.t = t
        self.lw = None
        self.readers = []
        self.sems = {}

    def __getitem__(self, idx):
        return self.t[idx]


class Kern:
    def __init__(self, nc, es):
        self.nc = nc
        self.es = es
        mk = lambda n: Sem(es.enter_context(nc.semaphore(n)))
        self.pe = Eng("pe", nc.tensor, mk("s_pe"))
        self.act = Eng("act", nc.scalar, mk("s_act"))
        self.dve = Eng("dve", nc.vector, mk("s_dve"))
        self.pool = Eng("pool", nc.gpsimd, mk("s_pool"))
        self.sp = Eng("sp", nc.sync, mk("s_sp"))
        self.engs = [self.pe, self.act, self.dve, self.pool, self.sp]
        self.dfree = {"sp": [mk(f"s_h{i}") for i in range(46)], "pool": [mk(f"s_w{i}") for i in range(40)]}
        self.phase_sems = []
        self.phase_bufs = []
        self.psum = []
        for i in range(8):
            t = es.enter_context(nc.psum_tensor(f"psb{i}", [128, 512], F32))
            self.psum.append(Buf(f"psb{i}", t))
        self.ps_avail = list(self.psum)
        self.ps_i = 0
        self.n_ins = 0

    def sb(self, st, name, shape, dt):
        self.uid = getattr(self, "uid", 0) + 1
        name = f"{name}_{self.uid}"
        t = st.enter_context(self.nc.sbuf_tensor(name, list(shape), dt))
        return Buf(name, t)

    def ps(self):
        b = self.ps_avail.pop(0)
        self.ps_avail.append(b)
        return b

    def ps_hold(self):
        return self.ps_avail.pop(0)

    def ps_release(self, b):
        self.ps_avail.append(b)

    def _dsem(self, b, kind, q):
        key = (kind, q.name)
        if key not in b.sems:
            pn = "pool" if q.name == "pool" else "sp"
            s = self.dfree[pn].pop()
            self.phase_sems.append((pn, s))
            self.phase_bufs.append(b)
            b.sems[key] = s
        return b.sems[key]

    def _wait(self, eng, deps):
        best = {}
        for d in deps:
            if d is None:
                continue
            s, v = d
            if s is eng.sem and eng is self.pe:
                continue
            if v > best.get(s, 0):
                best[s] = v
        for s, v in best.items():
            if eng.seen.get(s, 0) >= v:
                continue
            eng.e.wait_ge(s.h, v)
            eng.seen[s] = v

    def _deps(self, reads, writes):
        deps = []
        for b in reads:
            deps.append(b.lw)
        for b in writes:
            deps.append(b.lw)
            deps.extend(b.readers)
        return deps

    def op(self, eng, fn, reads=(), writes=()):
        self._wait(eng, self._deps(reads, writes))
        ins = fn()
        ins.then_inc(eng.sem.h, 1)
        eng.sem.cnt += 1
        tok = (eng.sem, eng.sem.cnt)
        for b in reads:
            b.readers.append(tok)
        for b in writes:
            b.lw = tok
            b.readers = []
        self.n_ins += 1

    def dma(self, q, pairs, reads=(), writes=()):
        self._wait(q, self._deps(reads, writes))
        if writes:
            s = self._dsem(writes[0], "w", q)
        else:
            s = self._dsem(reads[0], "r", q)
        for (o, i) in pairs:
            q.e.dma_start(out=o, in_=i).then_inc(s.h, 16)
            s.cnt += 16
        tok = (s, s.cnt)
        for b in reads:
            b.readers.append(tok)
        for b in writes:
            b.lw = tok
            b.readers = []
        self.n_ins += len(pairs)

    def barrier(self):
        allsems = [e.sem for e in self.engs] + [s for (_, s) in self.phase_sems]
        for e in self.engs:
            for s in allsems:
                if s is e.sem or s.cnt == 0:
                    continue
                if e.seen.get(s, 0) >= s.cnt:
                    continue
                e.e.wait_ge(s.h, s.cnt)
                e.seen[s] = s.cnt
        for b in self.psum:
            b.lw = None
            b.readers = []
        for (qn, s) in self.phase_sems:
            self.dfree[qn].append(s)
        self.phase_sems = []
        for b in self.phase_bufs:
            b.sems = {}
        self.phase_bufs = []

    def mm(self, out, lhsT, rhs, start, stop, reads, writes):
        self.op(self.pe, lambda: self.nc.tensor.matmul(out, lhsT, rhs, start=start, stop=stop), reads, writes)

    def actf(self, out, in_, func, reads, writes, bias=None, scale=None, eng=None):
        kw = {}
        if bias is not None:
            kw["bias"] = bias
        if scale is not None:
            kw["scale"] = scale
        self.op(self.act, lambda: self.nc.scalar.activation(out=out, in_=in_, func=func, **kw), reads, writes)

    def ts(self, eng, out, in0, s1, s2, op0, op1, reads, writes):
        if op1 is None:
            self.op(eng, lambda: eng.e.tensor_scalar(out, in0, s1, None, op0), reads, writes)
        else:
            self.op(eng, lambda: eng.e.tensor_scalar(out, in0, s1, s2, op0, op1), reads, writes)

    def tt(self, eng, out, in0, in1, op, reads, writes):
        self.op(eng, lambda: eng.e.tensor_tensor(out, in0, in1, op), reads, writes)

    def stt(self, eng, out, in0, scalar, in1, op0, op1, reads, writes):
        self.op(eng, lambda: eng.e.scalar_tensor_tensor(out, in0, scalar, in1, op0, op1), reads, writes)

    def copy(self, eng, out, in_, reads, writes):
        if eng is self.act:
            self.op(eng, lambda: self.nc.scalar.copy(out, in_), reads, writes)
        else:
            self.op(eng, lambda: eng.e.tensor_copy(out, in_), reads, writes)

    def memset(self, eng, ap, val, writes):
        self.op(eng, lambda: eng.e.memset(ap, val), (), writes)


def vec_layout(FC):
    o = {}
    o["attn"] = 0
    o["ffn"] = 16
    o["grp"] = 32
    o["ps"] = 48
    o["qn"] = 52
    o["kvn"] = 56
    o["qhA"] = 58
    o["qhB"] = 59
    o["khA"] = 60
    o["khB"] = 61
    o["hyw"] = 62
    o["hyb"] = 98
    o["hybias"] = 110
    o["fcw"] = 114
    o["fcb"] = 114 + 3 * FC
    o["NV"] = 114 + 4 * FC
    return o


def build(cfg):
    L = cfg["L"]
    NSEQ = cfg["NSEQ"]
    DEPTH = cfg["DEPTH"]
    DFF = cfg["DFF"]
    FC = DFF // 128
    NT = L // TT
    TC = L // 128
    VL = vec_layout(FC)
    NV = VL["NV"]
    dbg = cfg.get("debug", False)

    nc = bass.Bass("TRN2", target_bir_lowering=False)

    def din(name, shape, dt=F32):
        return nc.dram_tensor(name, list(shape), dt, kind="ExternalInput").ap()

    xT = din("xT", [NSEQ, D, L])
    w_in = din("w_in", [DEPTH, 23, 128, 16 * 128])
    w_uq = din("w_uq", [DEPTH, 16, 128, 4 * 128])
    w_ukn = din("w_ukn", [DEPTH, 8, 128, 2 * 128])
    w_ukv = din("w_ukv", [DEPTH, 128, 2 * 1024])
    pool_w = din("pool_w", [DEPTH, 128, 4 * 128])
    w_out = din("w_out", [DEPTH, 16, 128, 16 * 128])
    w_upg = din("w_upg", [DEPTH, FC, 128, 16 * 128])
    w_upv = din("w_upv", [DEPTH, FC, 128, 16 * 128])
    w_dn = din("w_dn", [DEPTH, 16, 128, FC * 128])
    vecs = din("vecs", [DEPTH, 128, NV])
    hf_w1 = din("hf_w1", [DEPTH, NPF, FH])
    hf_w2 = din("hf_w2", [DEPTH, FH, FH])
    hf_w3 = din("hf_w3", [DEPTH, FH, 2 * HY])
    hf_b3 = din("hf_b3", [DEPTH, 128, 2 * HY])
    hvec = din("hvec", [DEPTH, FH, 4])
    zfeatT = din("zfeatT", [NPF, L])
    decay = din("decay", [TC, 128, HY])
    dft_f = din("dft_f", [TC, 2, 128, TC * 128], BF16)
    dft_i = din("dft_i", [NT, 2, 128, TC * TT], BF16)
    ropeT = din("ropeT", [128, L])
    foldm = din("foldm", [128, 128], BF16)
    identm = din("identm", [128, 128], BF16)
    pcorr = din("pcorr", [128, 4 * 16])

    yT = nc.dram_tensor("yT", [NSEQ, D, L], F32, kind="ExternalOutput").ap()
    xs = nc.dram_tensor("xs_scr", [NSEQ, D, L], F32).ap()
    xmid = nc.dram_tensor("xmid_scr", [NSEQ, D, L], F32).ap()
    Ysc = nc.dram_tensor("y_scr", [NSEQ, 16, 128, L], BF16).ap()
    Kf = nc.dram_tensor("kf_scr", [2, TC, 128, HY], F32).ap()
    Usc = nc.dram_tensor("u_scr", [12, 128, L], BF16).ap()
    Psc = nc.dram_tensor("p_scr", [4, 128, L], F32).ap()
    dbg_out = None
    if dbg:
        dbg_out = nc.dram_tensor("dbgY", [NSEQ, 16, 128, L], BF16, kind="ExternalOutput").ap()

    W32 = {"w_in": w_in, "w_uq": w_uq, "w_ukn": w_ukn, "w_ukv": w_ukv, "pool_w": pool_w, "w_out": w_out,
           "w_upg": w_upg, "w_upv": w_upv, "w_dn": w_dn}
    WB = {}
    for nm, ap_ in W32.items():
        shp = list(ap_.shape)[1:]
        WB[nm] = nc.dram_tensor("wb_" + nm, [2] + shp, BF16).ap()

    with ExitStack() as es:
        K = Kern(nc, es)
        pe, act, dve, pool, sp = K.pe, K.act, K.dve, K.pool, K.sp
        cvs = [Sem(es.enter_context(nc.semaphore(f"s_cv{i}"))) for i in range(2)]
        cv_target = {}

        def convert(l):
            par = l % 2
            sm = cvs[par]
            for nm, src in W32.items():
                s_l = src[l]
                d_l = WB[nm][par]
                if len(s_l.shape) == 2:
                    pool.e.dma_start(out=d_l, in_=s_l).then_inc(sm.h, 16)
                    sm.cnt += 16
                else:
                    n0 = s_l.shape[0]
                    step = 4
                    for i0 in range(0, n0, step):
                        i1 = min(n0, i0 + step)
                        pool.e.dma_start(out=d_l[i0:i1], in_=s_l[i0:i1]).then_inc(sm.h, 16)
                        sm.cnt += 16
            cv_target[l] = sm.cnt

        def wait_converted(l):
            sm = cvs[l % 2]
            for e in (sp, pool, act):
                e.e.wait_ge(sm.h, cv_target[l])

        convert(0)

        ones = K.sb(es, "ones", [128, 128], BF16)
        fold = K.sb(es, "fold", [128, 128], BF16)
        ident = K.sb(es, "ident", [128, 128], BF16)
        vec = K.sb(es, "vec", [128, NV], F32)
        corr = K.sb(es, "corr", [128, 64], F32)
        K.memset(dve, ones[:], 1.0, [ones])
        K.dma(sp, [(fold[:], foldm)], writes=[fold])
        K.dma(sp, [(ident[:], identm)], writes=[ident])
        K.dma(sp, [(corr[:], pcorr)], writes=[corr])

        def V(k):
            return vec[:, k:k + 1]

        cst = K.sb(es, "cst", [128, 4], F32)
        K.memset(dve, cst[:, 0:1], EPS, [cst])
        K.memset(dve, cst[:, 1:2], EPS * 192.0, [cst])

        def rstd_from_ss(dst, ss_ap, n, w, reads, writes, extra_scale=None):
            if extra_scale is None:
                K.actf(dst, ss_ap, AF.Ln, list(reads) + [cst], writes, bias=cst[:, 0:1], scale=1.0 / n)
            else:
                assert abs(extra_scale ** 2 - 1.0 / 192.0) < 1e-9
                K.actf(dst, ss_ap, AF.Ln, list(reads) + [cst], writes, bias=cst[:, 1:2], scale=192.0 / n)
            K.actf(dst, dst, AF.Exp, writes, writes, scale=-0.5)

        for l in range(DEPTH):
            Xin = xT if l == 0 else xs
            Xout = yT if l == DEPTH - 1 else xs
            K.dma(sp, [(vec[:], vecs[l])], writes=[vec])
            if l >= 1:
                wait_converted(l)
            if l + 1 < DEPTH:
                convert(l + 1)
            wb_in, wb_uq, wb_ukn, wb_ukv, wb_pool, wb_out, wb_upg, wb_upv, wb_dn = (
                WB[k][l % 2] for k in ("w_in", "w_uq", "w_ukn", "w_ukv", "pool_w", "w_out", "w_upg", "w_upv", "w_dn"))

            with ExitStack() as st:
                zf = K.sb(st, "zf", [NPF, L], F32)
                fw1 = K.sb(st, "fw1", [NPF, FH], F32)
                fw2 = K.sb(st, "fw2", [FH, FH], F32)
                fw3 = K.sb(st, "fw3", [FH, 2 * HY], F32)
                fb3 = K.sb(st, "fb3", [128, 2 * HY], F32)
                hv = K.sb(st, "hv", [FH, 4], F32)
                h1 = K.sb(st, "h1", [FH, L], F32)
                h2 = K.sb(st, "h2", [FH, L], F32)
                hsum = K.sb(st, "hsum", [128, TC, HY], BF16)
                hdif = K.sb(st, "hdif", [128, TC, HY], BF16)
                targ = K.sb(st, "targ", [FH, TT], F32)
                tsq = K.sb(st, "tsq", [FH, TT], F32)
                K.dma(sp, [(zf[:], zfeatT)], writes=[zf])
                K.dma(sp, [(fw1[:], hf_w1[l])], writes=[fw1])
                K.dma(sp, [(fw2[:], hf_w2[l])], writes=[fw2])
                K.dma(sp, [(fw3[:], hf_w3[l])], writes=[fw3])
                K.dma(sp, [(fb3[:], hf_b3[l])], writes=[fb3])
                K.dma(sp, [(hv[:], hvec[l])], writes=[hv])
                for (wsrc, bcol, src, dst) in ((fw1, 0, zf, h1), (fw2, 2, h1, h2)):
                    for n in range(NT):
                        cs = slice(n * TT, (n + 1) * TT)
                        p = K.ps()
                        K.mm(p[0:FH, :], wsrc[:], src[:, cs], True, True, [wsrc, src], [p])
                        K.ts(dve, targ[:], p[0:FH, :], hv[:, bcol:bcol + 1], hv[:, 1:2], ALU.add, ALU.mult,
                             [p, hv], [targ])
                        K.actf(targ[:], targ[:], AF.Sin, [targ], [targ], scale=1.0 / 9.0)
                        for rep in range(2):
                            K.tt(dve, tsq[:], targ[:], targ[:], ALU.mult, [targ], [tsq])
                            K.ts(dve, tsq[:], tsq[:], -4.0, 3.0, ALU.mult, ALU.add, [tsq], [tsq])
                            K.tt(dve, (dst[:, cs] if rep == 1 else targ[:]), tsq[:], targ[:], ALU.mult, [tsq, targ],
                                 [dst if rep == 1 else targ])
                with ExitStack() as st2:
                    dct = [K.sb(st2, f"dct{i}", [128, HY], F32) for i in range(2)]
                    hf = [K.sb(st2, f"hf{i}", [128, HY], F32) for i in range(2)]
                    hb = [K.sb(st2, f"hb{i}", [128, HY], F32) for i in range(2)]
                    for j in range(TC):
                        dc_ = dct[j % 2]
                        hf_ = hf[j % 2]
                        hb_ = hb[j % 2]
                        K.dma(sp, [(dc_[:], decay[j])], writes=[dc_])
                        pf = K.ps()
                        K.mm(pf[:], h2[:, j * 128:(j + 1) * 128], fw3[:, 0:HY], True, True, [h2, fw3], [pf])
                        pb = K.ps()
                        K.mm(pb[:], h2[:, j * 128:(j + 1) * 128], fw3[:, HY:2 * HY], True, True, [h2, fw3], [pb])
                        K.tt(dve, hf_[:], pf[:], fb3[:, 0:HY], ALU.add, [pf, fb3], [hf_])
                        K.tt(dve, hb_[:], pb[:], fb3[:, HY:2 * HY], ALU.add, [pb, fb3], [hb_])
                        K.tt(pool, hf_[:], hf_[:], dc_[:], ALU.mult, [hf_, dc_], [hf_])
                        K.tt(pool, hb_[:], hb_[:], dc_[:], ALU.mult, [hb_, dc_], [hb_])
                        if j == 0:
                            K.memset(pool, hb_[0:1, :], 0.0, [hb_])
                        K.tt(dve, hsum[:, j, :], hf_[:], hb_[:], ALU.add, [hf_, hb_], [hsum])
                        K.tt(pool, hdif[:, j, :], hb_[:], hf_[:], ALU.subtract, [hf_, hb_], [hdif])
                    wc = [K.sb(st2, f"fwc{i}", [128, TC * 128], BF16) for i in range(2)]
                    wsn = [K.sb(st2, f"fws{i}", [128, TC * 128], BF16) for i in range(2)]
                    ko = [K.sb(st2, f"ko{i}", [128, 2, HY], F32) for i in range(2)]
                    for fc in range(TC):
                        c_ = wc[fc % 2]
                        s_ = wsn[fc % 2]
                        o_ = ko[fc % 2]
                        K.dma(sp, [(c_[:], dft_f[fc, 0])], writes=[c_])
                        K.dma(sp, [(s_[:], dft_f[fc, 1])], writes=[s_])
                        pr = K.ps()
                        for j in range(TC):
                            K.mm(pr[:], c_[:, j * 128:(j + 1) * 128], hsum[:, j, :], j == 0, j == TC - 1,
                                 [c_, hsum], [pr])
                        pi_ = K.ps()
                        for j in range(TC):
                            K.mm(pi_[:], s_[:, j * 128:(j + 1) * 128], hdif[:, j, :], j == 0, j == TC - 1,
                                 [s_, hdif], [pi_])
                        K.copy(act, o_[:, 0, :], pr[:], [pr], [o_])
                        K.copy(act, o_[:, 1, :], pi_[:], [pi_], [o_])
                        K.dma(sp, [(Kf[0, fc], o_[:, 0, :]), (Kf[1, fc], o_[:, 1, :])], reads=[o_])
                    K.barrier()
            if l == 0:
                wait_converted(l)
            for s in range(NSEQ):
                with ExitStack() as sq_st:
                    cqn = K.sb(sq_st, "cqn", [128, 4, L], BF16)
                    ckvn = K.sb(sq_st, "ckvn", [128, 2, L], BF16)
                    sspe = K.sb(sq_st, "sspe", [128, L], F32)
                    K2u = K.sb(sq_st, "K2u", [128, L], BF16)
                    rope = K.sb(sq_st, "rope", [128, L], F32)
                    K.dma(sp, [(rope[:], ropeT)], writes=[rope])
                    with ExitStack() as st:
                        xt = K.sb(st, "xt", [128, DC, TT], F32)
                        hhs = [K.sb(st, f"hh{i}", [128, DC, TT], BF16) for i in range(2)]
                        sqs = [K.sb(st, f"sq{i}", [128, TT], BF16) for i in range(4)]
                        sqp = [K.sb(st, f"sqp{i}", [128, TT], BF16) for i in range(3)]
                        rstd = K.sb(st, "rstd", [128, TT], F32)
                        rstd2 = K.sb(st, "rstd2", [128, TT], F32)
                        wts = [K.sb(st, f"wi{i}", [128, 16 * 128], BF16) for i in range(3)]
                        craw = K.sb(st, "craw", [128, 4, TT], F32)
                        kB = K.sb(st, "kB", [128, TT], F32)
                        kR = K.sb(st, "kR", [128, TT], BF16)
                        pst = [K.sb(st, f"pst{i}", [128, TT], F32) for i in range(4)]
                        ust = [K.sb(st, f"ust{i}", [128, TT], BF16) for i in range(6)]
                        sqi = 0

                        def prepA(n):
                            hh_ = hhs[n % 2]
                            cs_ = slice(n * TT, (n + 1) * TT)
                            K.dma(sp, [(xt[:], Xin[s].rearrange("(c p) t -> p c t", p=128)[:, :, cs_])], writes=[xt])
                            yield
                            yield
                            ssb = K.ps_hold()
                            for c in range(DC):
                                q_ = sqp[c % 3]
                                K.actf(q_[:], xt[:, c, :], AF.Square, [xt], [q_])
                                yield
                                K.mm(ssb[:], ones[:], q_[:], c == 0, c == DC - 1, [ones, q_], [ssb])
                            yield
                            rstd_from_ss(rstd[:], ssb[:], D, None, [ssb], [rstd])
                            K.ps_release(ssb)
                            yield
                            for c in range(DC):
                                K.stt(dve, hh_[:, c, :], xt[:, c, :], V(VL["attn"] + c), rstd[:], ALU.mult, ALU.mult,
                                      [xt, vec, rstd], [hh_])
                                yield

                        for _ in prepA(0):
                            pass
                        for n in range(NT):
                            cs = slice(n * TT, (n + 1) * TT)
                            hh = hhs[n % 2]
                            bg = prepA(n + 1) if n + 1 < NT else None
                            ssq = None
                            deferred = []

                            def run_deferred():
                                while deferred:
                                    deferred.pop(0)()

                            for j in range(23):
                                w_ = wts[j % 3]
                                K.dma(pool, [(w_[:], wb_in[j])], writes=[w_])
                                p = K.ps()
                                for c in range(DC):
                                    K.mm(p[:], w_[:, c * 128:(c + 1) * 128], hh[:, c, :], c == 0, c == DC - 1,
                                         [w_, hh], [p])
                                run_deferred()
                                if j < 4:
                                    t_ = pst[j % 4]
                                    K.copy(act, t_[:], p[:], [p], [t_])
                                    K.dma(sp, [(Psc[j][:, cs], t_[:])], reads=[t_])
                                elif j < 10:
                                    grp0, ng, dstb, gcol = (4, 4, cqn, VL["qn"]) if j < 8 else (8, 2, ckvn, VL["kvn"])
                                    c_ = j - grp0
                                    K.copy(act, craw[:, c_, :], p[:], [p], [craw])
                                    q_ = sqs[sqi % 4]
                                    sqi += 1
                                    K.actf(q_[:], p[:], AF.Square, [p], [q_])

                                    def dq(c_=c_, ng=ng, dstb=dstb, gcol=gcol, q_=q_):
                                        nonlocal ssq
                                        if c_ == 0:
                                            ssq = K.ps_hold()
                                        K.mm(ssq[:], ones[:], q_[:], c_ == 0, c_ == ng - 1, [ones, q_], [ssq])
                                        if c_ == ng - 1:
                                            rstd_from_ss(rstd2[:], ssq[:], ng * 128, None, [ssq], [rstd2])
                                            K.ps_release(ssq)
                                            for cc in range(ng):
                                                K.stt(dve, dstb[:, cc, cs], craw[:, cc, :], V(gcol + cc), rstd2[:],
                                                      ALU.mult, ALU.mult, [craw, vec, rstd2], [dstb])

                                    deferred.append(dq)
                                elif j == 10:
                                    K.copy(act, kB[:], p[:], [p], [kB])
                                    q_ = sqs[sqi % 4]
                                    sqi += 1
                                    K.actf(q_[0:64, :], p[0:64, :], AF.Square, [p], [q_])
                                    K.stt(dve, kR[:], kB[:], V(VL["khB"]), rope[:, cs], ALU.mult, ALU.mult,
                                          [kB, vec, rope], [kR])

                                    def dk(q_=q_):
                                        p2 = K.ps()
                                        K.mm(p2[:], ones[0:64, :], q_[0:64, :], True, True, [ones, q_], [p2])
                                        K.copy(act, sspe[:, cs], p2[:], [p2], [sspe])
                                        p3 = K.ps()
                                        K.mm(p3[:], fold[:], kR[:], True, True, [fold, kR], [p3])
                                        K.copy(act, K2u[:, cs], p3[:], [p3], [K2u])

                                    deferred.append(dk)
                                else:
                                    t_ = ust[j % 6]
                                    K.copy(act, t_[:], p[:], [p], [t_])
                                    K.dma(sp, [(Usc[j - 11][:, cs], t_[:])], reads=[t_])
                                if bg is not None:
                                    next(bg, None)
                                    next(bg, None)
                            run_deferred()
                            if bg is not None:
                                for _ in bg:
                                    pass
                        K.barrier()
                    with ExitStack() as st:
                        Vt = K.sb(st, "Vt", [128, TC, 1024], BF16)
                        wv = K.sb(st, "wv", [128, 2 * 1024], BF16)
                        K.dma(pool, [(wv[:], wb_ukv)], writes=[wv])
                        for kt in range(TC):
                            for hf_ in range(2):
                                p = K.ps()
                                for c in range(2):
                                    K.mm(p[:], ckvn[:, c, kt * 128:(kt + 1) * 128],
                                         wv[:, c * 1024 + hf_ * 512:c * 1024 + (hf_ + 1) * 512], c == 0, c == 1,
                                         [ckvn, wv], [p])
                                K.copy(act if hf_ == 0 else dve, Vt[:, kt, hf_ * 512:(hf_ + 1) * 512], p[:], [p], [Vt])
                        wqA = [K.sb(st, f"wqA{i}", [128, 4 * 128], BF16) for i in range(2)]
                        wqB = [K.sb(st, f"wqB{i}", [128, 4 * 128], BF16) for i in range(2)]
                        wkn = [K.sb(st, f"wkn{i}", [128, 2 * 128], BF16) for i in range(2)]
                        Kn = [K.sb(st, f"Kn{i}", [128, L], BF16) for i in range(2)]
                        Kr = [K.sb(st, f"Kr{i}", [128, L], BF16) for i in range(2)]
                        Qn = [K.sb(st, f"Qn{i}", [128, TT], BF16) for i in range(2)]
                        Qr = [K.sb(st, f"Qr{i}", [128, TT], BF16) for i in range(2)]
                        qrawA = K.sb(st, "qrawA", [128, TT], F32)
                        qrawB = K.sb(st, "qrawB", [128, TT], F32)
                        qtmp = K.sb(st, "qtmp", [128, TT], F32)
                        kraw = [K.sb(st, f"kraw{i}", [128, TT], F32) for i in range(2)]
                        sqk = [K.sb(st, f"sqk{i}", [128, TT], BF16) for i in range(2)]
                        sqq = [K.sb(st, f"sqq{i}", [128, TT], BF16) for i in range(2)]
                        rk = K.sb(st, "rk", [128, TT], F32)
                        rq = K.sb(st, "rq", [128, TT], F32)
                        rl = K.sb(st, "rl", [128, TT], F32)
                        Pb = [K.sb(st, f"Pb{i}", [128, TT], BF16) for i in range(4)]
                        yst = [K.sb(st, f"ybst{i}", [128, L], BF16) for i in range(2)]

                        def load_head_w(h):
                            K.dma(pool, [(wqA[h % 2][:], wb_uq[2 * h])], writes=[wqA[h % 2]])
                            K.dma(pool, [(wqB[h % 2][:], wb_uq[2 * h + 1])], writes=[wqB[h % 2]])
                            K.dma(pool, [(wkn[h % 2][:], wb_ukn[h])], writes=[wkn[h % 2]])

                        def Kprep(h):
                            wK, Kn_, Kr_ = wkn[h % 2], Kn[h % 2], Kr[h % 2]
                            for n in range(NT):
                                cs_ = slice(n * TT, (n + 1) * TT)
                                kr_, q_ = kraw[n % 2], sqk[n % 2]
                                p = K.ps()
                                for c in range(2):
                                    K.mm(p[:], wK[:, c * 128:(c + 1) * 128], ckvn[:, c, cs_], c == 0, c == 1, [wK, ckvn], [p])
                                K.copy(dve, kr_[:], p[:], [p], [kr_])
                                K.tt(dve, q_[:], kr_[:], kr_[:], ALU.mult, [kr_], [q_])
                                yield
                                p2 = K.ps()
                                K.mm(p2[:], ones[:], q_[:], True, True, [ones, q_], [p2])
                                K.tt(dve, rk[:], p2[:], sspe[:, cs_], ALU.add, [p2, sspe], [rk])
                                yield
                                K.actf(rk[:], rk[:], AF.Ln, [rk, cst], [rk], bias=cst[:, 0:1], scale=1.0 / 192.0)
                                yield
                                K.actf(rk[:], rk[:], AF.Exp, [rk], [rk], scale=-0.5)
                                yield
                                K.stt(dve, Kn_[:, cs_], kr_[:], V(VL["khA"]), rk[:], ALU.mult, ALU.mult, [kr_, vec, rk], [Kn_])
                                K.tt(pool, Kr_[:, cs_], K2u[:, cs_], rk[:], ALU.mult, [K2u, rk], [Kr_])
                                yield

                        def Qprep(h, n):
                            k_ = h * NT + n
                            wA, wB, Qn_, Qr_ = wqA[h % 2], wqB[h % 2], Qn[k_ % 2], Qr[k_ % 2]
                            cs_ = slice(n * TT, (n + 1) * TT)
                            qa, qb = sqq[0], sqq[1]
                            pA = K.ps()
                            for c in range(4):
                                K.mm(pA[:], wA[:, c * 128:(c + 1) * 128], cqn[:, c, cs_], c == 0, c == 3, [wA, cqn], [pA])
                            K.copy(dve, qrawA[:], pA[:], [pA], [qrawA])
                            K.tt(dve, qa[:], qrawA[:], qrawA[:], ALU.mult, [qrawA], [qa])
                            yield
                            pB = K.ps()
                            for c in range(4):
                                K.mm(pB[:], wB[:, c * 128:(c + 1) * 128], cqn[:, c, cs_], c == 0, c == 3, [wB, cqn], [pB])
                            K.copy(dve, qrawB[:], pB[:], [pB], [qrawB])
                            K.tt(dve, qb[0:64, :], qrawB[0:64, :], qrawB[0:64, :], ALU.mult, [qrawB], [qb])
                            yield
                            p2 = K.ps()
                            K.mm(p2[:], ones[:], qa[:], True, False, [ones, qa], [p2])
                            K.mm(p2[:], ones[0:64, :], qb[0:64, :], False, True, [ones, qb], [p2])
                            K.copy(dve, rq[:], p2[:], [p2], [rq])
                            yield
                            K.actf(rq[:], rq[:], AF.Ln, [rq, cst], [rq], bias=cst[:, 1:2], scale=1.0)
                            yield
                            K.actf(rq[:], rq[:], AF.Exp, [rq], [rq], scale=-0.5)
                            yield
                            K.stt(dve, Qn_[:], qrawA[:], V(VL["qhA"]), rq[:], ALU.mult, ALU.mult, [qrawA, vec, rq], [Qn_])
                            K.stt(dve, qtmp[:], qrawB[:], V(VL["qhB"]), rq[:], ALU.mult, ALU.mult, [qrawB, vec, rq], [qtmp])
                            K.tt(pool, Qr_[:], qtmp[:], rope[:, cs_], ALU.mult, [qtmp, rope], [Qr_])
                            yield

                        pw = K.sb(st, "pw", [128, 4 * 128], BF16)
                        Up = K.sb(st, "Up", [128, L + 16], F32)
                        ta = K.sb(st, "ta", [128, L + 16], F32)
                        tb = K.sb(st, "tb", [128, L + 16], F32)
                        pooled = K.sb(st, "pooled", [128, L], BF16)
                        ypst = [K.sb(st, f"ypst{i}", [128, L], BF16) for i in range(2)]

                        def poolgen():
                            K.dma(pool, [(pw[:], wb_pool)], writes=[pw])
                            K.memset(dve, Up[:, 0:8], 0.0, [Up])
                            K.memset(dve, Up[:, L + 8:L + 16], 0.0, [Up])
                            for g in range(4):
                                w = POOL_WINDOWS[g]
                                half = w // 2
                                K.dma(sp, [(Up[:, 8:8 + L], Psc[g])], writes=[Up])
                                yield
                                if w == 2:
                                    K.tt(dve, ta[:, 0:L], Up[:, 7:7 + L], Up[:, 8:8 + L], ALU.add, [Up], [ta])
                                    sres = ta
                                    yield
                                else:
                                    K.tt(dve, ta[:, 0:L + 15], Up[:, 0:L + 15], Up[:, 1:L + 16], ALU.add, [Up], [ta])
                                    yield
                                    cur, other, span, valid = ta, tb, 2, L + 15
                                    while span * 2 < w:
                                        K.tt(dve, other[:, 0:valid - span], cur[:, 0:valid - span], cur[:, span:valid], ALU.add,
                                             [cur], [other])
                                        yield
                                        valid -= span
                                        cur, other = other, cur
                                        span *= 2
                                    K.tt(dve, other[:, 0:L], cur[:, 8 - half:8 - half + L], cur[:, 8:8 + L], ALU.add, [cur], [other])
                                    sres = other
                                    yield
                                K.tt(dve, sres[:, 0:8], sres[:, 0:8], corr[:, g * 16:g * 16 + 8], ALU.mult, [sres, corr], [sres])
                                K.tt(dve, sres[:, L - 8:L], sres[:, L - 8:L], corr[:, g * 16 + 8:g * 16 + 16], ALU.mult,
                                     [sres, corr], [sres])
                                yield
                                K.stt(dve, pooled[:], sres[:, 0:L], 1.0 / w, Up[:, 8:8 + L], ALU.mult, ALU.subtract,
                                      [sres, Up], [pooled])
                                yield
                                yp_ = ypst[g % 2]
                                for n in range(NT):
                                    cs_ = slice(n * TT, (n + 1) * TT)
                                    p = K.ps()
                                    K.mm(p[:], pw[:, g * 128:(g + 1) * 128], pooled[:, cs_], True, True, [pw, pooled], [p])
                                    K.ts(dve, yp_[:, cs_], p[:], V(VL["ps"] + g), None, ALU.mult, None, [p, vec], [yp_])
                                    yield
                                K.dma(sp, [(Ysc[s, g], yp_[:])], reads=[yp_])
                                yield

                        pgen = poolgen()
                        load_head_w(0)
                        for _ in Kprep(0):
                            pass
                        for _ in Qprep(0, 0):
                            pass
                        pbi = 0
                        kbg = None
                        sring = [K.ps_hold() for _ in range(3)]
                        for h in range(H):
                            y_ = yst[h % 2]
                            Kn_, Kr_ = Kn[h % 2], Kr[h % 2]
                            if h + 1 < H:
                                load_head_w(h + 1)
                                kbg = Kprep(h + 1)
                            else:
                                kbg = None
                            for n in range(NT):
                                cs = slice(n * TT, (n + 1) * TT)
                                k_ = h * NT + n
                                Qn_, Qr_ = Qn[k_ % 2], Qr[k_ % 2]
                                if n + 1 < NT:
                                    qbg = Qprep(h, n + 1)
                                elif h + 1 < H:
                                    qbg = Qprep(h + 1, 0)
                                else:
                                    qbg = None
                                o_ps = K.ps_hold()
                                l_ps = K.ps_hold()

                                def smm(kt):
                                    sp_ = sring[kt % 3]
                                    ks = slice(kt * 128, (kt + 1) * 128)
                                    K.mm(sp_[:], Kn_[:, ks], Qn_[:], True, False, [Kn_, Qn_], [sp_])
                                    K.mm(sp_[:], Kr_[:, ks], Qr_[:], False, True, [Kr_, Qr_], [sp_])
                                    return sp_

                                S = {0: smm(0)}
                                if TC > 1:
                                    S[1] = smm(1)
                                for kt in range(TC):
                                    if kt + 2 < TC:
                                        S[kt + 2] = smm(kt + 2)
                                    s_cur = S.pop(kt)
                                    P = Pb[pbi % 4]
                                    pbi += 1
                                    K.actf(P[:], s_cur[:], AF.Exp, [s_cur], [P])
                                    K.mm(o_ps[:], Vt[:, kt, h * 128:(h + 1) * 128], P[:], kt == 0, kt == TC - 1, [Vt, P], [o_ps])
                                    K.mm(l_ps[:], ones[:], P[:], kt == 0, kt == TC - 1, [ones, P], [l_ps])
                                    if qbg is not None:
                                        next(qbg, None)
                                    if kbg is not None and kt % 2 == 1:
                                        next(kbg, None)
                                    if kt % 2 == 0:
                                        next(pgen, None)
                                K.op(dve, lambda: nc.vector.reciprocal(rl[:], l_ps[:]), [l_ps], [rl])
                                K.tt(dve, y_[:, cs], o_ps[:], rl[:], ALU.mult, [o_ps, rl], [y_])
                                K.ps_release(o_ps)
                                K.ps_release(l_ps)
                                if qbg is not None:
                                    for _ in qbg:
                                        pass
                            if kbg is not None:
                                for _ in kbg:
                                    pass
                            K.dma(sp, [(Ysc[s, 4 + h], y_[:])], reads=[y_])
                        for _ in pgen:
                            pass
                        for b_ in sring:
                            K.ps_release(b_)
                        K.barrier()
                with ExitStack() as st:
                    X0 = K.sb(st, "X0", [128, 4, L], BF16)
                    Zt = K.sb(st, "Zt", [128, 4, L], BF16)
                    Yr = K.sb(st, "Yr", [128, TC, HY], BF16)
                    Yi = K.sb(st, "Yi", [128, TC, HY], BF16)
                    ci0 = K.sb(st, "ci0", [128, TC * TT], BF16)
                    si0 = K.sb(st, "si0", [128, TC * TT], BF16)
                    K.dma(sp, [(ci0[:], dft_i[0, 0])], writes=[ci0])
                    K.dma(sp, [(si0[:], dft_i[0, 1])], writes=[si0])
                    with ExitStack() as st2:
                        ur = [K.sb(st2, f"ur{i}", [128, L + 2], BF16) for i in range(3)]
                        ctmp = [K.sb(st2, f"ctmp{i}", [128, L], F32) for i in range(4)]
                        for u_ in ur:
                            K.memset(pool, u_[:, 0:1], 0.0, [u_])
                            K.memset(pool, u_[:, L + 1:L + 2], 0.0, [u_])
                        ui = 0

                        def hconv(c, dst_ap, dst_bufs):
                            nonlocal ui
                            u_ = ur[ui % 3]
                            ui += 1
                            K.dma(sp, [(u_[:, 1:L + 1], Usc[c])], writes=[u_])
                            t_ = ctmp[ui % 3]
                            K.actf(t_[:], u_[:, 1:L + 1], AF.Identity, [u_, vec], [t_],
                                   bias=V(VL["hyb"] + c), scale=V(VL["hyw"] + 12 + c))
                            K.stt(dve, t_[:], u_[:, 0:L], V(VL["hyw"] + c), t_[:], ALU.mult, ALU.add,
                                  [u_, vec, t_], [t_])
                            K.stt(dve, dst_ap, u_[:, 2:L + 2], V(VL["hyw"] + 24 + c), t_[:], ALU.mult, ALU.add,
                                  [u_, vec, t_], dst_bufs)

                        for c in range(4):
                            hconv(c, X0[:, c, :], [X0])
                            hconv(4 + c, ctmp[3][:], [ctmp[3]])
                            hconv(8 + c, Zt[:, c, :], [Zt])
                            K.tt(pool, Zt[:, c, :], Zt[:, c, :], ctmp[3][:], ALU.mult, [Zt, ctmp[3]], [Zt])
                        K.barrier()
                    with ExitStack() as st2:
                        Ztm = K.sb(st2, "Ztm", [128, TC, HY], BF16)
                        for j in range(TC):
                            p = K.ps()
                            for c in range(4):
                                K.mm(p[:, c * 128:(c + 1) * 128], Zt[:, c, j * 128:(j + 1) * 128], ident[:], True, True,
                                     [Zt, ident], [p])
                            K.copy(act if j % 2 == 0 else dve, Ztm[:, j, :], p[:], [p], [Ztm])
                        wc = [K.sb(st2, f"hwc{i}", [128, TC * 128], BF16) for i in range(2)]
                        wsn = [K.sb(st2, f"hws{i}", [128, TC * 128], BF16) for i in range(2)]
                        kre = [K.sb(st2, f"kre{i}", [128, HY], F32) for i in range(2)]
                        kim = [K.sb(st2, f"kim{i}", [128, HY], F32) for i in range(2)]
                        As = [K.sb(st2, f"As{i}", [128, HY], F32) for i in range(2)]
                        Bs = [K.sb(st2, f"Bs{i}", [128, HY], F32) for i in range(2)]
                        t1 = K.sb(st2, "t1", [128, HY], F32)
                        t2 = K.sb(st2, "t2", [128, HY], F32)
                        t3 = K.sb(st2, "t3", [128, HY], F32)
                        t4 = K.sb(st2, "t4", [128, HY], F32)
                        for fc in range(TC):
                            i2 = fc % 2
                            K.dma(sp, [(wc[i2][:], dft_f[fc, 0])], writes=[wc[i2]])
                            K.dma(sp, [(wsn[i2][:], dft_f[fc, 1])], writes=[wsn[i2]])
                            K.dma(sp, [(kre[i2][:], Kf[0, fc])], writes=[kre[i2]])
                            K.dma(sp, [(kim[i2][:], Kf[1, fc])], writes=[kim[i2]])
                            pa = K.ps()
                            for j in range(TC):
                                K.mm(pa[:], wc[i2][:, j * 128:(j + 1) * 128], Ztm[:, j, :], j == 0, j == TC - 1,
                                     [wc[i2], Ztm], [pa])
                            pb = K.ps()
                            for j in range(TC):
                                K.mm(pb[:], wsn[i2][:, j * 128:(j + 1) * 128], Ztm[:, j, :], j == 0, j == TC - 1,
                                     [wsn[i2], Ztm], [pb])
                            A_, B_ = As[i2], Bs[i2]
                            K.copy(act, A_[:], pa[:], [pa], [A_])
                            K.copy(act, B_[:], pb[:], [pb], [B_])
                            K.tt(dve, t1[:], kre[i2][:], A_[:], ALU.mult, [kre[i2], A_], [t1])
                            K.tt(pool, t2[:], kim[i2][:], B_[:], ALU.mult, [kim[i2], B_], [t2])
                            K.tt(dve, Yr[:, fc, :], t1[:], t2[:], ALU.add, [t1, t2], [Yr])
                            K.tt(pool, t3[:], kim[i2][:], A_[:], ALU.mult, [kim[i2], A_], [t3])
                            K.tt(dve, t4[:], kre[i2][:], B_[:], ALU.mult, [kre[i2], B_], [t4])
                            K.tt(pool, Yi[:, fc, :], t3[:], t4[:], ALU.subtract, [t3, t4], [Yi])
                        K.barrier()
                    with ExitStack() as st2:
                        ci = [ci0, K.sb(st2, "ci1", [128, TC * TT], BF16)]
                        si = [si0, K.sb(st2, "si1", [128, TC * TT], BF16)]
                        ysth = K.sb(st2, "ysth", [128, 4, L], BF16)
                        ytmp = [K.sb(st2, f"ytmp{i}", [128, TT], F32) for i in range(2)]
                        for n in range(NT):
                            cs = slice(n * TT, (n + 1) * TT)
                            ci_, si_ = ci[n % 2], si[n % 2]
                            if n + 1 < NT:
                                K.dma(sp, [(ci[(n + 1) % 2][:], dft_i[n + 1, 0])], writes=[ci[(n + 1) % 2]])
                                K.dma(sp, [(si[(n + 1) % 2][:], dft_i[n + 1, 1])], writes=[si[(n + 1) % 2]])
                            for c in range(4):
                                p = K.ps()
                                for fc in range(TC):
                                    K.mm(p[:], Yr[:, fc, c * 128:(c + 1) * 128], ci_[:, fc * TT:(fc + 1) * TT], fc == 0, False,
                                         [Yr, ci_], [p])
                                    K.mm(p[:], Yi[:, fc, c * 128:(c + 1) * 128], si_[:, fc * TT:(fc + 1) * TT], False,
                                         fc == TC - 1, [Yi, si_], [p])
                                yt_ = ytmp[c % 2]
                                K.stt(dve, yt_[:], Zt[:, c, cs], V(VL["hybias"] + c), p[:], ALU.mult, ALU.add,
                                      [Zt, vec, p], [yt_])
                                K.tt(pool, ysth[:, c, cs], yt_[:], X0[:, c, cs], ALU.mult, [yt_, X0], [ysth])
                        for c in range(4):
                            K.dma(sp, [(Ysc[s, 12 + c], ysth[:, c, :])], reads=[ysth])
                        K.barrier()
            if dbg and l == 0:
                with ExitStack() as st:
                    dtile = K.sb(st, "dtile", [128, L], BF16)
                    for s in range(NSEQ):
                        for c in range(16):
                            K.dma(sp, [(dtile[:], Ysc[s, c])], writes=[dtile])
                            K.dma(sp, [(dbg_out[s, c], dtile[:])], reads=[dtile])
                    K.barrier()
            tiles = [(s, n) for s in range(NSEQ) for n in range(NT)]
            with ExitStack() as st:
                Yts = [K.sb(st, f"Ytc{i}", [128, DC, TT], BF16) for i in range(2)]
                Ms = [K.sb(st, f"Mc{i}", [128, DC, TT], BF16) for i in range(2)]
                sqs = [K.sb(st, f"sqc{i}", [128, TT], BF16) for i in range(3)]
                rg = [K.sb(st, f"rg{i}", [128, TT], F32) for i in range(3)]
                wos = [K.sb(st, f"wo{j}", [128, 16 * 128], BF16) for j in range(16)]
                xrc = [K.sb(st, f"xrc{i}", [128, TT], F32) for i in range(4)]
                xoc = [K.sb(st, f"xoc{i}", [128, TT], F32) for i in range(4)]
                for j in range(16):
                    K.dma(pool, [(wos[j][:], wb_out[j])], writes=[wos[j]])

                def prepC(i):
                    s_, n_ = tiles[i]
                    cs_ = slice(n_ * TT, (n_ + 1) * TT)
                    Yt_, M_ = Yts[i % 2], Ms[i % 2]
                    K.dma(sp, [(Yt_[:], Ysc[s_].rearrange("c p t -> p c t")[:, :, cs_])], writes=[Yt_])
                    yield
                    yield
                    for gi, (c0, c1) in enumerate(((0, 4), (4, 12), (12, 16))):
                        ssb = K.ps_hold()
                        for c in range(c0, c1):
                            q_ = sqs[c % 3]
                            K.actf(q_[:], Yt_[:, c, :], AF.Square, [Yt_], [q_])
                            yield
                            K.mm(ssb[:], ones[:], q_[:], c == c0, c == c1 - 1, [ones, q_], [ssb])
                        yield
                        rstd_from_ss(rg[gi][:], ssb[:], (c1 - c0) * 128, None, [ssb], [rg[gi]])
                        K.ps_release(ssb)
                        yield
                        for c in range(c0, c1):
                            K.stt(dve, M_[:, c, :], Yt_[:, c, :], V(VL["grp"] + c), rg[gi][:], ALU.mult, ALU.mult,
                                  [Yt_, vec, rg[gi]], [M_])
                            yield

                for _ in prepC(0):
                    pass
                xi = 0
                for i, (s, n) in enumerate(tiles):
                    cs = slice(n * TT, (n + 1) * TT)
                    bg = prepC(i + 1) if i + 1 < len(tiles) else None
                    M = Ms[i % 2]
                    xsrc = Xin[s]

                    def loadXc(j, k):
                        K.dma(sp, [(xrc[k % 4][:], xsrc[j * 128:(j + 1) * 128, cs])], writes=[xrc[k % 4]])

                    loadXc(0, xi)
                    loadXc(1, xi + 1)
                    for j in range(16):
                        if j + 2 < 16:
                            loadXc(j + 2, xi + j + 2)
                        w_ = wos[j]
                        p = K.ps()
                        for c in range(DC):
                            K.mm(p[:], w_[:, c * 128:(c + 1) * 128], M[:, c, :], c == 0, c == DC - 1, [w_, M], [p])
                        xr_, xo_ = xrc[(xi + j) % 4], xoc[(xi + j) % 4]
                        K.tt(dve, xo_[:], p[:], xr_[:], ALU.add, [p, xr_], [xo_])
                        K.dma(sp, [(xmid[s][j * 128:(j + 1) * 128, cs], xo_[:])], reads=[xo_])
                        if bg is not None:
                            for _ in range(3):
                                next(bg, None)
                    xi += 16
                    if bg is not None:
                        for _ in bg:
                            pass
                K.barrier()
            with ExitStack() as st:
                W = TT + 2
                xt = K.sb(st, "xtb", [128, DC, W], F32)
                hhs = [K.sb(st, f"hhb{i}", [128, DC, W], BF16) for i in range(2)]
                actb = K.sb(st, "actb", [128, FC, TT], BF16)
                sqs = [K.sb(st, f"sqb{i}", [128, TT], BF16) for i in range(3)]
                sqh = K.sb(st, "sqh", [128, DC, 2], BF16)
                rstd = K.sb(st, "rstdb", [128, W], F32)
                wg = [K.sb(st, f"wg{i}", [128, 16 * 128], BF16) for i in range(3)]
                wv_ = [K.sb(st, f"wvv{i}", [128, 16 * 128], BF16) for i in range(3)]
                wd = [K.sb(st, f"wd{i}", [128, FC * 128], BF16) for i in range(2)]
                G = [K.sb(st, f"G{i}", [128, W], F32) for i in range(2)]
                cen = [K.sb(st, f"cen{i}", [128, TT], F32) for i in range(2)]
                sg = [K.sb(st, f"sg{i}", [128, TT], F32) for i in range(2)]
                xr = [K.sb(st, f"xr{i}", [128, TT], F32) for i in range(3)]
                xo = [K.sb(st, f"xo{i}", [128, TT], F32) for i in range(3)]

                def prepB(i):
                    s, n = tiles[i]
                    hh = hhs[i % 2]
                    t0 = n * TT
                    lo = max(t0 - 1, 0)
                    hi = min(t0 + TT + 1, L)
                    if n == 0:
                        K.memset(pool, xt[:, :, 0:1], 0.0, [xt])
                    if n == NT - 1:
                        K.memset(pool, xt[:, :, W - 1:W], 0.0, [xt])
                    K.dma(sp, [(xt[:, :, lo - (t0 - 1):hi - (t0 - 1)],
                                xmid[s].rearrange("(c p) t -> p c t", p=128)[:, :, lo:hi])], writes=[xt])
                    yield
                    yield
                    ssA = K.ps_hold()
                    ssB = K.ps_hold()
                    K.actf(sqh[:], xt[:, :, TT:W], AF.Square, [xt], [sqh])
                    for c in range(DC):
                        q_ = sqs[c % 3]
                        K.actf(q_[:], xt[:, c, 0:TT], AF.Square, [xt], [q_])
                        yield
                        K.mm(ssA[:], ones[:], q_[:], c == 0, c == DC - 1, [ones, q_], [ssA])
                        K.mm(ssB[:, 0:2], ones[:], sqh[:, c, :], c == 0, c == DC - 1, [ones, sqh], [ssB])
                    yield
                    rstd_from_ss(rstd[:, 0:TT], ssA[:], D, None, [ssA], [rstd])
                    rstd_from_ss(rstd[:, TT:W], ssB[:, 0:2], D, None, [ssB], [rstd])
                    K.ps_release(ssA)
                    K.ps_release(ssB)
                    yield
                    for c in range(DC):
                        K.stt(dve, hh[:, c, :], xt[:, c, :], V(VL["ffn"] + c), rstd[:], ALU.mult, ALU.mult,
                              [xt, vec, rstd], [hh])
                        yield

                for _ in prepB(0):
                    pass
                for i, (s, n) in enumerate(tiles):
                    t0 = n * TT
                    cs = slice(t0, t0 + TT)
                    hh = hhs[i % 2]
                    bg = prepB(i + 1) if i + 1 < len(tiles) else None

                    def loadB(j):
                        K.dma(pool, [(wg[j % 3][:], wb_upg[j])], writes=[wg[j % 3]])
                        K.dma(pool, [(wv_[j % 3][:], wb_upv[j])], writes=[wv_[j % 3]])

                    loadB(0)
                    if FC > 1:
                        loadB(1)
                    for j in range(FC):
                        if j + 2 < FC:
                            loadB(j + 2)
                        g_, v_ = wg[j % 3], wv_[j % 3]
                        gA = K.ps()
                        for c in range(DC):
                            K.mm(gA[:], g_[:, c * 128:(c + 1) * 128], hh[:, c, 0:TT], c == 0, c == DC - 1, [g_, hh], [gA])
                        gB = K.ps()
                        for c in range(DC):
                            K.mm(gB[:, 0:2], g_[:, c * 128:(c + 1) * 128], hh[:, c, TT:W], c == 0, c == DC - 1, [g_, hh], [gB])
                        vP = K.ps()
                        for c in range(DC):
                            K.mm(vP[:], v_[:, c * 128:(c + 1) * 128], hh[:, c, 1:TT + 1], c == 0, c == DC - 1, [v_, hh], [vP])
                        G_, cen_, sg_ = G[j % 2], cen[j % 2], sg[j % 2]
                        K.copy(act, G_[:, 0:TT], gA[:], [gA], [G_])
                        K.copy(act, G_[:, TT:W], gB[:, 0:2], [gB], [G_])
                        K.actf(cen_[:], G_[:, 1:TT + 1], AF.Identity, [G_, vec], [cen_],
                               bias=V(VL["fcb"] + j), scale=V(VL["fcw"] + FC + j))
                        K.stt(dve, cen_[:], G_[:, 0:TT], V(VL["fcw"] + j), cen_[:], ALU.mult, ALU.add,
                              [G_, vec, cen_], [cen_])
                        K.stt(dve, cen_[:], G_[:, 2:W], V(VL["fcw"] + 2 * FC + j), cen_[:], ALU.mult, ALU.add,
                              [G_, vec, cen_], [cen_])
                        K.actf(sg_[:], cen_[:], AF.Silu, [cen_], [sg_])
                        K.tt(dve, actb[:, j, :], sg_[:], vP[:], ALU.mult, [sg_, vP], [actb])
                        if bg is not None:
                            next(bg, None)
                    K.dma(pool, [(wd[0][:], wb_dn[0])], writes=[wd[0]])
                    xsrc = xmid[s]

                    def loadX(dc):
                        K.dma(sp, [(xr[dc % 3][:], xsrc[dc * 128:(dc + 1) * 128, cs])], writes=[xr[dc % 3]])

                    loadX(0)
                    loadX(1)
                    for dc in range(DC):
                        if dc + 1 < DC:
                            K.dma(pool, [(wd[(dc + 1) % 2][:], wb_dn[dc + 1])], writes=[wd[(dc + 1) % 2]])
                        if dc + 2 < DC:
                            loadX(dc + 2)
                        d_ = wd[dc % 2]
                        p = K.ps()
                        for j in range(FC):
                            K.mm(p[:], d_[:, j * 128:(j + 1) * 128], actb[:, j, :], j == 0, j == FC - 1, [d_, actb], [p])
                        xo_ = xo[dc % 3]
                        K.tt(dve, xo_[:], p[:], xr[dc % 3][:], ALU.add, [p, xr[dc % 3]], [xo_])
                        K.dma(sp, [(Xout[s][dc * 128:(dc + 1) * 128, cs], xo_[:])], reads=[xo_])
                        if bg is not None:
                            next(bg, None)
                            next(bg, None)
                    if bg is not None:
                        for _ in bg:
                            pass
                K.barrier()
        K.barrier()
        print(f"[build] instructions emitted: {K.n_ins}")
    return nc


def _tile_lhsT(w, cols_list):
    Kd = w.shape[0]
    kc = Kd // 128
    outs = []
    for cols in cols_list:
        sel = w[:, cols]
        outs.append(sel.reshape(kc, 128, 128).transpose(1, 0, 2).reshape(128, kc * 128))
    return np.ascontiguousarray(np.stack(outs, 0))


def _pvec(v):
    return np.ascontiguousarray(v.reshape(-1, 128).T)


def make_consts(L):
    TC = L // 128
    NT = L // TT
    f32 = np.float32
    t = np.linspace(0.0, 1.0, L, dtype=f32)[:, None]
    w = (2.0 * math.pi * np.arange(L, dtype=f32)[:, None] / L).astype(f32)
    bands = np.linspace(1e-4, 8 - 1, 8, dtype=f32)[None, :]
    z = np.concatenate([t, np.cos(bands * w), -np.sin(bands * w)], axis=-1).astype(f32)
    max_decay = math.log(1e-2) / 0.3
    min_decay = math.log(1e-2) / 1.5
    deltas = np.linspace(min_decay, max_decay, HY, dtype=f32)
    dec = np.exp(-t * np.abs(deltas)[None, :]).astype(f32)
    tt_ = np.arange(L, dtype=np.float64)
    om = 2.0 * math.pi * (np.arange(L, dtype=np.float64) + 0.5) / (2.0 * L)
    ang = np.outer(tt_, om)
    Cf = np.cos(ang)
    Sf = np.sin(ang)
    dft_f = np.empty((TC, 2, 128, TC * 128), dtype=ml_dtypes.bfloat16)
    for fc in range(TC):
        for k, M_ in enumerate((Cf, Sf)):
            blk = M_[:, fc * 128:(fc + 1) * 128]
            dft_f[fc, k] = blk.reshape(TC, 128, 128).transpose(1, 0, 2).reshape(128, TC * 128).astype(ml_dtypes.bfloat16)
    dft_i = np.empty((NT, 2, 128, TC * TT), dtype=ml_dtypes.bfloat16)
    for n in range(NT):
        for k, M_ in enumerate((Cf / L, -Sf / L)):
            blk = M_[n * TT:(n + 1) * TT, :].T
            dft_i[n, k] = blk.reshape(TC, 128, TT).transpose(1, 0, 2).reshape(128, TC * TT).astype(ml_dtypes.bfloat16)
    freqs = (10000.0 ** (-np.arange(0, 64, 2, dtype=f32) / 64)).astype(f32)
    angr = (np.arange(L, dtype=f32)[:, None] * freqs[None, :]).astype(f32)
    cs_ = np.cos(angr.astype(np.float64)).T.astype(f32)
    sn_ = np.sin(angr.astype(np.float64)).T.astype(f32)
    rope = np.concatenate([cs_, cs_, -sn_, sn_], axis=0).astype(f32)
    p_ = np.arange(128)
    fold = (p_[:, None] % 64 == p_[None, :] % 64).astype(ml_dtypes.bfloat16)
    ident = np.eye(128).astype(ml_dtypes.bfloat16)
    pc = np.ones((4, 16), dtype=f32)
    for g, wdw in enumerate(POOL_WINDOWS):
        half = wdw // 2
        for k in range(16):
            tk = k if k < 8 else L - 16 + k
            cnt = min(tk + half, L) - max(tk - half, 0)
            pc[g, k] = wdw / cnt
    pcorr = np.broadcast_to(pc.reshape(1, 64), (128, 64)).copy()
    return {
        "zfeatT": np.ascontiguousarray(z.T),
        "decay": np.ascontiguousarray(dec.reshape(TC, 128, HY)),
        "dft_f": dft_f, "dft_i": dft_i, "ropeT": rope, "foldm": fold, "identm": ident, "pcorr": pcorr,
    }


def prep_weights(P, DEPTH, DFF):
    FC = DFF // 128
    VL = vec_layout(FC)
    f = lambda a: np.asarray(a, dtype=np.float32)
    out = {}
    std = lambda n0, n: [np.arange(n0 + 128 * j, n0 + 128 * (j + 1)) for j in range(n)]
    sw64 = lambda base: np.concatenate([np.arange(base + 32, base + 64), np.arange(base, base + 32)])
    cols_in = std(0, 10) + [np.concatenate([np.arange(1280, 1344), sw64(1280)])] + std(1344, 12)
    cols_uq = []
    for h in range(H):
        cols_uq.append(np.arange(192 * h, 192 * h + 128))
        cols_uq.append(np.concatenate([np.arange(192 * h + 128, 192 * h + 192), sw64(192 * h + 128)]))
    cols_kn = [np.arange(256 * h, 256 * h + 128) for h in range(H)]
    cols_v = np.concatenate([np.arange(256 * h + 128, 256 * h + 256) for h in range(H)])
    w_in, w_uq, w_ukn, w_ukv, pw, w_out, w_upg, w_upv, w_dn, vecs = [], [], [], [], [], [], [], [], [], []
    hb3, hvec = [], []
    for l in range(DEPTH):
        w_in.append(_tile_lhsT(f(P["w_in"][l]), cols_in))
        w_uq.append(_tile_lhsT(f(P["mla_w_uq"][l]), cols_uq))
        w_ukn.append(_tile_lhsT(f(P["mla_w_ukv"][l]), cols_kn))
        wv = f(P["mla_w_ukv"][l])[:, cols_v]
        w_ukv.append(wv.reshape(2, 128, 1024).transpose(1, 0, 2).reshape(128, 2048))
        pw.append(f(P["pool_w"][l]).transpose(1, 0, 2).reshape(128, 4 * 128))
        w_out.append(_tile_lhsT(f(P["w_out"][l]), std(0, 16)))
        w_upg.append(_tile_lhsT(f(P["ffn_w_up"][l]), std(0, FC)))
        w_upv.append(_tile_lhsT(f(P["ffn_w_up"][l]), std(DFF, FC)))
        wd = f(P["ffn_w_down"][l])
        w_dn.append(wd.reshape(FC, 128, 16, 128).transpose(2, 1, 0, 3).reshape(16, 128, FC * 128))
        v = np.zeros((128, VL["NV"]), dtype=np.float32)
        v[:, VL["attn"]:VL["attn"] + 16] = _pvec(f(P["attn_norm_g"][l]))
        v[:, VL["ffn"]:VL["ffn"] + 16] = _pvec(f(P["ffn_norm_g"][l]))
        v[:, VL["grp"]:VL["grp"] + 16] = _pvec(f(P["grp_norm_g"][l]))
        v[:, VL["ps"]:VL["ps"] + 4] = _pvec(f(P["pool_scale"][l]))
        v[:, VL["qn"]:VL["qn"] + 4] = _pvec(f(P["mla_q_norm_g"][l]))
        v[:, VL["kvn"]:VL["kvn"] + 2] = _pvec(f(P["mla_kv_norm_g"][l]))
        qh = f(P["mla_q_head_norm_g"][l])
        kh = f(P["mla_k_head_norm_g"][l])
        swg = lambda g: np.concatenate([g[128:192], g[160:192], g[128:160]])
        v[:, VL["qhA"]] = qh[:128]
        v[:, VL["qhB"]] = swg(qh)
        v[:, VL["khA"]] = kh[:128]
        v[:, VL["khB"]] = swg(kh)
        hw = f(P["hy_conv_w"][l])
        for tap in range(3):
            v[:, VL["hyw"] + tap * 12:VL["hyw"] + (tap + 1) * 12] = _pvec(hw[tap])
        v[:, VL["hyb"]:VL["hyb"] + 12] = _pvec(f(P["hy_conv_b"][l]))
        v[:, VL["hybias"]:VL["hybias"] + 4] = _pvec(f(P["hy_bias"][l]))
        fw = f(P["ffn_conv_w"][l])
        for tap in range(3):
            v[:, VL["fcw"] + tap * FC:VL["fcw"] + (tap + 1) * FC] = _pvec(fw[tap])
        v[:, VL["fcb"]:VL["fcb"] + FC] = _pvec(f(P["ffn_conv_b"][l]))
        vecs.append(v)
        hb3.append(np.broadcast_to(f(P["hy_filt_b3"][l])[None, :], (128, 2 * HY)).copy())
        hv = np.zeros((FH, 4), dtype=np.float32)
        hv[:, 0] = f(P["hy_filt_b1"][l])
        hv[:, 1] = f(P["hy_filt_freq"][l])
        hv[:, 2] = f(P["hy_filt_b2"][l])
        hvec.append(hv)
    st_ = lambda lst: np.ascontiguousarray(np.stack(lst, 0))
    out.update(w_in=st_(w_in), w_uq=st_(w_uq), w_ukn=st_(w_ukn), w_ukv=st_(w_ukv), pool_w=st_(pw), w_out=st_(w_out),
               w_upg=st_(w_upg), w_upv=st_(w_upv), w_dn=st_(w_dn), vecs=st_(vecs), hf_b3=st_(hb3), hvec=st_(hvec),
               hf_w1=np.ascontiguousarray(f(P["hy_filt_w1"][:DEPTH])), hf_w2=np.ascontiguousarray(f(P["hy_filt_w2"][:DEPTH])),
               hf_w3=np.ascontiguousarray(f(P["hy_filt_w3"][:DEPTH])))
    return out


def run(cfg, x_all, P, n_cores, trace=False):
    L, NSEQ, DEPTH, DFF = cfg["L"], cfg["NSEQ"], cfg["DEPTH"], cfg["DFF"]
    nc = build(cfg)
    shared = prep_weights(P, DEPTH, DFF)
    shared.update(make_consts(L))
    in_maps = []
    for i in range(n_cores):
        m = dict(shared)
        m["xT"] = np.ascontiguousarray(np.asarray(x_all[i * NSEQ:(i + 1) * NSEQ], dtype=np.float32).transpose(0, 2, 1))
        in_maps.append(m)
    res = run_bass_kernel_spmd(nc, in_maps, core_ids=list(range(n_cores)), trace=trace)
    ys = [np.asarray(r["yT"]).transpose(0, 2, 1) for r in res.results]
    return np.concatenate(ys, 0), res


def kernel(**inputs):
    xp = np.asarray(inputs["x_prompt"], dtype=np.float32)
    xsm = np.asarray(inputs["x_sample"], dtype=np.float32)
    nb = xp.shape[0]
    x_all = np.concatenate([xp, xsm], 0)
    nseq = x_all.shape[0] // N_CORES
    cfg = {"L": x_all.shape[1], "NSEQ": nseq, "DEPTH": inputs["w_in"].shape[0], "DFF": inputs["ffn_conv_b"].shape[1]}
    P = {k: v for k, v in inputs.items() if k not in ("x_prompt", "x_sample")}
    y_all, _ = run(cfg, x_all, P, N_CORES)
    y_all = np.ascontiguousarray(y_all, dtype=np.float32)
    return (np.ascontiguousarray(y_all[:nb]), np.ascontiguousarray(y_all[nb:]))
```

```python
import math
from contextlib import ExitStack
import numpy as np
import ml_dtypes
import concourse.bass as bass
import concourse.mybir as mybir
from concourse.bass_utils import run_bass_kernel_spmd

F32 = mybir.dt.float32
BF16 = mybir.dt.bfloat16
AF = mybir.ActivationFunctionType
ALU = mybir.AluOpType

D = 2048
DC = 16
POOL_WINDOWS = (2, 4, 8, 16)
H = 8
NPF = 17
FH = 64
HY = 512
EPS = 1e-6
TT = 512
N_CORES = 8


class Sem:
    def __init__(self, handle):
        self.h = handle
        self.cnt = 0


class Eng:
    def __init__(self, name, e, sem):
        self.name = name
        self.e = e
        self.sem = sem
        self.seen = {}


class Buf:
    def __init__(self, name, t=None):
        self.name = name
        self.t = t
        self.lw = None
        self.readers = []
        self.sems = {}

    def __getitem__(self, idx):
        return self.t[idx]


class Kern:
    def __init__(self, nc, es):
        self.nc = nc
        self.es = es
        mk = lambda n: Sem(es.enter_context(nc.semaphore(n)))
        self.pe = Eng("pe", nc.tensor, mk("s_pe"))
        self.act = Eng("act", nc.scalar, mk("s_act"))
        self.dve = Eng("dve", nc.vector, mk("s_dve"))
        self.pool = Eng("pool", nc.gpsimd, mk("s_pool"))
        self.sp = Eng("sp", nc.sync, mk("s_sp"))
        self.engs = [self.pe, self.act, self.dve, self.pool, self.sp]
        self.dfree = {"sp": [mk(f"s_h{i}") for i in range(46)], "pool": [mk(f"s_w{i}") for i in range(40)]}
        self.phase_sems = []
        self.phase_bufs = []
        self.psum = []
        for i in range(8):
            t = es.enter_context(nc.psum_tensor(f"psb{i}", [128, 512], F32))
            self.psum.append(Buf(f"psb{i}", t))
        self.ps_avail = list(self.psum)
        self.ps_i = 0
        self.n_ins = 0

    def sb(self, st, name, shape, dt):
        self.uid = getattr(self, "uid", 0) + 1
        name = f"{name}_{self.uid}"
        t = st.enter_context(self.nc.sbuf_tensor(name, list(shape), dt))
        return Buf(name, t)

    def ps(self):
        b = self.ps_avail.pop(0)
        self.ps_avail.append(b)
        return b

    def ps_hold(self):
        return self.ps_avail.pop(0)

    def ps_release(self, b):
        self.ps_avail.append(b)

    def _dsem(self, b, kind, q):
        key = (kind, q.name)
        if key not in b.sems:
            pn = "pool" if q.name == "pool" else "sp"
            s = self.dfree[pn].pop()
            self.phase_sems.append((pn, s))
            self.phase_bufs.append(b)
            b.sems[key] = s
        return b.sems[key]

    def _wait(self, eng, deps):
        best = {}
        for d in deps:
            if d is None:
                continue
            s, v = d
            if s is eng.sem and eng is self.pe:
                continue
            if v > best.get(s, 0):
                best[s] = v
        for s, v in best.items():
            if eng.seen.get(s, 0) >= v:
                continue
            eng.e.wait_ge(s.h, v)
            eng.seen[s] = v

    def _deps(self, reads, writes):
        deps = []
        for b in reads:
            deps.append(b.lw)
        for b in writes:
            deps.append(b.lw)
            deps.extend(b.readers)
        return deps

    def op(self, eng, fn, reads=(), writes=()):
        self._wait(eng, self._deps(reads, writes))
        ins = fn()
        ins.then_inc(eng.sem.h, 1)
        eng.sem.cnt += 1
        tok = (eng.sem, eng.sem.cnt)
        for b in reads:
            b.readers.append(tok)
        for b in writes:
            b.lw = tok
            b.readers = []
        self.n_ins += 1

    def dma(self, q, pairs, reads=(), writes=()):
        self._wait(q, self._deps(reads, writes))
        if writes:
            s = self._dsem(writes[0], "w", q)
        else:
            s = self._dsem(reads[0], "r", q)
        for (o, i) in pairs:
            q.e.dma_start(out=o, in_=i).then_inc(s.h, 16)
            s.cnt += 16
        tok = (s, s.cnt)
        for b in reads:
            b.readers.append(tok)
        for b in writes:
            b.lw = tok
            b.readers = []
        self.n_ins += len(pairs)

    def barrier(self):
        allsems = [e.sem for e in self.engs] + [s for (_, s) in self.phase_sems]
        for e in self.engs:
            for s in allsems:
                if s is e.sem or s.cnt == 0:
                    continue
                if e.seen.get(s, 0) >= s.cnt:
                    continue
                e.e.wait_ge(s.h, s.cnt)
                e.seen[s] = s.cnt
        for b in self.psum:
            b.lw = None
            b.readers = []
        for (qn, s) in self.phase_sems:
            self.dfree[qn].append(s)
        self.phase_sems = []
        for b in self.phase_bufs:
            b.sems = {}
        self.phase_bufs = []

    def mm(self, out, lhsT, rhs, start, stop, reads, writes):
        self.op(self.pe, lambda: self.nc.tensor.matmul(out, lhsT, rhs, start=start, stop=stop), reads, writes)

    def actf(self, out, in_, func, reads, writes, bias=None, scale=None, eng=None):
        kw = {}
        if bias is not None:
            kw["bias"] = bias
        if scale is not None:
            kw["scale"] = scale
        self.op(self.act, lambda: self.nc.scalar.activation(out=out, in_=in_, func=func, **kw), reads, writes)

    def ts(self, eng, out, in0, s1, s2, op0, op1, reads, writes):
        if op1 is None:
            self.op(eng, lambda: eng.e.tensor_scalar(out, in0, s1, None, op0), reads, writes)
        else:
            self.op(eng, lambda: eng.e.tensor_scalar(out, in0, s1, s2, op0, op1), reads, writes)

    def tt(self, eng, out, in0, in1, op, reads, writes):
        self.op(eng, lambda: eng.e.tensor_tensor(out, in0, in1, op), reads, writes)

    def stt(self, eng, out, in0, scalar, in1, op0, op1, reads, writes):
        self.op(eng, lambda: eng.e.scalar_tensor_tensor(out, in0, scalar, in1, op0, op1), reads, writes)

    def copy(self, eng, out, in_, reads, writes):
        if eng is self.act:
            self.op(eng, lambda: self.nc.scalar.copy(out, in_), reads, writes)
        else:
            self.op(eng, lambda: eng.e.tensor_copy(out, in_), reads, writes)

    def memset(self, eng, ap, val, writes):
        self.op(eng, lambda: eng.e.memset(ap, val), (), writes)


def vec_layout(FC):
    o = {}
    o["attn"] = 0
    o["ffn"] = 16
    o["grp"] = 32
    o["ps"] = 48
    o["qn"] = 52
    o["kvn"] = 56
    o["qhA"] = 58
    o["qhB"] = 59
    o["khA"] = 60
    o["khB"] = 61
    o["hyw"] = 62
    o["hyb"] = 98
    o["hybias"] = 110
    o["fcw"] = 114
    o["fcb"] = 114 + 3 * FC
    o["NV"] = 114 + 4 * FC
    return o


def build(cfg):
    L = cfg["L"]
    NSEQ = cfg["NSEQ"]
    DEPTH = cfg["DEPTH"]
    DFF = cfg["DFF"]
    FC = DFF // 128
    NT = L // TT
    TC = L // 128
    VL = vec_layout(FC)
    NV = VL["NV"]
    dbg = cfg.get("debug", False)

    nc = bass.Bass("TRN2", target_bir_lowering=False)

    def din(name, shape, dt=F32):
        return nc.dram_tensor(name, list(shape), dt, kind="ExternalInput").ap()

    xT = din("xT", [NSEQ, D, L])
    w_in = din("w_in", [DEPTH, 23, 128, 16 * 128])
    w_uq = din("w_uq", [DEPTH, 16, 128, 4 * 128])
    w_ukn = din("w_ukn", [DEPTH, 8, 128, 2 * 128])
    w_ukv = din("w_ukv", [DEPTH, 128, 2 * 1024])
    pool_w = din("pool_w", [DEPTH, 128, 4 * 128])
    w_out = din("w_out", [DEPTH, 16, 128, 16 * 128])
    w_upg = din("w_upg", [DEPTH, FC, 128, 16 * 128])
    w_upv = din("w_upv", [DEPTH, FC, 128, 16 * 128])
    w_dn = din("w_dn", [DEPTH, 16, 128, FC * 128])
    vecs = din("vecs", [DEPTH, 128, NV])
    hf_w1 = din("hf_w1", [DEPTH, NPF, FH])
    hf_w2 = din("hf_w2", [DEPTH, FH, FH])
    hf_w3 = din("hf_w3", [DEPTH, FH, 2 * HY])
    hf_b3 = din("hf_b3", [DEPTH, 128, 2 * HY])
    hvec = din("hvec", [DEPTH, FH, 4])
    zfeatT = din("zfeatT", [NPF, L])
    decay = din("decay", [TC, 128, HY])
    dft_f = din("dft_f", [TC, 2, 128, TC * 128], BF16)
    dft_i = din("dft_i", [NT, 2, 128, TC * TT], BF16)
    ropeT = din("ropeT", [128, L])
    foldm = din("foldm", [128, 128], BF16)
    identm = din("identm", [128, 128], BF16)
    pcorr = din("pcorr", [128, 4 * 16])

    yT = nc.dram_tensor("yT", [NSEQ, D, L], F32, kind="ExternalOutput").ap()
    xs = nc.dram_tensor("xs_scr", [NSEQ, D, L], F32).ap()
    xmid = nc.dram_tensor("xmid_scr", [NSEQ, D, L], F32).ap()
    Ysc = nc.dram_tensor("y_scr", [NSEQ, 16, 128, L], BF16).ap()
    Kf = nc.dram_tensor("kf_scr", [2, TC, 128, HY], F32).ap()
    Usc = nc.dram_tensor("u_scr", [12, 128, L], BF16).ap()
    Psc = nc.dram_tensor("p_scr", [4, 128, L], F32).ap()
    dbg_out = None
    if dbg:
        dbg_out = nc.dram_tensor("dbgY", [NSEQ, 16, 128, L], BF16, kind="ExternalOutput").ap()

    W32 = {"w_in": w_in, "w_uq": w_uq, "w_ukn": w_ukn, "w_ukv": w_ukv, "pool_w": pool_w, "w_out": w_out,
           "w_upg": w_upg, "w_upv": w_upv, "w_dn": w_dn}
    WB = {}
    for nm, ap_ in W32.items():
        shp = list(ap_.shape)[1:]
        WB[nm] = nc.dram_tensor("wb_" + nm, [2] + shp, BF16).ap()

    with ExitStack() as es:
        K = Kern(nc, es)
        pe, act, dve, pool, sp = K.pe, K.act, K.dve, K.pool, K.sp
        cvsA = [Sem(es.enter_context(nc.semaphore(f"s_cva{i}"))) for i in range(2)]
        cvsB = [Sem(es.enter_context(nc.semaphore(f"s_cvb{i}"))) for i in range(2)]
        cv_target = {}
        GROUP_A = ("w_in", "w_uq", "w_ukn", "w_ukv", "pool_w")

        def convert_gen(l):
            par = l % 2
            for grp, sm in ((0, cvsA[par]), (1, cvsB[par])):
                for nm, src_ in W32.items():
                    if (nm in GROUP_A) != (grp == 0):
                        continue
                    s_l = src_[l]
                    d_l = WB[nm][par]
                    if len(s_l.shape) == 2:
                        pool.e.dma_start(out=d_l, in_=s_l).then_inc(sm.h, 16)
                        sm.cnt += 16
                        yield
                    else:
                        n0 = s_l.shape[0]
                        step = 4
                        for i0 in range(0, n0, step):
                            i1 = min(n0, i0 + step)
                            pool.e.dma_start(out=d_l[i0:i1], in_=s_l[i0:i1]).then_inc(sm.h, 16)
                            sm.cnt += 16
                            yield
                cv_target[(l, grp)] = sm.cnt

        def convert(l):
            for _ in convert_gen(l):
                pass

        def wait_converted(l, grp):
            sm = (cvsA if grp == 0 else cvsB)[l % 2]
            for e in (sp, pool, act):
                e.e.wait_ge(sm.h, cv_target[(l, grp)])

        convert(0)

        ones = K.sb(es, "ones", [128, 128], BF16)
        fold = K.sb(es, "fold", [128, 128], BF16)
        ident = K.sb(es, "ident", [128, 128], BF16)
        vec = K.sb(es, "vec", [128, NV], F32)
        corr = K.sb(es, "corr", [128, 64], F32)
        K.memset(dve, ones[:], 1.0, [ones])
        K.dma(sp, [(fold[:], foldm)], writes=[fold])
        K.dma(sp, [(ident[:], identm)], writes=[ident])
        K.dma(sp, [(corr[:], pcorr)], writes=[corr])

        def V(k):
            return vec[:, k:k + 1]

        cst = K.sb(es, "cst", [128, 4], F32)
        K.memset(dve, cst[:, 0:1], EPS, [cst])
        K.memset(dve, cst[:, 1:2], EPS * 192.0, [cst])

        def rstd_from_ss(dst, ss_ap, n, w, reads, writes, extra_scale=None):
            if extra_scale is None:
                K.actf(dst, ss_ap, AF.Ln, list(reads) + [cst], writes, bias=cst[:, 0:1], scale=1.0 / n)
            else:
                assert abs(extra_scale ** 2 - 1.0 / 192.0) < 1e-9
                K.actf(dst, ss_ap, AF.Ln, list(reads) + [cst], writes, bias=cst[:, 1:2], scale=192.0 / n)
            K.actf(dst, dst, AF.Exp, writes, writes, scale=-0.5)

        for l in range(DEPTH):
            Xin = xT if l == 0 else xs
            Xout = yT if l == DEPTH - 1 else xs
            K.dma(sp, [(vec[:], vecs[l])], writes=[vec])
            if l >= 1:
                wait_converted(l, 0)
                wait_converted(l, 1)
            wb_in, wb_uq, wb_ukn, wb_ukv, wb_pool, wb_out, wb_upg, wb_upv, wb_dn = (
                WB[k][l % 2] for k in ("w_in", "w_uq", "w_ukn", "w_ukv", "pool_w", "w_out", "w_upg", "w_upv", "w_dn"))

            with ExitStack() as st:
                zf = K.sb(st, "zf", [NPF, L], F32)
                fw1 = K.sb(st, "fw1", [NPF, FH], F32)
                fw2 = K.sb(st, "fw2", [FH, FH], F32)
                fw3 = K.sb(st, "fw3", [FH, 2 * HY], F32)
                fb3 = K.sb(st, "fb3", [128, 2 * HY], F32)
                hv = K.sb(st, "hv", [FH, 4], F32)
                h1 = K.sb(st, "h1", [FH, L], F32)
                h2 = K.sb(st, "h2", [FH, L], F32)
                hsum = K.sb(st, "hsum", [128, TC, HY], BF16)
                hdif = K.sb(st, "hdif", [128, TC, HY], BF16)
                targ = K.sb(st, "targ", [FH, TT], F32)
                tsq = K.sb(st, "tsq", [FH, TT], F32)
                K.dma(sp, [(zf[:], zfeatT)], writes=[zf])
                K.dma(sp, [(fw1[:], hf_w1[l])], writes=[fw1])
                K.dma(sp, [(fw2[:], hf_w2[l])], writes=[fw2])
                K.dma(sp, [(fw3[:], hf_w3[l])], writes=[fw3])
                K.dma(sp, [(fb3[:], hf_b3[l])], writes=[fb3])
                K.dma(sp, [(hv[:], hvec[l])], writes=[hv])
                for (wsrc, bcol, src, dst) in ((fw1, 0, zf, h1), (fw2, 2, h1, h2)):
                    for n in range(NT):
                        cs = slice(n * TT, (n + 1) * TT)
                        p = K.ps()
                        K.mm(p[0:FH, :], wsrc[:], src[:, cs], True, True, [wsrc, src], [p])
                        K.ts(dve, targ[:], p[0:FH, :], hv[:, bcol:bcol + 1], hv[:, 1:2], ALU.add, ALU.mult,
                             [p, hv], [targ])
                        K.actf(targ[:], targ[:], AF.Sin, [targ], [targ], scale=1.0 / 9.0)
                        for rep in range(2):
                            K.tt(dve, tsq[:], targ[:], targ[:], ALU.mult, [targ], [tsq])
                            K.ts(dve, tsq[:], tsq[:], -4.0, 3.0, ALU.mult, ALU.add, [tsq], [tsq])
                            K.tt(dve, (dst[:, cs] if rep == 1 else targ[:]), tsq[:], targ[:], ALU.mult, [tsq, targ],
                                 [dst if rep == 1 else targ])
                with ExitStack() as st2:
                    dct = [K.sb(st2, f"dct{i}", [128, HY], F32) for i in range(2)]
                    hf = [K.sb(st2, f"hf{i}", [128, HY], F32) for i in range(2)]
                    hb = [K.sb(st2, f"hb{i}", [128, HY], F32) for i in range(2)]
                    for j in range(TC):
                        dc_ = dct[j % 2]
                        hf_ = hf[j % 2]
                        hb_ = hb[j % 2]
                        K.dma(sp, [(dc_[:], decay[j])], writes=[dc_])
                        pf = K.ps()
                        K.mm(pf[:], h2[:, j * 128:(j + 1) * 128], fw3[:, 0:HY], True, True, [h2, fw3], [pf])
                        pb = K.ps()
                        K.mm(pb[:], h2[:, j * 128:(j + 1) * 128], fw3[:, HY:2 * HY], True, True, [h2, fw3], [pb])
                        K.tt(dve, hf_[:], pf[:], fb3[:, 0:HY], ALU.add, [pf, fb3], [hf_])
                        K.tt(dve, hb_[:], pb[:], fb3[:, HY:2 * HY], ALU.add, [pb, fb3], [hb_])
                        K.tt(pool, hf_[:], hf_[:], dc_[:], ALU.mult, [hf_, dc_], [hf_])
                        K.tt(pool, hb_[:], hb_[:], dc_[:], ALU.mult, [hb_, dc_], [hb_])
                        if j == 0:
                            K.memset(pool, hb_[0:1, :], 0.0, [hb_])
                        K.tt(dve, hsum[:, j, :], hf_[:], hb_[:], ALU.add, [hf_, hb_], [hsum])
                        K.tt(pool, hdif[:, j, :], hb_[:], hf_[:], ALU.subtract, [hf_, hb_], [hdif])
                    wc = [K.sb(st2, f"fwc{i}", [128, TC * 128], BF16) for i in range(2)]
                    wsn = [K.sb(st2, f"fws{i}", [128, TC * 128], BF16) for i in range(2)]
                    ko = [K.sb(st2, f"ko{i}", [128, 2, HY], F32) for i in range(2)]
                    for fc in range(TC):
                        c_ = wc[fc % 2]
                        s_ = wsn[fc % 2]
                        o_ = ko[fc % 2]
                        K.dma(sp, [(c_[:], dft_f[fc, 0])], writes=[c_])
                        K.dma(sp, [(s_[:], dft_f[fc, 1])], writes=[s_])
                        pr = K.ps()
                        for j in range(TC):
                            K.mm(pr[:], c_[:, j * 128:(j + 1) * 128], hsum[:, j, :], j == 0, j == TC - 1,
                                 [c_, hsum], [pr])
                        pi_ = K.ps()
                        for j in range(TC):
                            K.mm(pi_[:], s_[:, j * 128:(j + 1) * 128], hdif[:, j, :], j == 0, j == TC - 1,
                                 [s_, hdif], [pi_])
                        K.copy(act, o_[:, 0, :], pr[:], [pr], [o_])
                        K.copy(act, o_[:, 1, :], pi_[:], [pi_], [o_])
                        K.dma(sp, [(Kf[0, fc], o_[:, 0, :]), (Kf[1, fc], o_[:, 1, :])], reads=[o_])
                    K.barrier()
            if l == 0:
                wait_converted(l, 0)
            for s in range(NSEQ):
                with ExitStack() as sq_st:
                    cqn = K.sb(sq_st, "cqn", [128, 4, L], BF16)
                    ckvn = K.sb(sq_st, "ckvn", [128, 2, L], BF16)
                    sspe = K.sb(sq_st, "sspe", [128, L], F32)
                    K2u = K.sb(sq_st, "K2u", [128, L], BF16)
                    rope = K.sb(sq_st, "rope", [128, L], F32)
                    K.dma(sp, [(rope[:], ropeT)], writes=[rope])
                    with ExitStack() as st:
                        xt = K.sb(st, "xt", [128, DC, TT], F32)
                        hhs = [K.sb(st, f"hh{i}", [128, DC, TT], BF16) for i in range(2)]
                        sqs = [K.sb(st, f"sq{i}", [128, TT], BF16) for i in range(4)]
                        sqp = [K.sb(st, f"sqp{i}", [128, TT], BF16) for i in range(3)]
                        rstd = K.sb(st, "rstd", [128, TT], F32)
                        rstd2 = K.sb(st, "rstd2", [128, TT], F32)
                        NWI = 6
                        wts = [K.sb(st, f"wi{i}", [128, 16 * 128], BF16) for i in range(NWI)]
                        craw = K.sb(st, "craw", [128, 4, TT], F32)
                        kB = K.sb(st, "kB", [128, TT], F32)
                        kR = K.sb(st, "kR", [128, TT], BF16)
                        pst = [K.sb(st, f"pst{i}", [128, TT], F32) for i in range(4)]
                        ust = [K.sb(st, f"ust{i}", [128, TT], BF16) for i in range(6)]
                        sqi = 0

                        def prepA(n):
                            hh_ = hhs[n % 2]
                            cs_ = slice(n * TT, (n + 1) * TT)
                            K.dma(sp, [(xt[:], Xin[s].rearrange("(c p) t -> p c t", p=128)[:, :, cs_])], writes=[xt])
                            yield
                            yield
                            ssb = K.ps_hold()
                            for c in range(DC):
                                q_ = sqp[c % 3]
                                K.actf(q_[:], xt[:, c, :], AF.Square, [xt], [q_])
                                yield
                                K.mm(ssb[:], ones[:], q_[:], c == 0, c == DC - 1, [ones, q_], [ssb])
                            yield
                            rstd_from_ss(rstd[:], ssb[:], D, None, [ssb], [rstd])
                            K.ps_release(ssb)
                            yield
                            for c in range(DC):
                                K.stt(dve, hh_[:, c, :], xt[:, c, :], V(VL["attn"] + c), rstd[:], ALU.mult, ALU.mult,
                                      [xt, vec, rstd], [hh_])
                                yield

                        for _ in prepA(0):
                            pass
                        for n in range(NT):
                            cs = slice(n * TT, (n + 1) * TT)
                            hh = hhs[n % 2]
                            bg = prepA(n + 1) if n + 1 < NT else None
                            ssq = None
                            deferred = []

                            def run_deferred():
                                while deferred:
                                    deferred.pop(0)()

                            for j in range(23):
                                w_ = wts[(n * 23 + j) % NWI]
                                K.dma(pool, [(w_[:], wb_in[j])], writes=[w_])
                                p = K.ps()
                                for c in range(DC):
                                    K.mm(p[:], w_[:, c * 128:(c + 1) * 128], hh[:, c, :], c == 0, c == DC - 1,
                                         [w_, hh], [p])
                                run_deferred()
                                if j < 4:
                                    t_ = pst[j % 4]
                                    K.copy(act, t_[:], p[:], [p], [t_])
                                    K.dma(sp, [(Psc[j][:, cs], t_[:])], reads=[t_])
                                elif j < 10:
                                    grp0, ng, dstb, gcol = (4, 4, cqn, VL["qn"]) if j < 8 else (8, 2, ckvn, VL["kvn"])
                                    c_ = j - grp0
                                    K.copy(act, craw[:, c_, :], p[:], [p], [craw])
                                    q_ = sqs[sqi % 4]
                                    sqi += 1
                                    K.actf(q_[:], p[:], AF.Square, [p], [q_])

                                    def dq(c_=c_, ng=ng, dstb=dstb, gcol=gcol, q_=q_):
                                        nonlocal ssq
                                        if c_ == 0:
                                            ssq = K.ps_hold()
                                        K.mm(ssq[:], ones[:], q_[:], c_ == 0, c_ == ng - 1, [ones, q_], [ssq])
                                        if c_ == ng - 1:
                                            rstd_from_ss(rstd2[:], ssq[:], ng * 128, None, [ssq], [rstd2])
                                            K.ps_release(ssq)
                                            for cc in range(ng):
                                                K.stt(dve, dstb[:, cc, cs], craw[:, cc, :], V(gcol + cc), rstd2[:],
                                                      ALU.mult, ALU.mult, [craw, vec, rstd2], [dstb])

                                    deferred.append(dq)
                                elif j == 10:
                                    K.copy(act, kB[:], p[:], [p], [kB])
                                    q_ = sqs[sqi % 4]
                                    sqi += 1
                                    K.actf(q_[0:64, :], p[0:64, :], AF.Square, [p], [q_])
                                    K.stt(dve, kR[:], kB[:], V(VL["khB"]), rope[:, cs], ALU.mult, ALU.mult,
                                          [kB, vec, rope], [kR])

                                    def dk(q_=q_):
                                        p2 = K.ps()
                                        K.mm(p2[:], ones[0:64, :], q_[0:64, :], True, True, [ones, q_], [p2])
                                        K.copy(act, sspe[:, cs], p2[:], [p2], [sspe])
                                        p3 = K.ps()
                                        K.mm(p3[:], fold[:], kR[:], True, True, [fold, kR], [p3])
                                        K.copy(act, K2u[:, cs], p3[:], [p3], [K2u])

                                    deferred.append(dk)
                                else:
                                    t_ = ust[j % 6]
                                    K.copy(act, t_[:], p[:], [p], [t_])
                                    K.dma(sp, [(Usc[j - 11][:, cs], t_[:])], reads=[t_])
                                if bg is not None:
                                    next(bg, None)
                                    next(bg, None)
                            run_deferred()
                            if bg is not None:
                                for _ in bg:
                                    pass
                        K.barrier()
                    with ExitStack() as st:
                        Vt = K.sb(st, "Vt", [128, TC, 1024], BF16)
                        wv = K.sb(st, "wv", [128, 2 * 1024], BF16)
                        K.dma(pool, [(wv[:], wb_ukv)], writes=[wv])
                        for kt in range(TC):
                            for hf_ in range(2):
                                p = K.ps()
                                for c in range(2):
                                    K.mm(p[:], ckvn[:, c, kt * 128:(kt + 1) * 128],
                                         wv[:, c * 1024 + hf_ * 512:c * 1024 + (hf_ + 1) * 512], c == 0, c == 1,
                                         [ckvn, wv], [p])
                                K.copy(act if hf_ == 0 else dve, Vt[:, kt, hf_ * 512:(hf_ + 1) * 512], p[:], [p], [Vt])
                        wqA = [K.sb(st, f"wqA{i}", [128, 4 * 128], BF16) for i in range(2)]
                        wqB = [K.sb(st, f"wqB{i}", [128, 4 * 128], BF16) for i in range(2)]
                        wkn = [K.sb(st, f"wkn{i}", [128, 2 * 128], BF16) for i in range(2)]
                        Kn = [K.sb(st, f"Kn{i}", [128, L], BF16) for i in range(2)]
                        Kr = [K.sb(st, f"Kr{i}", [128, L], BF16) for i in range(2)]
                        Qn = [K.sb(st, f"Qn{i}", [128, TT], BF16) for i in range(2)]
                        Qr = [K.sb(st, f"Qr{i}", [128, TT], BF16) for i in range(2)]
                        qrawA = K.sb(st, "qrawA", [128, TT], F32)
                        qrawB = K.sb(st, "qrawB", [128, TT], F32)
                        qtmp = K.sb(st, "qtmp", [128, TT], F32)
                        kraw = [K.sb(st, f"kraw{i}", [128, TT], F32) for i in range(2)]
                        sqk = [K.sb(st, f"sqk{i}", [128, TT], BF16) for i in range(2)]
                        sqq = [K.sb(st, f"sqq{i}", [128, TT], BF16) for i in range(2)]
                        rk = K.sb(st, "rk", [128, TT], F32)
                        rq = K.sb(st, "rq", [128, TT], F32)
                        rl = K.sb(st, "rl", [128, TT], F32)
                        Pb = [K.sb(st, f"Pb{i}", [128, TT], BF16) for i in range(4)]
                        yst = [K.sb(st, f"ybst{i}", [128, L], BF16) for i in range(2)]

                        def load_head_w(h):
                            K.dma(pool, [(wqA[h % 2][:], wb_uq[2 * h])], writes=[wqA[h % 2]])
                            K.dma(pool, [(wqB[h % 2][:], wb_uq[2 * h + 1])], writes=[wqB[h % 2]])
                            K.dma(pool, [(wkn[h % 2][:], wb_ukn[h])], writes=[wkn[h % 2]])

                        def Kprep(h):
                            wK, Kn_, Kr_ = wkn[h % 2], Kn[h % 2], Kr[h % 2]
                            for n in range(NT):
                                cs_ = slice(n * TT, (n + 1) * TT)
                                kr_, q_ = kraw[n % 2], sqk[n % 2]
                                p = K.ps()
                                for c in range(2):
                                    K.mm(p[:], wK[:, c * 128:(c + 1) * 128], ckvn[:, c, cs_], c == 0, c == 1, [wK, ckvn], [p])
                                K.copy(dve, kr_[:], p[:], [p], [kr_])
                                K.tt(dve, q_[:], kr_[:], kr_[:], ALU.mult, [kr_], [q_])
                                yield
                                p2 = K.ps()
                                K.mm(p2[:], ones[:], q_[:], True, True, [ones, q_], [p2])
                                K.tt(dve, rk[:], p2[:], sspe[:, cs_], ALU.add, [p2, sspe], [rk])
                                yield
                                K.actf(rk[:], rk[:], AF.Ln, [rk, cst], [rk], bias=cst[:, 0:1], scale=1.0 / 192.0)
                                yield
                                K.actf(rk[:], rk[:], AF.Exp, [rk], [rk], scale=-0.5)
                                yield
                                K.stt(dve, Kn_[:, cs_], kr_[:], V(VL["khA"]), rk[:], ALU.mult, ALU.mult, [kr_, vec, rk], [Kn_])
                                K.tt(pool, Kr_[:, cs_], K2u[:, cs_], rk[:], ALU.mult, [K2u, rk], [Kr_])
                                yield

                        def Qprep(h, n):
                            k_ = h * NT + n
                            wA, wB, Qn_, Qr_ = wqA[h % 2], wqB[h % 2], Qn[k_ % 2], Qr[k_ % 2]
                            cs_ = slice(n * TT, (n + 1) * TT)
                            qa, qb = sqq[0], sqq[1]
                            pA = K.ps()
                            for c in range(4):
                                K.mm(pA[:], wA[:, c * 128:(c + 1) * 128], cqn[:, c, cs_], c == 0, c == 3, [wA, cqn], [pA])
                            K.copy(dve, qrawA[:], pA[:], [pA], [qrawA])
                            K.tt(dve, qa[:], qrawA[:], qrawA[:], ALU.mult, [qrawA], [qa])
                            yield
                            pB = K.ps()
                            for c in range(4):
                                K.mm(pB[:], wB[:, c * 128:(c + 1) * 128], cqn[:, c, cs_], c == 0, c == 3, [wB, cqn], [pB])
                            K.copy(dve, qrawB[:], pB[:], [pB], [qrawB])
                            K.tt(dve, qb[0:64, :], qrawB[0:64, :], qrawB[0:64, :], ALU.mult, [qrawB], [qb])
                            yield
                            p2 = K.ps()
                            K.mm(p2[:], ones[:], qa[:], True, False, [ones, qa], [p2])
                            K.mm(p2[:], ones[0:64, :], qb[0:64, :], False, True, [ones, qb], [p2])
                            K.copy(dve, rq[:], p2[:], [p2], [rq])
                            yield
                            K.actf(rq[:], rq[:], AF.Ln, [rq, cst], [rq], bias=cst[:, 1:2], scale=1.0)
                            yield
                            K.actf(rq[:], rq[:], AF.Exp, [rq], [rq], scale=-0.5)
                            yield
                            K.stt(dve, Qn_[:], qrawA[:], V(VL["qhA"]), rq[:], ALU.mult, ALU.mult, [qrawA, vec, rq], [Qn_])
                            K.stt(dve, qtmp[:], qrawB[:], V(VL["qhB"]), rq[:], ALU.mult, ALU.mult, [qrawB, vec, rq], [qtmp])
                            K.tt(pool, Qr_[:], qtmp[:], rope[:, cs_], ALU.mult, [qtmp, rope], [Qr_])
                            yield

                        pw = K.sb(st, "pw", [128, 4 * 128], BF16)
                        Up = K.sb(st, "Up", [128, L + 16], F32)
                        ta = K.sb(st, "ta", [128, L + 16], F32)
                        tb = K.sb(st, "tb", [128, L + 16], F32)
                        pooled = K.sb(st, "pooled", [128, L], BF16)
                        ypst = [K.sb(st, f"ypst{i}", [128, L], BF16) for i in range(2)]

                        def poolgen():
                            K.dma(pool, [(pw[:], wb_pool)], writes=[pw])
                            K.memset(dve, Up[:, 0:8], 0.0, [Up])
                            K.memset(dve, Up[:, L + 8:L + 16], 0.0, [Up])
                            for g in range(4):
                                w = POOL_WINDOWS[g]
                                half = w // 2
                                K.dma(sp, [(Up[:, 8:8 + L], Psc[g])], writes=[Up])
                                yield
                                if w == 2:
                                    K.tt(dve, ta[:, 0:L], Up[:, 7:7 + L], Up[:, 8:8 + L], ALU.add, [Up], [ta])
                                    sres = ta
                                    yield
                                else:
                                    K.tt(dve, ta[:, 0:L + 15], Up[:, 0:L + 15], Up[:, 1:L + 16], ALU.add, [Up], [ta])
                                    yield
                                    cur, other, span, valid = ta, tb, 2, L + 15
                                    while span * 2 < w:
                                        K.tt(dve, other[:, 0:valid - span], cur[:, 0:valid - span], cur[:, span:valid], ALU.add,
                                             [cur], [other])
                                        yield
                                        valid -= span
                                        cur, other = other, cur
                                        span *= 2
                                    K.tt(dve, other[:, 0:L], cur[:, 8 - half:8 - half + L], cur[:, 8:8 + L], ALU.add, [cur], [other])
                                    sres = other
                                    yield
                                K.tt(dve, sres[:, 0:8], sres[:, 0:8], corr[:, g * 16:g * 16 + 8], ALU.mult, [sres, corr], [sres])
                                K.tt(dve, sres[:, L - 8:L], sres[:, L - 8:L], corr[:, g * 16 + 8:g * 16 + 16], ALU.mult,
                                     [sres, corr], [sres])
                                yield
                                K.stt(dve, pooled[:], sres[:, 0:L], 1.0 / w, Up[:, 8:8 + L], ALU.mult, ALU.subtract,
                                      [sres, Up], [pooled])
                                yield
                                yp_ = ypst[g % 2]
                                for n in range(NT):
                                    cs_ = slice(n * TT, (n + 1) * TT)
                                    p = K.ps()
                                    K.mm(p[:], pw[:, g * 128:(g + 1) * 128], pooled[:, cs_], True, True, [pw, pooled], [p])
                                    K.ts(dve, yp_[:, cs_], p[:], V(VL["ps"] + g), None, ALU.mult, None, [p, vec], [yp_])
                                    yield
                                K.dma(sp, [(Ysc[s, g], yp_[:])], reads=[yp_])
                                yield

                        pgen = poolgen()
                        load_head_w(0)
                        for _ in Kprep(0):
                            pass
                        for _ in Qprep(0, 0):
                            pass
                        pbi = 0
                        kbg = None
                        sring = [K.ps_hold() for _ in range(3)]
                        for h in range(H):
                            y_ = yst[h % 2]
                            Kn_, Kr_ = Kn[h % 2], Kr[h % 2]
                            if h + 1 < H:
                                load_head_w(h + 1)
                                kbg = Kprep(h + 1)
                            else:
                                kbg = None
                            for n in range(NT):
                                cs = slice(n * TT, (n + 1) * TT)
                                k_ = h * NT + n
                                Qn_, Qr_ = Qn[k_ % 2], Qr[k_ % 2]
                                if n + 1 < NT:
                                    qbg = Qprep(h, n + 1)
                                elif h + 1 < H:
                                    qbg = Qprep(h + 1, 0)
                                else:
                                    qbg = None
                                o_ps = K.ps_hold()
                                l_ps = K.ps_hold()

                                def smm(kt):
                                    sp_ = sring[kt % 3]
                                    ks = slice(kt * 128, (kt + 1) * 128)
                                    K.mm(sp_[:], Kn_[:, ks], Qn_[:], True, False, [Kn_, Qn_], [sp_])
                                    K.mm(sp_[:], Kr_[:, ks], Qr_[:], False, True, [Kr_, Qr_], [sp_])
                                    return sp_

                                S = {0: smm(0)}
                                if TC > 1:
                                    S[1] = smm(1)
                                for kt in range(TC):
                                    if kt + 2 < TC:
                                        S[kt + 2] = smm(kt + 2)
                                    s_cur = S.pop(kt)
                                    P = Pb[pbi % 4]
                                    pbi += 1
                                    K.actf(P[:], s_cur[:], AF.Exp, [s_cur], [P])
                                    K.mm(o_ps[:], Vt[:, kt, h * 128:(h + 1) * 128], P[:], kt == 0, kt == TC - 1, [Vt, P], [o_ps])
                                    K.mm(l_ps[:], ones[:], P[:], kt == 0, kt == TC - 1, [ones, P], [l_ps])
                                    if qbg is not None:
                                        next(qbg, None)
                                    if kbg is not None and kt % 2 == 1:
                                        next(kbg, None)
                                    if kt % 2 == 0:
                                        next(pgen, None)
                                K.op(dve, lambda: nc.vector.reciprocal(rl[:], l_ps[:]), [l_ps], [rl])
                                K.tt(dve, y_[:, cs], o_ps[:], rl[:], ALU.mult, [o_ps, rl], [y_])
                                K.ps_release(o_ps)
                                K.ps_release(l_ps)
                                if qbg is not None:
                                    for _ in qbg:
                                        pass
                            if kbg is not None:
                                for _ in kbg:
                                    pass
                            K.dma(sp, [(Ysc[s, 4 + h], y_[:])], reads=[y_])
                        for _ in pgen:
                            pass
                        for b_ in sring:
                            K.ps_release(b_)
                        K.barrier()
                with ExitStack() as st:
                    X0 = K.sb(st, "X0", [128, 4, L], BF16)
                    Zt = K.sb(st, "Zt", [128, 4, L], BF16)
                    Yr = K.sb(st, "Yr", [128, TC, HY], BF16)
                    Yi = K.sb(st, "Yi", [128, TC, HY], BF16)
                    ci0 = K.sb(st, "ci0", [128, TC * TT], BF16)
                    si0 = K.sb(st, "si0", [128, TC * TT], BF16)
                    K.dma(sp, [(ci0[:], dft_i[0, 0])], writes=[ci0])
                    K.dma(sp, [(si0[:], dft_i[0, 1])], writes=[si0])
                    with ExitStack() as st2:
                        ur = [K.sb(st2, f"ur{i}", [128, L + 2], BF16) for i in range(3)]
                        ctmp = [K.sb(st2, f"ctmp{i}", [128, L], F32) for i in range(4)]
                        for u_ in ur:
                            K.memset(pool, u_[:, 0:1], 0.0, [u_])
                            K.memset(pool, u_[:, L + 1:L + 2], 0.0, [u_])
                        ui = 0

                        def hconv(c, dst_ap, dst_bufs):
                            nonlocal ui
                            u_ = ur[ui % 3]
                            ui += 1
                            K.dma(sp, [(u_[:, 1:L + 1], Usc[c])], writes=[u_])
                            t_ = ctmp[ui % 3]
                            K.actf(t_[:], u_[:, 1:L + 1], AF.Identity, [u_, vec], [t_],
                                   bias=V(VL["hyb"] + c), scale=V(VL["hyw"] + 12 + c))
                            K.stt(dve, t_[:], u_[:, 0:L], V(VL["hyw"] + c), t_[:], ALU.mult, ALU.add,
                                  [u_, vec, t_], [t_])
                            K.stt(dve, dst_ap, u_[:, 2:L + 2], V(VL["hyw"] + 24 + c), t_[:], ALU.mult, ALU.add,
                                  [u_, vec, t_], dst_bufs)

                        for c in range(4):
                            hconv(c, X0[:, c, :], [X0])
                            hconv(4 + c, ctmp[3][:], [ctmp[3]])
                            hconv(8 + c, Zt[:, c, :], [Zt])
                            K.tt(pool, Zt[:, c, :], Zt[:, c, :], ctmp[3][:], ALU.mult, [Zt, ctmp[3]], [Zt])
                        K.barrier()
                    with ExitStack() as st2:
                        Ztm = K.sb(st2, "Ztm", [128, TC, HY], BF16)
                        for j in range(TC):
                            p = K.ps()
                            for c in range(4):
                                K.mm(p[:, c * 128:(c + 1) * 128], Zt[:, c, j * 128:(j + 1) * 128], ident[:], True, True,
                                     [Zt, ident], [p])
                            K.copy(act if j % 2 == 0 else dve, Ztm[:, j, :], p[:], [p], [Ztm])
                        wc = [K.sb(st2, f"hwc{i}", [128, TC * 128], BF16) for i in range(2)]
                        wsn = [K.sb(st2, f"hws{i}", [128, TC * 128], BF16) for i in range(2)]
                        kre = [K.sb(st2, f"kre{i}", [128, HY], F32) for i in range(2)]
                        kim = [K.sb(st2, f"kim{i}", [128, HY], F32) for i in range(2)]
                        As = [K.sb(st2, f"As{i}", [128, HY], F32) for i in range(2)]
                        Bs = [K.sb(st2, f"Bs{i}", [128, HY], F32) for i in range(2)]
                        t1 = K.sb(st2, "t1", [128, HY], F32)
                        t2 = K.sb(st2, "t2", [128, HY], F32)
                        t3 = K.sb(st2, "t3", [128, HY], F32)
                        t4 = K.sb(st2, "t4", [128, HY], F32)
                        for fc in range(TC):
                            i2 = fc % 2
                            K.dma(sp, [(wc[i2][:], dft_f[fc, 0])], writes=[wc[i2]])
                            K.dma(sp, [(wsn[i2][:], dft_f[fc, 1])], writes=[wsn[i2]])
                            K.dma(sp, [(kre[i2][:], Kf[0, fc])], writes=[kre[i2]])
                            K.dma(sp, [(kim[i2][:], Kf[1, fc])], writes=[kim[i2]])
                            pa = K.ps()
                            for j in range(TC):
                                K.mm(pa[:], wc[i2][:, j * 128:(j + 1) * 128], Ztm[:, j, :], j == 0, j == TC - 1,
                                     [wc[i2], Ztm], [pa])
                            pb = K.ps()
                            for j in range(TC):
                                K.mm(pb[:], wsn[i2][:, j * 128:(j + 1) * 128], Ztm[:, j, :], j == 0, j == TC - 1,
                                     [wsn[i2], Ztm], [pb])
                            A_, B_ = As[i2], Bs[i2]
                            K.copy(act, A_[:], pa[:], [pa], [A_])
                            K.copy(act, B_[:], pb[:], [pb], [B_])
                            K.tt(dve, t1[:], kre[i2][:], A_[:], ALU.mult, [kre[i2], A_], [t1])
                            K.tt(pool, t2[:], kim[i2][:], B_[:], ALU.mult, [kim[i2], B_], [t2])
                            K.tt(dve, Yr[:, fc, :], t1[:], t2[:], ALU.add, [t1, t2], [Yr])
                            K.tt(pool, t3[:], kim[i2][:], A_[:], ALU.mult, [kim[i2], A_], [t3])
                            K.tt(dve, t4[:], kre[i2][:], B_[:], ALU.mult, [kre[i2], B_], [t4])
                            K.tt(pool, Yi[:, fc, :], t3[:], t4[:], ALU.subtract, [t3, t4], [Yi])
                        K.barrier()
                    with ExitStack() as st2:
                        ci = [ci0, K.sb(st2, "ci1", [128, TC * TT], BF16)]
                        si = [si0, K.sb(st2, "si1", [128, TC * TT], BF16)]
                        ysth = K.sb(st2, "ysth", [128, 4, L], BF16)
                        ytmp = [K.sb(st2, f"ytmp{i}", [128, TT], F32) for i in range(2)]
                        for n in range(NT):
                            cs = slice(n * TT, (n + 1) * TT)
                            ci_, si_ = ci[n % 2], si[n % 2]
                            if n + 1 < NT:
                                K.dma(sp, [(ci[(n + 1) % 2][:], dft_i[n + 1, 0])], writes=[ci[(n + 1) % 2]])
                                K.dma(sp, [(si[(n + 1) % 2][:], dft_i[n + 1, 1])], writes=[si[(n + 1) % 2]])
                            for c in range(4):
                                p = K.ps()
                                for fc in range(TC):
                                    K.mm(p[:], Yr[:, fc, c * 128:(c + 1) * 128], ci_[:, fc * TT:(fc + 1) * TT], fc == 0, False,
                                         [Yr, ci_], [p])
                                    K.mm(p[:], Yi[:, fc, c * 128:(c + 1) * 128], si_[:, fc * TT:(fc + 1) * TT], False,
                                         fc == TC - 1, [Yi, si_], [p])
                                yt_ = ytmp[c % 2]
                                K.stt(dve, yt_[:], Zt[:, c, cs], V(VL["hybias"] + c), p[:], ALU.mult, ALU.add,
                                      [Zt, vec, p], [yt_])
                                K.tt(pool, ysth[:, c, cs], yt_[:], X0[:, c, cs], ALU.mult, [yt_, X0], [ysth])
                        for c in range(4):
                            K.dma(sp, [(Ysc[s, 12 + c], ysth[:, c, :])], reads=[ysth])
                        K.barrier()
            if dbg and l == 0:
                with ExitStack() as st:
                    dtile = K.sb(st, "dtile", [128, L], BF16)
                    for s in range(NSEQ):
                        for c in range(16):
                            K.dma(sp, [(dtile[:], Ysc[s, c])], writes=[dtile])
                            K.dma(sp, [(dbg_out[s, c], dtile[:])], reads=[dtile])
                    K.barrier()
            tiles = [(s, n) for s in range(NSEQ) for n in range(NT)]
            if l == 0:
                wait_converted(l, 1)
            cgen = convert_gen(l + 1) if l + 1 < DEPTH else None
            with ExitStack() as st:
                Yts = [K.sb(st, f"Ytc{i}", [128, DC, TT], BF16) for i in range(2)]
                Ms = [K.sb(st, f"Mc{i}", [128, DC, TT], BF16) for i in range(2)]
                sqs = [K.sb(st, f"sqc{i}", [128, TT], BF16) for i in range(3)]
                rg = [K.sb(st, f"rg{i}", [128, TT], F32) for i in range(3)]
                wos = [K.sb(st, f"wo{j}", [128, 16 * 128], BF16) for j in range(16)]
                NXC = 8
                xrc = [K.sb(st, f"xrc{i}", [128, TT], F32) for i in range(NXC)]
                xoc = [K.sb(st, f"xoc{i}", [128, TT], F32) for i in range(NXC)]
                for j in range(16):
                    K.dma(pool, [(wos[j][:], wb_out[j])], writes=[wos[j]])

                def prepC(i):
                    s_, n_ = tiles[i]
                    cs_ = slice(n_ * TT, (n_ + 1) * TT)
                    Yt_, M_ = Yts[i % 2], Ms[i % 2]
                    K.dma(sp, [(Yt_[:], Ysc[s_].rearrange("c p t -> p c t")[:, :, cs_])], writes=[Yt_])
                    yield
                    yield
                    for gi, (c0, c1) in enumerate(((0, 4), (4, 12), (12, 16))):
                        ssb = K.ps_hold()
                        for c in range(c0, c1):
                            q_ = sqs[c % 3]
                            K.actf(q_[:], Yt_[:, c, :], AF.Square, [Yt_], [q_])
                            yield
                            K.mm(ssb[:], ones[:], q_[:], c == c0, c == c1 - 1, [ones, q_], [ssb])
                        yield
                        rstd_from_ss(rg[gi][:], ssb[:], (c1 - c0) * 128, None, [ssb], [rg[gi]])
                        K.ps_release(ssb)
                        yield
                        for c in range(c0, c1):
                            K.stt(dve, M_[:, c, :], Yt_[:, c, :], V(VL["grp"] + c), rg[gi][:], ALU.mult, ALU.mult,
                                  [Yt_, vec, rg[gi]], [M_])
                            yield

                for _ in prepC(0):
                    pass
                xi = 0
                for i, (s, n) in enumerate(tiles):
                    cs = slice(n * TT, (n + 1) * TT)
                    bg = prepC(i + 1) if i + 1 < len(tiles) else None
                    M = Ms[i % 2]
                    xsrc = Xin[s]

                    def loadXc(j, k):
                        K.dma(sp, [(xrc[k % NXC][:], xsrc[j * 128:(j + 1) * 128, cs])], writes=[xrc[k % NXC]])

                    for j0 in range(4):
                        loadXc(j0, xi + j0)
                    for j in range(16):
                        if j + 4 < 16:
                            loadXc(j + 4, xi + j + 4)
                        w_ = wos[j]
                        p = K.ps()
                        for c in range(DC):
                            K.mm(p[:], w_[:, c * 128:(c + 1) * 128], M[:, c, :], c == 0, c == DC - 1, [w_, M], [p])
                        xr_, xo_ = xrc[(xi + j) % NXC], xoc[(xi + j) % NXC]
                        K.tt(dve, xo_[:], p[:], xr_[:], ALU.add, [p, xr_], [xo_])
                        K.dma(sp, [(xmid[s][j * 128:(j + 1) * 128, cs], xo_[:])], reads=[xo_])
                        if bg is not None:
                            for _ in range(3):
                                next(bg, None)
                    xi += 16
                    if bg is not None:
                        for _ in bg:
                            pass
                K.barrier()
            with ExitStack() as st:
                W = TT + 2
                xt = K.sb(st, "xtb", [128, DC, W], F32)
                hhs = [K.sb(st, f"hhb{i}", [128, DC, W], BF16) for i in range(2)]
                actb = K.sb(st, "actb", [128, FC, TT], BF16)
                sqs = [K.sb(st, f"sqb{i}", [128, TT], BF16) for i in range(3)]
                sqh = K.sb(st, "sqh", [128, DC, 2], BF16)
                rstd = K.sb(st, "rstdb", [128, W], F32)
                wg = [K.sb(st, f"wg{i}", [128, 16 * 128], BF16) for i in range(3)]
                wv_ = [K.sb(st, f"wvv{i}", [128, 16 * 128], BF16) for i in range(3)]
                wd = [K.sb(st, f"wd{i}", [128, FC * 128], BF16) for i in range(2)]
                G = [K.sb(st, f"G{i}", [128, W], F32) for i in range(2)]
                cen = [K.sb(st, f"cen{i}", [128, TT], F32) for i in range(2)]
                sg = [K.sb(st, f"sg{i}", [128, TT], F32) for i in range(2)]
                xr = [K.sb(st, f"xr{i}", [128, TT], F32) for i in range(3)]
                xo = [K.sb(st, f"xo{i}", [128, TT], F32) for i in range(3)]

                def prepB(i):
                    s, n = tiles[i]
                    hh = hhs[i % 2]
                    t0 = n * TT
                    lo = max(t0 - 1, 0)
                    hi = min(t0 + TT + 1, L)
                    if n == 0:
                        K.memset(pool, xt[:, :, 0:1], 0.0, [xt])
                    if n == NT - 1:
                        K.memset(pool, xt[:, :, W - 1:W], 0.0, [xt])
                    K.dma(sp, [(xt[:, :, lo - (t0 - 1):hi - (t0 - 1)],
                                xmid[s].rearrange("(c p) t -> p c t", p=128)[:, :, lo:hi])], writes=[xt])
                    yield
                    yield
                    ssA = K.ps_hold()
                    ssB = K.ps_hold()
                    K.actf(sqh[:], xt[:, :, TT:W], AF.Square, [xt], [sqh])
                    for c in range(DC):
                        q_ = sqs[c % 3]
                        K.actf(q_[:], xt[:, c, 0:TT], AF.Square, [xt], [q_])
                        yield
                        K.mm(ssA[:], ones[:], q_[:], c == 0, c == DC - 1, [ones, q_], [ssA])
                        K.mm(ssB[:, 0:2], ones[:], sqh[:, c, :], c == 0, c == DC - 1, [ones, sqh], [ssB])
                    yield
                    rstd_from_ss(rstd[:, 0:TT], ssA[:], D, None, [ssA], [rstd])
                    rstd_from_ss(rstd[:, TT:W], ssB[:, 0:2], D, None, [ssB], [rstd])
                    K.ps_release(ssA)
                    K.ps_release(ssB)
                    yield
                    for c in range(DC):
                        K.stt(dve, hh[:, c, :], xt[:, c, :], V(VL["ffn"] + c), rstd[:], ALU.mult, ALU.mult,
                              [xt, vec, rstd], [hh])
                        yield

                for _ in prepB(0):
                    pass
                for i, (s, n) in enumerate(tiles):
                    t0 = n * TT
                    cs = slice(t0, t0 + TT)
                    hh = hhs[i % 2]
                    bg = prepB(i + 1) if i + 1 < len(tiles) else None

                    def loadB(j):
                        K.dma(pool, [(wg[j % 3][:], wb_upg[j])], writes=[wg[j % 3]])
                        K.dma(pool, [(wv_[j % 3][:], wb_upv[j])], writes=[wv_[j % 3]])

                    loadB(0)
                    if FC > 1:
                        loadB(1)
                    for j in range(FC):
                        if j + 2 < FC:
                            loadB(j + 2)
                        g_, v_ = wg[j % 3], wv_[j % 3]
                        gA = K.ps()
                        for c in range(DC):
                            K.mm(gA[:], g_[:, c * 128:(c + 1) * 128], hh[:, c, 0:TT], c == 0, c == DC - 1, [g_, hh], [gA])
                        gB = K.ps()
                        for c in range(DC):
                            K.mm(gB[:, 0:2], g_[:, c * 128:(c + 1) * 128], hh[:, c, TT:W], c == 0, c == DC - 1, [g_, hh], [gB])
                        vP = K.ps()
                        for c in range(DC):
                            K.mm(vP[:], v_[:, c * 128:(c + 1) * 128], hh[:, c, 1:TT + 1], c == 0, c == DC - 1, [v_, hh], [vP])
                        G_, cen_, sg_ = G[j % 2], cen[j % 2], sg[j % 2]
                        K.copy(act, G_[:, 0:TT], gA[:], [gA], [G_])
                        K.copy(act, G_[:, TT:W], gB[:, 0:2], [gB], [G_])
                        K.actf(cen_[:], G_[:, 1:TT + 1], AF.Identity, [G_, vec], [cen_],
                               bias=V(VL["fcb"] + j), scale=V(VL["fcw"] + FC + j))
                        K.stt(dve, cen_[:], G_[:, 0:TT], V(VL["fcw"] + j), cen_[:], ALU.mult, ALU.add,
                              [G_, vec, cen_], [cen_])
                        K.stt(dve, cen_[:], G_[:, 2:W], V(VL["fcw"] + 2 * FC + j), cen_[:], ALU.mult, ALU.add,
                              [G_, vec, cen_], [cen_])
                        K.actf(sg_[:], cen_[:], AF.Silu, [cen_], [sg_])
                        K.tt(dve, actb[:, j, :], sg_[:], vP[:], ALU.mult, [sg_, vP], [actb])
                        if bg is not None:
                            next(bg, None)
                        if cgen is not None:
                            next(cgen, None)
                    K.dma(pool, [(wd[0][:], wb_dn[0])], writes=[wd[0]])
                    xsrc = xmid[s]

                    def loadX(dc):
                        K.dma(sp, [(xr[dc % 3][:], xsrc[dc * 128:(dc + 1) * 128, cs])], writes=[xr[dc % 3]])

                    loadX(0)
                    loadX(1)
                    for dc in range(DC):
                        if dc + 1 < DC:
                            K.dma(pool, [(wd[(dc + 1) % 2][:], wb_dn[dc + 1])], writes=[wd[(dc + 1) % 2]])
                        if dc + 2 < DC:
                            loadX(dc + 2)
                        d_ = wd[dc % 2]
                        p = K.ps()
                        for j in range(FC):
                            K.mm(p[:], d_[:, j * 128:(j + 1) * 128], actb[:, j, :], j == 0, j == FC - 1, [d_, actb], [p])
                        xo_ = xo[dc % 3]
                        K.tt(dve, xo_[:], p[:], xr[dc % 3][:], ALU.add, [p, xr[dc % 3]], [xo_])
                        K.dma(sp, [(Xout[s][dc * 128:(dc + 1) * 128, cs], xo_[:])], reads=[xo_])
                        if bg is not None:
                            next(bg, None)
                            next(bg, None)
                    if bg is not None:
                        for _ in bg:
                            pass
                if cgen is not None:
                    for _ in cgen:
                        pass
                K.barrier()
        K.barrier()
        print(f"[build] instructions emitted: {K.n_ins}")
    return nc


def _tile_lhsT(w, cols_list):
    Kd = w.shape[0]
    kc = Kd // 128
    outs = []
    for cols in cols_list:
        sel = w[:, cols]
        outs.append(sel.reshape(kc, 128, 128).transpose(1, 0, 2).reshape(128, kc * 128))
    return np.ascontiguousarray(np.stack(outs, 0))


def _pvec(v):
    return np.ascontiguousarray(v.reshape(-1, 128).T)


def make_consts(L):
    TC = L // 128
    NT = L // TT
    f32 = np.float32
    t = np.linspace(0.0, 1.0, L, dtype=f32)[:, None]
    w = (2.0 * math.pi * np.arange(L, dtype=f32)[:, None] / L).astype(f32)
    bands = np.linspace(1e-4, 8 - 1, 8, dtype=f32)[None, :]
    z = np.concatenate([t, np.cos(bands * w), -np.sin(bands * w)], axis=-1).astype(f32)
    max_decay = math.log(1e-2) / 0.3
    min_decay = math.log(1e-2) / 1.5
    deltas = np.linspace(min_decay, max_decay, HY, dtype=f32)
    dec = np.exp(-t * np.abs(deltas)[None, :]).astype(f32)
    tt_ = np.arange(L, dtype=np.float64)
    om = 2.0 * math.pi * (np.arange(L, dtype=np.float64) + 0.5) / (2.0 * L)
    ang = np.outer(tt_, om)
    Cf = np.cos(ang)
    Sf = np.sin(ang)
    dft_f = np.empty((TC, 2, 128, TC * 128), dtype=ml_dtypes.bfloat16)
    for fc in range(TC):
        for k, M_ in enumerate((Cf, Sf)):
            blk = M_[:, fc * 128:(fc + 1) * 128]
            dft_f[fc, k] = blk.reshape(TC, 128, 128).transpose(1, 0, 2).reshape(128, TC * 128).astype(ml_dtypes.bfloat16)
    dft_i = np.empty((NT, 2, 128, TC * TT), dtype=ml_dtypes.bfloat16)
    for n in range(NT):
        for k, M_ in enumerate((Cf / L, -Sf / L)):
            blk = M_[n * TT:(n + 1) * TT, :].T
            dft_i[n, k] = blk.reshape(TC, 128, TT).transpose(1, 0, 2).reshape(128, TC * TT).astype(ml_dtypes.bfloat16)
    freqs = (10000.0 ** (-np.arange(0, 64, 2, dtype=f32) / 64)).astype(f32)
    angr = (np.arange(L, dtype=f32)[:, None] * freqs[None, :]).astype(f32)
    cs_ = np.cos(angr.astype(np.float64)).T.astype(f32)
    sn_ = np.sin(angr.astype(np.float64)).T.astype(f32)
    rope = np.concatenate([cs_, cs_, -sn_, sn_], axis=0).astype(f32)
    p_ = np.arange(128)
    fold = (p_[:, None] % 64 == p_[None, :] % 64).astype(ml_dtypes.bfloat16)
    ident = np.eye(128).astype(ml_dtypes.bfloat16)
    pc = np.ones((4, 16), dtype=f32)
    for g, wdw in enumerate(POOL_WINDOWS):
        half = wdw // 2
        for k in range(16):
            tk = k if k < 8 else L - 16 + k
            cnt = min(tk + half, L) - max(tk - half, 0)
            pc[g, k] = wdw / cnt
    pcorr = np.broadcast_to(pc.reshape(1, 64), (128, 64)).copy()
    return {
        "zfeatT": np.ascontiguousarray(z.T),
        "decay": np.ascontiguousarray(dec.reshape(TC, 128, HY)),
        "dft_f": dft_f, "dft_i": dft_i, "ropeT": rope, "foldm": fold, "identm": ident, "pcorr": pcorr,
    }


def prep_weights(P, DEPTH, DFF):
    FC = DFF // 128
    VL = vec_layout(FC)
    f = lambda a: np.asarray(a, dtype=np.float32)
    out = {}
    std = lambda n0, n: [np.arange(n0 + 128 * j, n0 + 128 * (j + 1)) for j in range(n)]
    sw64 = lambda base: np.concatenate([np.arange(base + 32, base + 64), np.arange(base, base + 32)])
    cols_in = std(0, 10) + [np.concatenate([np.arange(1280, 1344), sw64(1280)])] + std(1344, 12)
    cols_uq = []
    for h in range(H):
        cols_uq.append(np.arange(192 * h, 192 * h + 128))
        cols_uq.append(np.concatenate([np.arange(192 * h + 128, 192 * h + 192), sw64(192 * h + 128)]))
    cols_kn = [np.arange(256 * h, 256 * h + 128) for h in range(H)]
    cols_v = np.concatenate([np.arange(256 * h + 128, 256 * h + 256) for h in range(H)])
    w_in, w_uq, w_ukn, w_ukv, pw, w_out, w_upg, w_upv, w_dn, vecs = [], [], [], [], [], [], [], [], [], []
    hb3, hvec = [], []
    for l in range(DEPTH):
        w_in.append(_tile_lhsT(f(P["w_in"][l]), cols_in))
        w_uq.append(_tile_lhsT(f(P["mla_w_uq"][l]), cols_uq))
        w_ukn.append(_tile_lhsT(f(P["mla_w_ukv"][l]), cols_kn))
        wv = f(P["mla_w_ukv"][l])[:, cols_v]
        w_ukv.append(wv.reshape(2, 128, 1024).transpose(1, 0, 2).reshape(128, 2048))
        pw.append(f(P["pool_w"][l]).transpose(1, 0, 2).reshape(128, 4 * 128))
        w_out.append(_tile_lhsT(f(P["w_out"][l]), std(0, 16)))
        w_upg.append(_tile_lhsT(f(P["ffn_w_up"][l]), std(0, FC)))
        w_upv.append(_tile_lhsT(f(P["ffn_w_up"][l]), std(DFF, FC)))
        wd = f(P["ffn_w_down"][l])
        w_dn.append(wd.reshape(FC, 128, 16, 128).transpose(2, 1, 0, 3).reshape(16, 128, FC * 128))
        v = np.zeros((128, VL["NV"]), dtype=np.float32)
        v[:, VL["attn"]:VL["attn"] + 16] = _pvec(f(P["attn_norm_g"][l]))
        v[:, VL["ffn"]:VL["ffn"] + 16] = _pvec(f(P["ffn_norm_g"][l]))
        v[:, VL["grp"]:VL["grp"] + 16] = _pvec(f(P["grp_norm_g"][l]))
        v[:, VL["ps"]:VL["ps"] + 4] = _pvec(f(P["pool_scale"][l]))
        v[:, VL["qn"]:VL["qn"] + 4] = _pvec(f(P["mla_q_norm_g"][l]))
        v[:, VL["kvn"]:VL["kvn"] + 2] = _pvec(f(P["mla_kv_norm_g"][l]))
        qh = f(P["mla_q_head_norm_g"][l])
        kh = f(P["mla_k_head_norm_g"][l])
        swg = lambda g: np.concatenate([g[128:192], g[160:192], g[128:160]])
        v[:, VL["qhA"]] = qh[:128]
        v[:, VL["qhB"]] = swg(qh)
        v[:, VL["khA"]] = kh[:128]
        v[:, VL["khB"]] = swg(kh)
        hw = f(P["hy_conv_w"][l])
        for tap in range(3):
            v[:, VL["hyw"] + tap * 12:VL["hyw"] + (tap + 1) * 12] = _pvec(hw[tap])
        v[:, VL["hyb"]:VL["hyb"] + 12] = _pvec(f(P["hy_conv_b"][l]))
        v[:, VL["hybias"]:VL["hybias"] + 4] = _pvec(f(P["hy_bias"][l]))
        fw = f(P["ffn_conv_w"][l])
        for tap in range(3):
            v[:, VL["fcw"] + tap * FC:VL["fcw"] + (tap + 1) * FC] = _pvec(fw[tap])
        v[:, VL["fcb"]:VL["fcb"] + FC] = _pvec(f(P["ffn_conv_b"][l]))
        vecs.append(v)
        hb3.append(np.broadcast_to(f(P["hy_filt_b3"][l])[None, :], (128, 2 * HY)).copy())
        hv = np.zeros((FH, 4), dtype=np.float32)
        hv[:, 0] = f(P["hy_filt_b1"][l])
        hv[:, 1] = f(P["hy_filt_freq"][l])
        hv[:, 2] = f(P["hy_filt_b2"][l])
        hvec.append(hv)
    st_ = lambda lst: np.ascontiguousarray(np.stack(lst, 0))
    out.update(w_in=st_(w_in), w_uq=st_(w_uq), w_ukn=st_(w_ukn), w_ukv=st_(w_ukv), pool_w=st_(pw), w_out=st_(w_out),
               w_upg=st_(w_upg), w_upv=st_(w_upv), w_dn=st_(w_dn), vecs=st_(vecs), hf_b3=st_(hb3), hvec=st_(hvec),
               hf_w1=np.ascontiguousarray(f(P["hy_filt_w1"][:DEPTH])), hf_w2=np.ascontiguousarray(f(P["hy_filt_w2"][:DEPTH])),
               hf_w3=np.ascontiguousarray(f(P["hy_filt_w3"][:DEPTH])))
    return out


def run(cfg, x_all, P, n_cores, trace=False):
    L, NSEQ, DEPTH, DFF = cfg["L"], cfg["NSEQ"], cfg["DEPTH"], cfg["DFF"]
    nc = build(cfg)
    shared = prep_weights(P, DEPTH, DFF)
    shared.update(make_consts(L))
    in_maps = []
    for i in range(n_cores):
        m = dict(shared)
        m["xT"] = np.ascontiguousarray(np.asarray(x_all[i * NSEQ:(i + 1) * NSEQ], dtype=np.float32).transpose(0, 2, 1))
        in_maps.append(m)
    res = run_bass_kernel_spmd(nc, in_maps, core_ids=list(range(n_cores)), trace=trace)
    ys = [np.asarray(r["yT"]).transpose(0, 2, 1) for r in res.results]
    return np.concatenate(ys, 0), res


def kernel(**inputs):
    xp = np.asarray(inputs["x_prompt"], dtype=np.float32)
    xsm = np.asarray(inputs["x_sample"], dtype=np.float32)
    nb = xp.shape[0]
    x_all = np.concatenate([xp, xsm], 0)
    nseq = x_all.shape[0] // N_CORES
    cfg = {"L": x_all.shape[1], "NSEQ": nseq, "DEPTH": inputs["w_in"].shape[0], "DFF": inputs["ffn_conv_b"].shape[1]}
    P = {k: v for k, v in inputs.items() if k not in ("x_prompt", "x_sample")}
    y_all, _ = run(cfg, x_all, P, N_CORES)
    y_all = np.ascontiguousarray(y_all, dtype=np.float32)
    return (np.ascontiguousarray(y_all[:nb]), np.ascontiguousarray(y_all[nb:]))
```

```python
import math
from contextlib import ExitStack
import numpy as np
import ml_dtypes
import concourse.bass as bass
import concourse.mybir as mybir
from concourse.bass_utils import run_bass_kernel_spmd

F32 = mybir.dt.float32
BF16 = mybir.dt.bfloat16
AF = mybir.ActivationFunctionType
ALU = mybir.AluOpType

D = 2048
DC = 16
POOL_WINDOWS = (2, 4, 8, 16)
H = 8
NPF = 17
FH = 64
HY = 512
EPS = 1e-6
TT = 512
N_CORES = 8


class Sem:
    def __init__(self, handle):
        self.h = handle
        self.cnt = 0


class Eng:
    def __init__(self, name, e, sem):
        self.name = name
        self.e = e
        self.sem = sem
        self.seen = {}


class Buf:
    def __init__(self, name, t=None):
        self.name = name
        self.t = t
        self.lw = None
        self.readers = []
        self.sems = {}

    def __getitem__(self, idx):
        return self.t[idx]


class Kern:
    def __init__(self, nc, es):
        self.nc = nc
        self.es = es
        mk = lambda n: Sem(es.enter_context(nc.semaphore(n)))
        self.pe = Eng("pe", nc.tensor, mk("s_pe"))
        self.act = Eng("act", nc.scalar, mk("s_act"))
        self.dve = Eng("dve", nc.vector, mk("s_dve"))
        self.pool = Eng("pool", nc.gpsimd, mk("s_pool"))
        self.sp = Eng("sp", nc.sync, mk("s_sp"))
        self.engs = [self.pe, self.act, self.dve, self.pool, self.sp]
        self.dfree = {"sp": [mk(f"s_h{i}") for i in range(46)], "pool": [mk(f"s_w{i}") for i in range(40)]}
        self.phase_sems = []
        self.phase_bufs = []
        self.psum = []
        for i in range(8):
            t = es.enter_context(nc.psum_tensor(f"psb{i}", [128, 512], F32))
            self.psum.append(Buf(f"psb{i}", t))
        self.ps_avail = list(self.psum)
        self.ps_i = 0
        self.n_ins = 0

    def sb(self, st, name, shape, dt):
        self.uid = getattr(self, "uid", 0) + 1
        name = f"{name}_{self.uid}"
        t = st.enter_context(self.nc.sbuf_tensor(name, list(shape), dt))
        return Buf(name, t)

    def ps(self):
        b = self.ps_avail.pop(0)
        self.ps_avail.append(b)
        return b

    def ps_hold(self):
        return self.ps_avail.pop(0)

    def ps_release(self, b):
        self.ps_avail.append(b)

    def _dsem(self, b, kind, q):
        key = (kind, q.name)
        if key not in b.sems:
            pn = "pool" if q.name == "pool" else "sp"
            s = self.dfree[pn].pop()
            self.phase_sems.append((pn, s))
            self.phase_bufs.append(b)
            b.sems[key] = s
        return b.sems[key]

    def _wait(self, eng, deps):
        best = {}
        for d in deps:
            if d is None:
                continue
            s, v = d
            if s is eng.sem and eng is self.pe:
                continue
            if v > best.get(s, 0):
                best[s] = v
        for s, v in best.items():
            if eng.seen.get(s, 0) >= v:
                continue
            eng.e.wait_ge(s.h, v)
            eng.seen[s] = v

    def _deps(self, reads, writes):
        deps = []
        for b in reads:
            deps.append(b.lw)
        for b in writes:
            deps.append(b.lw)
            deps.extend(b.readers)
        return deps

    def op(self, eng, fn, reads=(), writes=()):
        self._wait(eng, self._deps(reads, writes))
        ins = fn()
        ins.then_inc(eng.sem.h, 1)
        eng.sem.cnt += 1
        tok = (eng.sem, eng.sem.cnt)
        for b in reads:
            b.readers.append(tok)
        for b in writes:
            b.lw = tok
            b.readers = []
        self.n_ins += 1

    def dma(self, q, pairs, reads=(), writes=()):
        self._wait(q, self._deps(reads, writes))
        if writes:
            s = self._dsem(writes[0], "w", q)
        else:
            s = self._dsem(reads[0], "r", q)
        for (o, i) in pairs:
            q.e.dma_start(out=o, in_=i).then_inc(s.h, 16)
            s.cnt += 16
        tok = (s, s.cnt)
        for b in reads:
            b.readers.append(tok)
        for b in writes:
            b.lw = tok
            b.readers = []
        self.n_ins += len(pairs)

    def barrier(self):
        allsems = [e.sem for e in self.engs] + [s for (_, s) in self.phase_sems]
        for e in self.engs:
            for s in allsems:
                if s is e.sem or s.cnt == 0:
                    continue
                if e.seen.get(s, 0) >= s.cnt:
                    continue
                e.e.wait_ge(s.h, s.cnt)
                e.seen[s] = s.cnt
        for b in self.psum:
            b.lw = None
            b.readers = []
        for (qn, s) in self.phase_sems:
            self.dfree[qn].append(s)
        self.phase_sems = []
        for b in self.phase_bufs:
            b.sems = {}
        self.phase_bufs = []

    def mm(self, out, lhsT, rhs, start, stop, reads, writes):
        self.op(self.pe, lambda: self.nc.tensor.matmul(out, lhsT, rhs, start=start, stop=stop), reads, writes)

    def actf(self, out, in_, func, reads, writes, bias=None, scale=None, eng=None):
        kw = {}
        if bias is not None:
            kw["bias"] = bias
        if scale is not None:
            kw["scale"] = scale
        self.op(self.act, lambda: self.nc.scalar.activation(out=out, in_=in_, func=func, **kw), reads, writes)

    def ts(self, eng, out, in0, s1, s2, op0, op1, reads, writes):
        if op1 is None:
            self.op(eng, lambda: eng.e.tensor_scalar(out, in0, s1, None, op0), reads, writes)
        else:
            self.op(eng, lambda: eng.e.tensor_scalar(out, in0, s1, s2, op0, op1), reads, writes)

    def tt(self, eng, out, in0, in1, op, reads, writes):
        self.op(eng, lambda: eng.e.tensor_tensor(out, in0, in1, op), reads, writes)

    def stt(self, eng, out, in0, scalar, in1, op0, op1, reads, writes):
        self.op(eng, lambda: eng.e.scalar_tensor_tensor(out, in0, scalar, in1, op0, op1), reads, writes)

    def copy(self, eng, out, in_, reads, writes):
        if eng is self.act:
            self.op(eng, lambda: self.nc.scalar.copy(out, in_), reads, writes)
        else:
            self.op(eng, lambda: eng.e.tensor_copy(out, in_), reads, writes)

    def memset(self, eng, ap, val, writes):
        self.op(eng, lambda: eng.e.memset(ap, val), (), writes)


def vec_layout(FC):
    o = {}
    o["attn"] = 0
    o["ffn"] = 16
    o["grp"] = 32
    o["ps"] = 48
    o["qn"] = 52
    o["kvn"] = 56
    o["qhA"] = 58
    o["qhB"] = 59
    o["khA"] = 60
    o["khB"] = 61
    o["hyw"] = 62
    o["hyb"] = 98
    o["hybias"] = 110
    o["fcw"] = 114
    o["fcb"] = 114 + 3 * FC
    o["NV"] = 114 + 4 * FC
    return o


def build(cfg):
    L = cfg["L"]
    NSEQ = cfg["NSEQ"]
    DEPTH = cfg["DEPTH"]
    DFF = cfg["DFF"]
    FC = DFF // 128
    NT = L // TT
    TC = L // 128
    VL = vec_layout(FC)
    NV = VL["NV"]
    dbg = cfg.get("debug", False)

    nc = bass.Bass("TRN2", target_bir_lowering=False)

    def din(name, shape, dt=F32):
        return nc.dram_tensor(name, list(shape), dt, kind="ExternalInput").ap()

    xT = din("xT", [NSEQ, D, L])
    w_in = din("w_in", [DEPTH, 23, 128, 16 * 128])
    w_uq = din("w_uq", [DEPTH, 16, 128, 4 * 128])
    w_ukn = din("w_ukn", [DEPTH, 8, 128, 2 * 128])
    w_ukv = din("w_ukv", [DEPTH, 128, 2 * 1024])
    pool_w = din("pool_w", [DEPTH, 128, 4 * 128])
    w_out = din("w_out", [DEPTH, 16, 128, 16 * 128])
    w_upg = din("w_upg", [DEPTH, FC, 128, 16 * 128])
    w_upv = din("w_upv", [DEPTH, FC, 128, 16 * 128])
    w_dn = din("w_dn", [DEPTH, 16, 128, FC * 128])
    vecs = din("vecs", [DEPTH, 128, NV])
    hf_w1 = din("hf_w1", [DEPTH, NPF, FH])
    hf_w2 = din("hf_w2", [DEPTH, FH, FH])
    hf_w3 = din("hf_w3", [DEPTH, FH, 2 * HY])
    hf_b3 = din("hf_b3", [DEPTH, 128, 2 * HY])
    hvec = din("hvec", [DEPTH, FH, 4])
    zfeatT = din("zfeatT", [NPF, L])
    decay = din("decay", [TC, 128, HY])
    dft_f = din("dft_f", [TC, 2, 128, TC * 128], BF16)
    dft_i = din("dft_i", [NT, 2, 128, TC * TT], BF16)
    ropeT = din("ropeT", [128, L])
    foldm = din("foldm", [128, 128], BF16)
    identm = din("identm", [128, 128], BF16)
    pcorr = din("pcorr", [128, 4 * 16])

    yT = nc.dram_tensor("yT", [NSEQ, D, L], F32, kind="ExternalOutput").ap()
    xs = nc.dram_tensor("xs_scr", [NSEQ, D, L], F32).ap()
    xmid = nc.dram_tensor("xmid_scr", [NSEQ, D, L], F32).ap()
    Ysc = nc.dram_tensor("y_scr", [NSEQ, 16, 128, L], BF16).ap()
    Kf = nc.dram_tensor("kf_scr", [2, TC, 128, HY], F32).ap()
    Usc = nc.dram_tensor("u_scr", [12, 128, L], BF16).ap()
    Psc = nc.dram_tensor("p_scr", [4, 128, L], F32).ap()
    dbg_out = None
    if dbg:
        dbg_out = nc.dram_tensor("dbgY", [NSEQ, 16, 128, L], BF16, kind="ExternalOutput").ap()

    W32 = {"w_in": w_in, "w_uq": w_uq, "w_ukn": w_ukn, "w_ukv": w_ukv, "pool_w": pool_w, "w_out": w_out,
           "w_upg": w_upg, "w_upv": w_upv, "w_dn": w_dn}
    WB = {}
    for nm, ap_ in W32.items():
        shp = list(ap_.shape)[1:]
        WB[nm] = nc.dram_tensor("wb_" + nm, [2] + shp, BF16).ap()

    with ExitStack() as es:
        K = Kern(nc, es)
        pe, act, dve, pool, sp = K.pe, K.act, K.dve, K.pool, K.sp
        cvsA = [Sem(es.enter_context(nc.semaphore(f"s_cva{i}"))) for i in range(2)]
        cvsB = [Sem(es.enter_context(nc.semaphore(f"s_cvb{i}"))) for i in range(2)]
        cv_target = {}
        GROUP_A = ("w_in", "w_uq", "w_ukn", "w_ukv", "pool_w")

        def convert_gen(l):
            par = l % 2
            for grp, sm in ((0, cvsA[par]), (1, cvsB[par])):
                for nm, src_ in W32.items():
                    if (nm in GROUP_A) != (grp == 0):
                        continue
                    s_l = src_[l]
                    d_l = WB[nm][par]
                    if len(s_l.shape) == 2:
                        pool.e.dma_start(out=d_l, in_=s_l).then_inc(sm.h, 16)
                        sm.cnt += 16
                        yield
                    else:
                        n0 = s_l.shape[0]
                        step = 4
                        for i0 in range(0, n0, step):
                            i1 = min(n0, i0 + step)
                            pool.e.dma_start(out=d_l[i0:i1], in_=s_l[i0:i1]).then_inc(sm.h, 16)
                            sm.cnt += 16
                            yield
                cv_target[(l, grp)] = sm.cnt

        def convert(l):
            for _ in convert_gen(l):
                pass

        def wait_converted(l, grp):
            sm = (cvsA if grp == 0 else cvsB)[l % 2]
            for e in (sp, pool, act):
                e.e.wait_ge(sm.h, cv_target[(l, grp)])

        convert(0)

        ones = K.sb(es, "ones", [128, 128], BF16)
        fold = K.sb(es, "fold", [128, 128], BF16)
        ident = K.sb(es, "ident", [128, 128], BF16)
        vec = K.sb(es, "vec", [128, NV], F32)
        corr = K.sb(es, "corr", [128, 64], F32)
        K.memset(dve, ones[:], 1.0, [ones])
        K.dma(sp, [(fold[:], foldm)], writes=[fold])
        K.dma(sp, [(ident[:], identm)], writes=[ident])
        K.dma(sp, [(corr[:], pcorr)], writes=[corr])

        def V(k):
            return vec[:, k:k + 1]

        cst = K.sb(es, "cst", [128, 4], F32)
        K.memset(dve, cst[:, 0:1], EPS, [cst])
        K.memset(dve, cst[:, 1:2], EPS * 192.0, [cst])

        def rstd_from_ss(dst, ss_ap, n, w, reads, writes, extra_scale=None):
            if extra_scale is None:
                K.actf(dst, ss_ap, AF.Ln, list(reads) + [cst], writes, bias=cst[:, 0:1], scale=1.0 / n)
            else:
                assert abs(extra_scale ** 2 - 1.0 / 192.0) < 1e-9
                K.actf(dst, ss_ap, AF.Ln, list(reads) + [cst], writes, bias=cst[:, 1:2], scale=192.0 / n)
            K.actf(dst, dst, AF.Exp, writes, writes, scale=-0.5)

        for l in range(DEPTH):
            Xin = xT if l == 0 else xs
            Xout = yT if l == DEPTH - 1 else xs
            K.dma(sp, [(vec[:], vecs[l])], writes=[vec])
            if l >= 1:
                wait_converted(l, 0)
                wait_converted(l, 1)
            wb_in, wb_uq, wb_ukn, wb_ukv, wb_pool, wb_out, wb_upg, wb_upv, wb_dn = (
                WB[k][l % 2] for k in ("w_in", "w_uq", "w_ukn", "w_ukv", "pool_w", "w_out", "w_upg", "w_upv", "w_dn"))

            with ExitStack() as st:
                zf = K.sb(st, "zf", [NPF, L], F32)
                fw1 = K.sb(st, "fw1", [NPF, FH], F32)
                fw2 = K.sb(st, "fw2", [FH, FH], F32)
                fw3 = K.sb(st, "fw3", [FH, 2 * HY], F32)
                fb3 = K.sb(st, "fb3", [128, 2 * HY], F32)
                hv = K.sb(st, "hv", [FH, 4], F32)
                h1 = K.sb(st, "h1", [FH, L], F32)
                h2 = K.sb(st, "h2", [FH, L], F32)
                hsum = K.sb(st, "hsum", [128, TC, HY], BF16)
                hdif = K.sb(st, "hdif", [128, TC, HY], BF16)
                targ = K.sb(st, "targ", [FH, TT], F32)
                tsq = K.sb(st, "tsq", [FH, TT], F32)
                K.dma(sp, [(zf[:], zfeatT)], writes=[zf])
                K.dma(sp, [(fw1[:], hf_w1[l])], writes=[fw1])
                K.dma(sp, [(fw2[:], hf_w2[l])], writes=[fw2])
                K.dma(sp, [(fw3[:], hf_w3[l])], writes=[fw3])
                K.dma(sp, [(fb3[:], hf_b3[l])], writes=[fb3])
                K.dma(sp, [(hv[:], hvec[l])], writes=[hv])
                for (wsrc, bcol, src, dst) in ((fw1, 0, zf, h1), (fw2, 2, h1, h2)):
                    for n in range(NT):
                        cs = slice(n * TT, (n + 1) * TT)
                        p = K.ps()
                        K.mm(p[0:FH, :], wsrc[:], src[:, cs], True, True, [wsrc, src], [p])
                        K.ts(dve, targ[:], p[0:FH, :], hv[:, bcol:bcol + 1], hv[:, 1:2], ALU.add, ALU.mult,
                             [p, hv], [targ])
                        K.actf(targ[:], targ[:], AF.Sin, [targ], [targ], scale=1.0 / 9.0)
                        for rep in range(2):
                            K.tt(dve, tsq[:], targ[:], targ[:], ALU.mult, [targ], [tsq])
                            K.ts(dve, tsq[:], tsq[:], -4.0, 3.0, ALU.mult, ALU.add, [tsq], [tsq])
                            K.tt(dve, (dst[:, cs] if rep == 1 else targ[:]), tsq[:], targ[:], ALU.mult, [tsq, targ],
                                 [dst if rep == 1 else targ])
                with ExitStack() as st2:
                    dct = [K.sb(st2, f"dct{i}", [128, HY], F32) for i in range(2)]
                    hf = [K.sb(st2, f"hf{i}", [128, HY], F32) for i in range(2)]
                    hb = [K.sb(st2, f"hb{i}", [128, HY], F32) for i in range(2)]
                    for j in range(TC):
                        dc_ = dct[j % 2]
                        hf_ = hf[j % 2]
                        hb_ = hb[j % 2]
                        K.dma(sp, [(dc_[:], decay[j])], writes=[dc_])
                        pf = K.ps()
                        K.mm(pf[:], h2[:, j * 128:(j + 1) * 128], fw3[:, 0:HY], True, True, [h2, fw3], [pf])
                        pb = K.ps()
                        K.mm(pb[:], h2[:, j * 128:(j + 1) * 128], fw3[:, HY:2 * HY], True, True, [h2, fw3], [pb])
                        K.tt(dve, hf_[:], pf[:], fb3[:, 0:HY], ALU.add, [pf, fb3], [hf_])
                        K.tt(dve, hb_[:], pb[:], fb3[:, HY:2 * HY], ALU.add, [pb, fb3], [hb_])
                        K.tt(pool, hf_[:], hf_[:], dc_[:], ALU.mult, [hf_, dc_], [hf_])
                        K.tt(pool, hb_[:], hb_[:], dc_[:], ALU.mult, [hb_, dc_], [hb_])
                        if j == 0:
                            K.memset(pool, hb_[0:1, :], 0.0, [hb_])
                        K.tt(dve, hsum[:, j, :], hf_[:], hb_[:], ALU.add, [hf_, hb_], [hsum])
                        K.tt(pool, hdif[:, j, :], hb_[:], hf_[:], ALU.subtract, [hf_, hb_], [hdif])
                    wc = [K.sb(st2, f"fwc{i}", [128, TC * 128], BF16) for i in range(2)]
                    wsn = [K.sb(st2, f"fws{i}", [128, TC * 128], BF16) for i in range(2)]
                    ko = [K.sb(st2, f"ko{i}", [128, 2, HY], F32) for i in range(2)]
                    for fc in range(TC):
                        c_ = wc[fc % 2]
                        s_ = wsn[fc % 2]
                        o_ = ko[fc % 2]
                        K.dma(sp, [(c_[:], dft_f[fc, 0])], writes=[c_])
                        K.dma(sp, [(s_[:], dft_f[fc, 1])], writes=[s_])
                        pr = K.ps()
                        for j in range(TC):
                            K.mm(pr[:], c_[:, j * 128:(j + 1) * 128], hsum[:, j, :], j == 0, j == TC - 1,
                                 [c_, hsum], [pr])
                        pi_ = K.ps()
                        for j in range(TC):
                            K.mm(pi_[:], s_[:, j * 128:(j + 1) * 128], hdif[:, j, :], j == 0, j == TC - 1,
                                 [s_, hdif], [pi_])
                        K.copy(act, o_[:, 0, :], pr[:], [pr], [o_])
                        K.copy(act, o_[:, 1, :], pi_[:], [pi_], [o_])
                        K.dma(sp, [(Kf[0, fc], o_[:, 0, :]), (Kf[1, fc], o_[:, 1, :])], reads=[o_])
                    K.barrier()
            if l == 0:
                wait_converted(l, 0)
            for s in range(NSEQ):
                with ExitStack() as sq_st:
                    cqn = K.sb(sq_st, "cqn", [128, 4, L], BF16)
                    ckvn = K.sb(sq_st, "ckvn", [128, 2, L], BF16)
                    sspe = K.sb(sq_st, "sspe", [128, L], F32)
                    K2u = K.sb(sq_st, "K2u", [128, L], BF16)
                    rope = K.sb(sq_st, "rope", [128, L], F32)
                    K.dma(sp, [(rope[:], ropeT)], writes=[rope])
                    with ExitStack() as st:
                        xt = K.sb(st, "xt", [128, DC, TT], F32)
                        hhs = [K.sb(st, f"hh{i}", [128, DC, TT], BF16) for i in range(2)]
                        sqs = [K.sb(st, f"sq{i}", [128, TT], BF16) for i in range(4)]
                        sqp = [K.sb(st, f"sqp{i}", [128, TT], BF16) for i in range(4)]
                        rstd = K.sb(st, "rstd", [128, TT], F32)
                        rstd2 = K.sb(st, "rstd2", [128, TT], F32)
                        NWI = 6
                        wts = [K.sb(st, f"wi{i}", [128, 16 * 128], BF16) for i in range(NWI)]
                        craw = K.sb(st, "craw", [128, 4, TT], F32)
                        kB = K.sb(st, "kB", [128, TT], F32)
                        kR = K.sb(st, "kR", [128, TT], BF16)
                        pst = [K.sb(st, f"pst{i}", [128, TT], F32) for i in range(4)]
                        ust = [K.sb(st, f"ust{i}", [128, TT], BF16) for i in range(6)]
                        sqi = 0

                        def prepA(n):
                            hh_ = hhs[n % 2]
                            cs_ = slice(n * TT, (n + 1) * TT)
                            K.dma(sp, [(xt[:], Xin[s].rearrange("(c p) t -> p c t", p=128)[:, :, cs_])], writes=[xt])
                            yield
                            yield
                            ssb = K.ps_hold()
                            pend = []
                            for c0 in range(0, DC, 2):
                                for (c, q_) in pend:
                                    K.mm(ssb[:], ones[:], q_[:], c == 0, c == DC - 1, [ones, q_], [ssb])
                                pend = []
                                for c in (c0, c0 + 1):
                                    q_ = sqp[c % 4]
                                    K.actf(q_[:], xt[:, c, :], AF.Square, [xt], [q_])
                                    pend.append((c, q_))
                                yield
                            for (c, q_) in pend:
                                K.mm(ssb[:], ones[:], q_[:], c == 0, c == DC - 1, [ones, q_], [ssb])
                            yield
                            rstd_from_ss(rstd[:], ssb[:], D, None, [ssb], [rstd])
                            K.ps_release(ssb)
                            yield
                            for c0 in range(0, DC, 3):
                                for c in range(c0, min(DC, c0 + 3)):
                                    K.stt(dve, hh_[:, c, :], xt[:, c, :], V(VL["attn"] + c), rstd[:], ALU.mult, ALU.mult,
                                          [xt, vec, rstd], [hh_])
                                yield

                        for _ in prepA(0):
                            pass
                        for n in range(NT):
                            cs = slice(n * TT, (n + 1) * TT)
                            hh = hhs[n % 2]
                            bg = prepA(n + 1) if n + 1 < NT else None
                            ssq = None
                            deferred = []

                            def run_deferred():
                                while deferred:
                                    deferred.pop(0)()

                            for j in range(23):
                                w_ = wts[(n * 23 + j) % NWI]
                                K.dma(pool, [(w_[:], wb_in[j])], writes=[w_])
                                p = K.ps()
                                for c in range(DC):
                                    K.mm(p[:], w_[:, c * 128:(c + 1) * 128], hh[:, c, :], c == 0, c == DC - 1,
                                         [w_, hh], [p])
                                run_deferred()
                                if j < 4:
                                    t_ = pst[j % 4]
                                    K.copy(act, t_[:], p[:], [p], [t_])
                                    K.dma(sp, [(Psc[j][:, cs], t_[:])], reads=[t_])
                                elif j < 10:
                                    grp0, ng, dstb, gcol = (4, 4, cqn, VL["qn"]) if j < 8 else (8, 2, ckvn, VL["kvn"])
                                    c_ = j - grp0
                                    K.copy(act, craw[:, c_, :], p[:], [p], [craw])
                                    q_ = sqs[sqi % 4]
                                    sqi += 1
                                    K.actf(q_[:], p[:], AF.Square, [p], [q_])

                                    def dq(c_=c_, ng=ng, dstb=dstb, gcol=gcol, q_=q_):
                                        nonlocal ssq
                                        if c_ == 0:
                                            ssq = K.ps_hold()
                                        K.mm(ssq[:], ones[:], q_[:], c_ == 0, c_ == ng - 1, [ones, q_], [ssq])
                                        if c_ == ng - 1:
                                            rstd_from_ss(rstd2[:], ssq[:], ng * 128, None, [ssq], [rstd2])
                                            K.ps_release(ssq)
                                            for cc in range(ng):
                                                K.stt(dve, dstb[:, cc, cs], craw[:, cc, :], V(gcol + cc), rstd2[:],
                                                      ALU.mult, ALU.mult, [craw, vec, rstd2], [dstb])

                                    deferred.append(dq)
                                elif j == 10:
                                    K.copy(act, kB[:], p[:], [p], [kB])
                                    q_ = sqs[sqi % 4]
                                    sqi += 1
                                    K.actf(q_[0:64, :], p[0:64, :], AF.Square, [p], [q_])
                                    K.stt(dve, kR[:], kB[:], V(VL["khB"]), rope[:, cs], ALU.mult, ALU.mult,
                                          [kB, vec, rope], [kR])

                                    def dk(q_=q_):
                                        p2 = K.ps()
                                        K.mm(p2[:], ones[0:64, :], q_[0:64, :], True, True, [ones, q_], [p2])
                                        K.copy(act, sspe[:, cs], p2[:], [p2], [sspe])
                                        p3 = K.ps()
                                        K.mm(p3[:], fold[:], kR[:], True, True, [fold, kR], [p3])
                                        K.copy(act, K2u[:, cs], p3[:], [p3], [K2u])

                                    deferred.append(dk)
                                else:
                                    t_ = ust[j % 6]
                                    K.copy(act, t_[:], p[:], [p], [t_])
                                    K.dma(sp, [(Usc[j - 11][:, cs], t_[:])], reads=[t_])
                                if bg is not None:
                                    next(bg, None)
                            run_deferred()
                            if bg is not None:
                                for _ in bg:
                                    pass
                        K.barrier()
                    with ExitStack() as st:
                        Vt = K.sb(st, "Vt", [128, TC, 1024], BF16)
                        wv = K.sb(st, "wv", [128, 2 * 1024], BF16)
                        K.dma(pool, [(wv[:], wb_ukv)], writes=[wv])
                        for kt in range(TC):
                            for hf_ in range(2):
                                p = K.ps()
                                for c in range(2):
                                    K.mm(p[:], ckvn[:, c, kt * 128:(kt + 1) * 128],
                                         wv[:, c * 1024 + hf_ * 512:c * 1024 + (hf_ + 1) * 512], c == 0, c == 1,
                                         [ckvn, wv], [p])
                                K.copy(act if hf_ == 0 else dve, Vt[:, kt, hf_ * 512:(hf_ + 1) * 512], p[:], [p], [Vt])
                        wqA = [K.sb(st, f"wqA{i}", [128, 4 * 128], BF16) for i in range(2)]
                        wqB = [K.sb(st, f"wqB{i}", [128, 4 * 128], BF16) for i in range(2)]
                        wkn = [K.sb(st, f"wkn{i}", [128, 2 * 128], BF16) for i in range(2)]
                        Kn = [K.sb(st, f"Kn{i}", [128, L], BF16) for i in range(2)]
                        Kr = [K.sb(st, f"Kr{i}", [128, L], BF16) for i in range(2)]
                        Qn = [K.sb(st, f"Qn{i}", [128, TT], BF16) for i in range(2)]
                        Qr = [K.sb(st, f"Qr{i}", [128, TT], BF16) for i in range(2)]
                        qrawA = K.sb(st, "qrawA", [128, TT], F32)
                        qrawB = K.sb(st, "qrawB", [128, TT], F32)
                        qtmp = K.sb(st, "qtmp", [128, TT], F32)
                        kraw = [K.sb(st, f"kraw{i}", [128, TT], F32) for i in range(2)]
                        sqk = [K.sb(st, f"sqk{i}", [128, TT], BF16) for i in range(2)]
                        sqq = [K.sb(st, f"sqq{i}", [128, TT], BF16) for i in range(2)]
                        rk = K.sb(st, "rk", [128, TT], F32)
                        rq = K.sb(st, "rq", [128, TT], F32)
                        rl = K.sb(st, "rl", [128, TT], F32)
                        Pb = [K.sb(st, f"Pb{i}", [128, TT], BF16) for i in range(4)]
                        yst = [K.sb(st, f"ybst{i}", [128, L], BF16) for i in range(2)]

                        def load_head_w(h):
                            K.dma(pool, [(wqA[h % 2][:], wb_uq[2 * h])], writes=[wqA[h % 2]])
                            K.dma(pool, [(wqB[h % 2][:], wb_uq[2 * h + 1])], writes=[wqB[h % 2]])
                            K.dma(pool, [(wkn[h % 2][:], wb_ukn[h])], writes=[wkn[h % 2]])

                        def Kprep(h):
                            wK, Kn_, Kr_ = wkn[h % 2], Kn[h % 2], Kr[h % 2]
                            for n in range(NT):
                                cs_ = slice(n * TT, (n + 1) * TT)
                                kr_, q_ = kraw[n % 2], sqk[n % 2]
                                p = K.ps()
                                for c in range(2):
                                    K.mm(p[:], wK[:, c * 128:(c + 1) * 128], ckvn[:, c, cs_], c == 0, c == 1, [wK, ckvn], [p])
                                K.copy(dve, kr_[:], p[:], [p], [kr_])
                                K.tt(dve, q_[:], kr_[:], kr_[:], ALU.mult, [kr_], [q_])
                                yield
                                p2 = K.ps()
                                K.mm(p2[:], ones[:], q_[:], True, True, [ones, q_], [p2])
                                K.tt(dve, rk[:], p2[:], sspe[:, cs_], ALU.add, [p2, sspe], [rk])
                                yield
                                K.actf(rk[:], rk[:], AF.Ln, [rk, cst], [rk], bias=cst[:, 0:1], scale=1.0 / 192.0)
                                yield
                                K.actf(rk[:], rk[:], AF.Exp, [rk], [rk], scale=-0.5)
                                yield
                                K.stt(dve, Kn_[:, cs_], kr_[:], V(VL["khA"]), rk[:], ALU.mult, ALU.mult, [kr_, vec, rk], [Kn_])
                                K.tt(pool, Kr_[:, cs_], K2u[:, cs_], rk[:], ALU.mult, [K2u, rk], [Kr_])
                                yield

                        def Qprep(h, n):
                            k_ = h * NT + n
                            wA, wB, Qn_, Qr_ = wqA[h % 2], wqB[h % 2], Qn[k_ % 2], Qr[k_ % 2]
                            cs_ = slice(n * TT, (n + 1) * TT)
                            qa, qb = sqq[0], sqq[1]
                            pA = K.ps()
                            for c in range(4):
                                K.mm(pA[:], wA[:, c * 128:(c + 1) * 128], cqn[:, c, cs_], c == 0, c == 3, [wA, cqn], [pA])
                            K.copy(dve, qrawA[:], pA[:], [pA], [qrawA])
                            K.tt(dve, qa[:], qrawA[:], qrawA[:], ALU.mult, [qrawA], [qa])
                            yield
                            pB = K.ps()
                            for c in range(4):
                                K.mm(pB[:], wB[:, c * 128:(c + 1) * 128], cqn[:, c, cs_], c == 0, c == 3, [wB, cqn], [pB])
                            K.copy(dve, qrawB[:], pB[:], [pB], [qrawB])
                            K.tt(dve, qb[0:64, :], qrawB[0:64, :], qrawB[0:64, :], ALU.mult, [qrawB], [qb])
                            yield
                            p2 = K.ps()
                            K.mm(p2[:], ones[:], qa[:], True, False, [ones, qa], [p2])
                            K.mm(p2[:], ones[0:64, :], qb[0:64, :], False, True, [ones, qb], [p2])
                            K.copy(dve, rq[:], p2[:], [p2], [rq])
                            yield
                            K.actf(rq[:], rq[:], AF.Ln, [rq, cst], [rq], bias=cst[:, 1:2], scale=1.0)
                            yield
                            K.actf(rq[:], rq[:], AF.Exp, [rq], [rq], scale=-0.5)
                            yield
                            K.stt(dve, Qn_[:], qrawA[:], V(VL["qhA"]), rq[:], ALU.mult, ALU.mult, [qrawA, vec, rq], [Qn_])
                            K.stt(dve, qtmp[:], qrawB[:], V(VL["qhB"]), rq[:], ALU.mult, ALU.mult, [qrawB, vec, rq], [qtmp])
                            K.tt(pool, Qr_[:], qtmp[:], rope[:, cs_], ALU.mult, [qtmp, rope], [Qr_])
                            yield

                        pw = K.sb(st, "pw", [128, 4 * 128], BF16)
                        Up = K.sb(st, "Up", [128, L + 16], F32)
                        ta = K.sb(st, "ta", [128, L + 16], F32)
                        tb = K.sb(st, "tb", [128, L + 16], F32)
                        pooled = K.sb(st, "pooled", [128, L], BF16)
                        ypst = [K.sb(st, f"ypst{i}", [128, L], BF16) for i in range(2)]

                        def poolgen():
                            K.dma(pool, [(pw[:], wb_pool)], writes=[pw])
                            K.memset(dve, Up[:, 0:8], 0.0, [Up])
                            K.memset(dve, Up[:, L + 8:L + 16], 0.0, [Up])
                            for g in range(4):
                                w = POOL_WINDOWS[g]
                                half = w // 2
                                K.dma(sp, [(Up[:, 8:8 + L], Psc[g])], writes=[Up])
                                yield
                                if w == 2:
                                    K.tt(dve, ta[:, 0:L], Up[:, 7:7 + L], Up[:, 8:8 + L], ALU.add, [Up], [ta])
                                    sres = ta
                                    yield
                                else:
                                    K.tt(dve, ta[:, 0:L + 15], Up[:, 0:L + 15], Up[:, 1:L + 16], ALU.add, [Up], [ta])
                                    yield
                                    cur, other, span, valid = ta, tb, 2, L + 15
                                    while span * 2 < w:
                                        K.tt(dve, other[:, 0:valid - span], cur[:, 0:valid - span], cur[:, span:valid], ALU.add,
                                             [cur], [other])
                                        yield
                                        valid -= span
                                        cur, other = other, cur
                                        span *= 2
                                    K.tt(dve, other[:, 0:L], cur[:, 8 - half:8 - half + L], cur[:, 8:8 + L], ALU.add, [cur], [other])
                                    sres = other
                                    yield
                                K.tt(dve, sres[:, 0:8], sres[:, 0:8], corr[:, g * 16:g * 16 + 8], ALU.mult, [sres, corr], [sres])
                                K.tt(dve, sres[:, L - 8:L], sres[:, L - 8:L], corr[:, g * 16 + 8:g * 16 + 16], ALU.mult,
                                     [sres, corr], [sres])
                                yield
                                K.stt(dve, pooled[:], sres[:, 0:L], 1.0 / w, Up[:, 8:8 + L], ALU.mult, ALU.subtract,
                                      [sres, Up], [pooled])
                                yield
                                yp_ = ypst[g % 2]
                                for n in range(NT):
                                    cs_ = slice(n * TT, (n + 1) * TT)
                                    p = K.ps()
                                    K.mm(p[:], pw[:, g * 128:(g + 1) * 128], pooled[:, cs_], True, True, [pw, pooled], [p])
                                    K.ts(dve, yp_[:, cs_], p[:], V(VL["ps"] + g), None, ALU.mult, None, [p, vec], [yp_])
                                    yield
                                K.dma(sp, [(Ysc[s, g], yp_[:])], reads=[yp_])
                                yield

                        pgen = poolgen()
                        load_head_w(0)
                        for _ in Kprep(0):
                            pass
                        for _ in Qprep(0, 0):
                            pass
                        pbi = 0
                        kbg = None
                        sring = [K.ps_hold() for _ in range(3)]
                        for h in range(H):
                            y_ = yst[h % 2]
                            Kn_, Kr_ = Kn[h % 2], Kr[h % 2]
                            if h + 1 < H:
                                load_head_w(h + 1)
                                kbg = Kprep(h + 1)
                            else:
                                kbg = None
                            for n in range(NT):
                                cs = slice(n * TT, (n + 1) * TT)
                                k_ = h * NT + n
                                Qn_, Qr_ = Qn[k_ % 2], Qr[k_ % 2]
                                if n + 1 < NT:
                                    qbg = Qprep(h, n + 1)
                                elif h + 1 < H:
                                    qbg = Qprep(h + 1, 0)
                                else:
                                    qbg = None
                                o_ps = K.ps_hold()
                                l_ps = K.ps_hold()

                                def smm(kt):
                                    sp_ = sring[kt % 3]
                                    ks = slice(kt * 128, (kt + 1) * 128)
                                    K.mm(sp_[:], Kn_[:, ks], Qn_[:], True, False, [Kn_, Qn_], [sp_])
                                    K.mm(sp_[:], Kr_[:, ks], Qr_[:], False, True, [Kr_, Qr_], [sp_])
                                    return sp_

                                S = {0: smm(0)}
                                if TC > 1:
                                    S[1] = smm(1)
                                for kt in range(TC):
                                    if kt + 2 < TC:
                                        S[kt + 2] = smm(kt + 2)
                                    s_cur = S.pop(kt)
                                    P = Pb[pbi % 4]
                                    pbi += 1
                                    K.actf(P[:], s_cur[:], AF.Exp, [s_cur], [P])
                                    K.mm(o_ps[:], Vt[:, kt, h * 128:(h + 1) * 128], P[:], kt == 0, kt == TC - 1, [Vt, P], [o_ps])
                                    K.mm(l_ps[:], ones[:], P[:], kt == 0, kt == TC - 1, [ones, P], [l_ps])
                                    if qbg is not None:
                                        next(qbg, None)
                                    if kbg is not None and kt % 2 == 1:
                                        next(kbg, None)
                                    if kt % 2 == 0:
                                        next(pgen, None)
                                K.op(dve, lambda: nc.vector.reciprocal(rl[:], l_ps[:]), [l_ps], [rl])
                                K.tt(dve, y_[:, cs], o_ps[:], rl[:], ALU.mult, [o_ps, rl], [y_])
                                K.ps_release(o_ps)
                                K.ps_release(l_ps)
                                if qbg is not None:
                                    for _ in qbg:
                                        pass
                            if kbg is not None:
                                for _ in kbg:
                                    pass
                            K.dma(sp, [(Ysc[s, 4 + h], y_[:])], reads=[y_])
                        for _ in pgen:
                            pass
                        for b_ in sring:
                            K.ps_release(b_)
                        K.barrier()
                with ExitStack() as st:
                    X0 = K.sb(st, "X0", [128, 4, L], BF16)
                    Zt = K.sb(st, "Zt", [128, 4, L], BF16)
                    Yr = K.sb(st, "Yr", [128, TC, HY], BF16)
                    Yi = K.sb(st, "Yi", [128, TC, HY], BF16)
                    ci0 = K.sb(st, "ci0", [128, TC * TT], BF16)
                    si0 = K.sb(st, "si0", [128, TC * TT], BF16)
                    K.dma(sp, [(ci0[:], dft_i[0, 0])], writes=[ci0])
                    K.dma(sp, [(si0[:], dft_i[0, 1])], writes=[si0])
                    with ExitStack() as st2:
                        ur = [K.sb(st2, f"ur{i}", [128, L + 2], BF16) for i in range(3)]
                        ctmp = [K.sb(st2, f"ctmp{i}", [128, L], F32) for i in range(4)]
                        for u_ in ur:
                            K.memset(pool, u_[:, 0:1], 0.0, [u_])
                            K.memset(pool, u_[:, L + 1:L + 2], 0.0, [u_])
                        ui = 0

                        def hconv(c, dst_ap, dst_bufs):
                            nonlocal ui
                            u_ = ur[ui % 3]
                            ui += 1
                            K.dma(sp, [(u_[:, 1:L + 1], Usc[c])], writes=[u_])
                            t_ = ctmp[ui % 3]
                            K.actf(t_[:], u_[:, 1:L + 1], AF.Identity, [u_, vec], [t_],
                                   bias=V(VL["hyb"] + c), scale=V(VL["hyw"] + 12 + c))
                            K.stt(dve, t_[:], u_[:, 0:L], V(VL["hyw"] + c), t_[:], ALU.mult, ALU.add,
                                  [u_, vec, t_], [t_])
                            K.stt(dve, dst_ap, u_[:, 2:L + 2], V(VL["hyw"] + 24 + c), t_[:], ALU.mult, ALU.add,
                                  [u_, vec, t_], dst_bufs)

                        for c in range(4):
                            hconv(c, X0[:, c, :], [X0])
                            hconv(4 + c, ctmp[3][:], [ctmp[3]])
                            hconv(8 + c, Zt[:, c, :], [Zt])
                            K.tt(pool, Zt[:, c, :], Zt[:, c, :], ctmp[3][:], ALU.mult, [Zt, ctmp[3]], [Zt])
                        K.barrier()
                    with ExitStack() as st2:
                        Ztm = K.sb(st2, "Ztm", [128, TC, HY], BF16)
                        for j in range(TC):
                            p = K.ps()
                            for c in range(4):
                                K.mm(p[:, c * 128:(c + 1) * 128], Zt[:, c, j * 128:(j + 1) * 128], ident[:], True, True,
                                     [Zt, ident], [p])
                            K.copy(act if j % 2 == 0 else dve, Ztm[:, j, :], p[:], [p], [Ztm])
                        wc = [K.sb(st2, f"hwc{i}", [128, TC * 128], BF16) for i in range(2)]
                        wsn = [K.sb(st2, f"hws{i}", [128, TC * 128], BF16) for i in range(2)]
                        kre = [K.sb(st2, f"kre{i}", [128, HY], F32) for i in range(2)]
                        kim = [K.sb(st2, f"kim{i}", [128, HY], F32) for i in range(2)]
                        As = [K.sb(st2, f"As{i}", [128, HY], F32) for i in range(2)]
                        Bs = [K.sb(st2, f"Bs{i}", [128, HY], F32) for i in range(2)]
                        t1 = K.sb(st2, "t1", [128, HY], F32)
                        t2 = K.sb(st2, "t2", [128, HY], F32)
                        t3 = K.sb(st2, "t3", [128, HY], F32)
                        t4 = K.sb(st2, "t4", [128, HY], F32)
                        for fc in range(TC):
                            i2 = fc % 2
                            K.dma(sp, [(wc[i2][:], dft_f[fc, 0])], writes=[wc[i2]])
                            K.dma(sp, [(wsn[i2][:], dft_f[fc, 1])], writes=[wsn[i2]])
                            K.dma(sp, [(kre[i2][:], Kf[0, fc])], writes=[kre[i2]])
                            K.dma(sp, [(kim[i2][:], Kf[1, fc])], writes=[kim[i2]])
                            pa = K.ps()
                            for j in range(TC):
                                K.mm(pa[:], wc[i2][:, j * 128:(j + 1) * 128], Ztm[:, j, :], j == 0, j == TC - 1,
                                     [wc[i2], Ztm], [pa])
                            pb = K.ps()
                            for j in range(TC):
                                K.mm(pb[:], wsn[i2][:, j * 128:(j + 1) * 128], Ztm[:, j, :], j == 0, j == TC - 1,
                                     [wsn[i2], Ztm], [pb])
                            A_, B_ = As[i2], Bs[i2]
                            K.copy(act, A_[:], pa[:], [pa], [A_])
                            K.copy(act, B_[:], pb[:], [pb], [B_])
                            K.tt(dve, t1[:], kre[i2][:], A_[:], ALU.mult, [kre[i2], A_], [t1])
                            K.tt(pool, t2[:], kim[i2][:], B_[:], ALU.mult, [kim[i2], B_], [t2])
                            K.tt(dve, Yr[:, fc, :], t1[:], t2[:], ALU.add, [t1, t2], [Yr])
                            K.tt(pool, t3[:], kim[i2][:], A_[:], ALU.mult, [kim[i2], A_], [t3])
                            K.tt(dve, t4[:], kre[i2][:], B_[:], ALU.mult, [kre[i2], B_], [t4])
                            K.tt(pool, Yi[:, fc, :], t3[:], t4[:], ALU.subtract, [t3, t4], [Yi])
                        K.barrier()
                    with ExitStack() as st2:
                        ci = [ci0, K.sb(st2, "ci1", [128, TC * TT], BF16)]
                        si = [si0, K.sb(st2, "si1", [128, TC * TT], BF16)]
                        ysth = K.sb(st2, "ysth", [128, 4, L], BF16)
                        ytmp = [K.sb(st2, f"ytmp{i}", [128, TT], F32) for i in range(2)]
                        for n in range(NT):
                            cs = slice(n * TT, (n + 1) * TT)
                            ci_, si_ = ci[n % 2], si[n % 2]
                            if n + 1 < NT:
                                K.dma(sp, [(ci[(n + 1) % 2][:], dft_i[n + 1, 0])], writes=[ci[(n + 1) % 2]])
                                K.dma(sp, [(si[(n + 1) % 2][:], dft_i[n + 1, 1])], writes=[si[(n + 1) % 2]])
                            for c in range(4):
                                p = K.ps()
                                for fc in range(TC):
                                    K.mm(p[:], Yr[:, fc, c * 128:(c + 1) * 128], ci_[:, fc * TT:(fc + 1) * TT], fc == 0, False,
                                         [Yr, ci_], [p])
                                    K.mm(p[:], Yi[:, fc, c * 128:(c + 1) * 128], si_[:, fc * TT:(fc + 1) * TT], False,
                                         fc == TC - 1, [Yi, si_], [p])
                                yt_ = ytmp[c % 2]
                                K.stt(dve, yt_[:], Zt[:, c, cs], V(VL["hybias"] + c), p[:], ALU.mult, ALU.add,
                                      [Zt, vec, p], [yt_])
                                K.tt(pool, ysth[:, c, cs], yt_[:], X0[:, c, cs], ALU.mult, [yt_, X0], [ysth])
                        for c in range(4):
                            K.dma(sp, [(Ysc[s, 12 + c], ysth[:, c, :])], reads=[ysth])
                        K.barrier()
            if dbg and l == 0:
                with ExitStack() as st:
                    dtile = K.sb(st, "dtile", [128, L], BF16)
                    for s in range(NSEQ):
                        for c in range(16):
                            K.dma(sp, [(dtile[:], Ysc[s, c])], writes=[dtile])
                            K.dma(sp, [(dbg_out[s, c], dtile[:])], reads=[dtile])
                    K.barrier()
            tiles = [(s, n) for s in range(NSEQ) for n in range(NT)]
            if l == 0:
                wait_converted(l, 1)
            cgen = convert_gen(l + 1) if l + 1 < DEPTH else None
            with ExitStack() as st:
                Yts = [K.sb(st, f"Ytc{i}", [128, DC, TT], BF16) for i in range(2)]
                Ms = [K.sb(st, f"Mc{i}", [128, DC, TT], BF16) for i in range(2)]
                sqs = [K.sb(st, f"sqc{i}", [128, TT], BF16) for i in range(6)]
                rg = [K.sb(st, f"rg{i}", [128, TT], F32) for i in range(3)]
                wos = [K.sb(st, f"wo{j}", [128, 16 * 128], BF16) for j in range(16)]
                NXC = 8
                xrc = [K.sb(st, f"xrc{i}", [128, TT], F32) for i in range(NXC)]
                xoc = [K.sb(st, f"xoc{i}", [128, TT], F32) for i in range(NXC)]
                for j in range(16):
                    K.dma(pool, [(wos[j][:], wb_out[j])], writes=[wos[j]])

                def prepC(i):
                    s_, n_ = tiles[i]
                    cs_ = slice(n_ * TT, (n_ + 1) * TT)
                    Yt_, M_ = Yts[i % 2], Ms[i % 2]
                    K.dma(sp, [(Yt_[:], Ysc[s_].rearrange("c p t -> p c t")[:, :, cs_])], writes=[Yt_])
                    yield
                    yield
                    grp = ((0, 4), (4, 12), (12, 16))
                    gof = lambda c: 0 if c < 4 else (1 if c < 12 else 2)
                    ssb = [K.ps_hold() for _ in range(3)]
                    fifo = []

                    def drain(stage, lag):
                        while fifo and fifo[0][0] <= stage - lag:
                            _, c, q_ = fifo.pop(0)
                            g = gof(c)
                            K.mm(ssb[g][:], ones[:], q_[:], c == grp[g][0], c == grp[g][1] - 1, [ones, q_], [ssb[g]])

                    stage = 0
                    for c0 in range(0, DC, 2):
                        drain(stage, 2)
                        for c in (c0, c0 + 1):
                            q_ = sqs[c % 6]
                            K.actf(q_[:], Yt_[:, c, :], AF.Square, [Yt_], [q_])
                            fifo.append((stage, c, q_))
                        stage += 1
                        yield
                    for _ in range(2):
                        drain(stage, 2)
                        stage += 1
                        yield
                    for gi, (c0, c1) in enumerate(grp):
                        rstd_from_ss(rg[gi][:], ssb[gi][:], (c1 - c0) * 128, None, [ssb[gi]], [rg[gi]])
                        K.ps_release(ssb[gi])
                    yield
                    for c0 in range(0, DC, 3):
                        for c in range(c0, min(DC, c0 + 3)):
                            gi = gof(c)
                            K.stt(dve, M_[:, c, :], Yt_[:, c, :], V(VL["grp"] + c), rg[gi][:], ALU.mult, ALU.mult,
                                  [Yt_, vec, rg[gi]], [M_])
                        yield

                for _ in prepC(0):
                    pass
                xi = 0
                for i, (s, n) in enumerate(tiles):
                    cs = slice(n * TT, (n + 1) * TT)
                    bg = prepC(i + 1) if i + 1 < len(tiles) else None
                    M = Ms[i % 2]
                    xsrc = Xin[s]

                    def loadXc(j, k):
                        K.dma(sp, [(xrc[k % NXC][:], xsrc[j * 128:(j + 1) * 128, cs])], writes=[xrc[k % NXC]])

                    for j0 in range(4):
                        loadXc(j0, xi + j0)
                    for j in range(16):
                        if j + 4 < 16:
                            loadXc(j + 4, xi + j + 4)
                        w_ = wos[j]
                        p = K.ps()
                        for c in range(DC):
                            K.mm(p[:], w_[:, c * 128:(c + 1) * 128], M[:, c, :], c == 0, c == DC - 1, [w_, M], [p])
                        xr_, xo_ = xrc[(xi + j) % NXC], xoc[(xi + j) % NXC]
                        K.tt(dve, xo_[:], p[:], xr_[:], ALU.add, [p, xr_], [xo_])
                        K.dma(sp, [(xmid[s][j * 128:(j + 1) * 128, cs], xo_[:])], reads=[xo_])
                        if bg is not None:
                            for _ in range(2):
                                next(bg, None)
                    xi += 16
                    if bg is not None:
                        for _ in bg:
                            pass
                K.barrier()
            with ExitStack() as st:
                W = TT + 2
                xt = K.sb(st, "xtb", [128, DC, W], F32)
                hhs = [K.sb(st, f"hhb{i}", [128, DC, W], BF16) for i in range(2)]
                actb = K.sb(st, "actb", [128, FC, TT], BF16)
                sqs = [K.sb(st, f"sqb{i}", [128, TT], BF16) for i in range(3)]
                sqh = K.sb(st, "sqh", [128, DC, 2], BF16)
                rstd = K.sb(st, "rstdb", [128, W], F32)
                wg = [K.sb(st, f"wg{i}", [128, 16 * 128], BF16) for i in range(3)]
                wv_ = [K.sb(st, f"wvv{i}", [128, 16 * 128], BF16) for i in range(3)]
                wd = [K.sb(st, f"wd{i}", [128, FC * 128], BF16) for i in range(2)]
                G = [K.sb(st, f"G{i}", [128, W], F32) for i in range(2)]
                cen = [K.sb(st, f"cen{i}", [128, TT], F32) for i in range(2)]
                sg = [K.sb(st, f"sg{i}", [128, TT], F32) for i in range(2)]
                xr = [K.sb(st, f"xr{i}", [128, TT], F32) for i in range(3)]
                xo = [K.sb(st, f"xo{i}", [128, TT], F32) for i in range(3)]

                def prepB(i):
                    s, n = tiles[i]
                    hh = hhs[i % 2]
                    t0 = n * TT
                    lo = max(t0 - 1, 0)
                    hi = min(t0 + TT + 1, L)
                    if n == 0:
                        K.memset(pool, xt[:, :, 0:1], 0.0, [xt])
                    if n == NT - 1:
                        K.memset(pool, xt[:, :, W - 1:W], 0.0, [xt])
                    K.dma(sp, [(xt[:, :, lo - (t0 - 1):hi - (t0 - 1)],
                                xmid[s].rearrange("(c p) t -> p c t", p=128)[:, :, lo:hi])], writes=[xt])
                    yield
                    yield
                    ssA = K.ps_hold()
                    ssB = K.ps_hold()
                    K.actf(sqh[:], xt[:, :, TT:W], AF.Square, [xt], [sqh])
                    for c in range(DC):
                        q_ = sqs[c % 3]
                        K.actf(q_[:], xt[:, c, 0:TT], AF.Square, [xt], [q_])
                        yield
                        K.mm(ssA[:], ones[:], q_[:], c == 0, c == DC - 1, [ones, q_], [ssA])
                        K.mm(ssB[:, 0:2], ones[:], sqh[:, c, :], c == 0, c == DC - 1, [ones, sqh], [ssB])
                    yield
                    rstd_from_ss(rstd[:, 0:TT], ssA[:], D, None, [ssA], [rstd])
                    rstd_from_ss(rstd[:, TT:W], ssB[:, 0:2], D, None, [ssB], [rstd])
                    K.ps_release(ssA)
                    K.ps_release(ssB)
                    yield
                    for c in range(DC):
                        K.stt(dve, hh[:, c, :], xt[:, c, :], V(VL["ffn"] + c), rstd[:], ALU.mult, ALU.mult,
                              [xt, vec, rstd], [hh])
                        yield

                for _ in prepB(0):
                    pass
                for i, (s, n) in enumerate(tiles):
                    t0 = n * TT
                    cs = slice(t0, t0 + TT)
                    hh = hhs[i % 2]
                    bg = prepB(i + 1) if i + 1 < len(tiles) else None

                    def loadB(j):
                        K.dma(pool, [(wg[j % 3][:], wb_upg[j])], writes=[wg[j % 3]])
                        K.dma(pool, [(wv_[j % 3][:], wb_upv[j])], writes=[wv_[j % 3]])

                    loadB(0)
                    if FC > 1:
                        loadB(1)
                    for j in range(FC):
                        if j + 2 < FC:
                            loadB(j + 2)
                        g_, v_ = wg[j % 3], wv_[j % 3]
                        gA = K.ps()
                        for c in range(DC):
                            K.mm(gA[:], g_[:, c * 128:(c + 1) * 128], hh[:, c, 0:TT], c == 0, c == DC - 1, [g_, hh], [gA])
                        gB = K.ps()
                        for c in range(DC):
                            K.mm(gB[:, 0:2], g_[:, c * 128:(c + 1) * 128], hh[:, c, TT:W], c == 0, c == DC - 1, [g_, hh], [gB])
                        vP = K.ps()
                        for c in range(DC):
                            K.mm(vP[:], v_[:, c * 128:(c + 1) * 128], hh[:, c, 1:TT + 1], c == 0, c == DC - 1, [v_, hh], [vP])
                        G_, cen_, sg_ = G[j % 2], cen[j % 2], sg[j % 2]
                        K.copy(act, G_[:, 0:TT], gA[:], [gA], [G_])
                        K.copy(act, G_[:, TT:W], gB[:, 0:2], [gB], [G_])
                        K.actf(cen_[:], G_[:, 1:TT + 1], AF.Identity, [G_, vec], [cen_],
                               bias=V(VL["fcb"] + j), scale=V(VL["fcw"] + FC + j))
                        K.stt(dve, cen_[:], G_[:, 0:TT], V(VL["fcw"] + j), cen_[:], ALU.mult, ALU.add,
                              [G_, vec, cen_], [cen_])
                        K.stt(dve, cen_[:], G_[:, 2:W], V(VL["fcw"] + 2 * FC + j), cen_[:], ALU.mult, ALU.add,
                              [G_, vec, cen_], [cen_])
                        K.actf(sg_[:], cen_[:], AF.Silu, [cen_], [sg_])
                        K.tt(dve, actb[:, j, :], sg_[:], vP[:], ALU.mult, [sg_, vP], [actb])
                        if bg is not None:
                            next(bg, None)
                        if cgen is not None:
                            next(cgen, None)
                    K.dma(pool, [(wd[0][:], wb_dn[0])], writes=[wd[0]])
                    xsrc = xmid[s]

                    def loadX(dc):
                        K.dma(sp, [(xr[dc % 3][:], xsrc[dc * 128:(dc + 1) * 128, cs])], writes=[xr[dc % 3]])

                    loadX(0)
                    loadX(1)
                    for dc in range(DC):
                        if dc + 1 < DC:
                            K.dma(pool, [(wd[(dc + 1) % 2][:], wb_dn[dc + 1])], writes=[wd[(dc + 1) % 2]])
                        if dc + 2 < DC:
                            loadX(dc + 2)
                        d_ = wd[dc % 2]
                        p = K.ps()
                        for j in range(FC):
                            K.mm(p[:], d_[:, j * 128:(j + 1) * 128], actb[:, j, :], j == 0, j == FC - 1, [d_, actb], [p])
                        xo_ = xo[dc % 3]
                        K.tt(dve, xo_[:], p[:], xr[dc % 3][:], ALU.add, [p, xr[dc % 3]], [xo_])
                        K.dma(sp, [(Xout[s][dc * 128:(dc + 1) * 128, cs], xo_[:])], reads=[xo_])
                        if bg is not None:
                            next(bg, None)
                            next(bg, None)
                    if bg is not None:
                        for _ in bg:
                            pass
                if cgen is not None:
                    for _ in cgen:
                        pass
                K.barrier()
        K.barrier()
        print(f"[build] instructions emitted: {K.n_ins}")
    return nc


def _tile_lhsT(w, cols_list):
    Kd = w.shape[0]
    kc = Kd // 128
    outs = []
    for cols in cols_list:
        sel = w[:, cols]
        outs.append(sel.reshape(kc, 128, 128).transpose(1, 0, 2).reshape(128, kc * 128))
    return np.ascontiguousarray(np.stack(outs, 0))


def _pvec(v):
    return np.ascontiguousarray(v.reshape(-1, 128).T)


def make_consts(L):
    TC = L // 128
    NT = L // TT
    f32 = np.float32
    t = np.linspace(0.0, 1.0, L, dtype=f32)[:, None]
    w = (2.0 * math.pi * np.arange(L, dtype=f32)[:, None] / L).astype(f32)
    bands = np.linspace(1e-4, 8 - 1, 8, dtype=f32)[None, :]
    z = np.concatenate([t, np.cos(bands * w), -np.sin(bands * w)], axis=-1).astype(f32)
    max_decay = math.log(1e-2) / 0.3
    min_decay = math.log(1e-2) / 1.5
    deltas = np.linspace(min_decay, max_decay, HY, dtype=f32)
    dec = np.exp(-t * np.abs(deltas)[None, :]).astype(f32)
    tt_ = np.arange(L, dtype=np.float64)
    om = 2.0 * math.pi * (np.arange(L, dtype=np.float64) + 0.5) / (2.0 * L)
    ang = np.outer(tt_, om)
    Cf = np.cos(ang)
    Sf = np.sin(ang)
    dft_f = np.empty((TC, 2, 128, TC * 128), dtype=ml_dtypes.bfloat16)
    for fc in range(TC):
        for k, M_ in enumerate((Cf, Sf)):
            blk = M_[:, fc * 128:(fc + 1) * 128]
            dft_f[fc, k] = blk.reshape(TC, 128, 128).transpose(1, 0, 2).reshape(128, TC * 128).astype(ml_dtypes.bfloat16)
    dft_i = np.empty((NT, 2, 128, TC * TT), dtype=ml_dtypes.bfloat16)
    for n in range(NT):
        for k, M_ in enumerate((Cf / L, -Sf / L)):
            blk = M_[n * TT:(n + 1) * TT, :].T
            dft_i[n, k] = blk.reshape(TC, 128, TT).transpose(1, 0, 2).reshape(128, TC * TT).astype(ml_dtypes.bfloat16)
    freqs = (10000.0 ** (-np.arange(0, 64, 2, dtype=f32) / 64)).astype(f32)
    angr = (np.arange(L, dtype=f32)[:, None] * freqs[None, :]).astype(f32)
    cs_ = np.cos(angr.astype(np.float64)).T.astype(f32)
    sn_ = np.sin(angr.astype(np.float64)).T.astype(f32)
    rope = np.concatenate([cs_, cs_, -sn_, sn_], axis=0).astype(f32)
    p_ = np.arange(128)
    fold = (p_[:, None] % 64 == p_[None, :] % 64).astype(ml_dtypes.bfloat16)
    ident = np.eye(128).astype(ml_dtypes.bfloat16)
    pc = np.ones((4, 16), dtype=f32)
    for g, wdw in enumerate(POOL_WINDOWS):
        half = wdw // 2
        for k in range(16):
            tk = k if k < 8 else L - 16 + k
            cnt = min(tk + half, L) - max(tk - half, 0)
            pc[g, k] = wdw / cnt
    pcorr = np.broadcast_to(pc.reshape(1, 64), (128, 64)).copy()
    return {
        "zfeatT": np.ascontiguousarray(z.T),
        "decay": np.ascontiguousarray(dec.reshape(TC, 128, HY)),
        "dft_f": dft_f, "dft_i": dft_i, "ropeT": rope, "foldm": fold, "identm": ident, "pcorr": pcorr,
    }


def prep_weights(P, DEPTH, DFF):
    FC = DFF // 128
    VL = vec_layout(FC)
    f = lambda a: np.asarray(a, dtype=np.float32)
    out = {}
    std = lambda n0, n: [np.arange(n0 + 128 * j, n0 + 128 * (j + 1)) for j in range(n)]
    sw64 = lambda base: np.concatenate([np.arange(base + 32, base + 64), np.arange(base, base + 32)])
    cols_in = std(0, 10) + [np.concatenate([np.arange(1280, 1344), sw64(1280)])] + std(1344, 12)
    cols_uq = []
    for h in range(H):
        cols_uq.append(np.arange(192 * h, 192 * h + 128))
        cols_uq.append(np.concatenate([np.arange(192 * h + 128, 192 * h + 192), sw64(192 * h + 128)]))
    cols_kn = [np.arange(256 * h, 256 * h + 128) for h in range(H)]
    cols_v = np.concatenate([np.arange(256 * h + 128, 256 * h + 256) for h in range(H)])
    w_in, w_uq, w_ukn, w_ukv, pw, w_out, w_upg, w_upv, w_dn, vecs = [], [], [], [], [], [], [], [], [], []
    hb3, hvec = [], []
    for l in range(DEPTH):
        w_in.append(_tile_lhsT(f(P["w_in"][l]), cols_in))
        w_uq.append(_tile_lhsT(f(P["mla_w_uq"][l]), cols_uq))
        w_ukn.append(_tile_lhsT(f(P["mla_w_ukv"][l]), cols_kn))
        wv = f(P["mla_w_ukv"][l])[:, cols_v]
        w_ukv.append(wv.reshape(2, 128, 1024).transpose(1, 0, 2).reshape(128, 2048))
        pw.append(f(P["pool_w"][l]).transpose(1, 0, 2).reshape(128, 4 * 128))
        w_out.append(_tile_lhsT(f(P["w_out"][l]), std(0, 16)))
        w_upg.append(_tile_lhsT(f(P["ffn_w_up"][l]), std(0, FC)))
        w_upv.append(_tile_lhsT(f(P["ffn_w_up"][l]), std(DFF, FC)))
        wd = f(P["ffn_w_down"][l])
        w_dn.append(wd.reshape(FC, 128, 16, 128).transpose(2, 1, 0, 3).reshape(16, 128, FC * 128))
        v = np.zeros((128, VL["NV"]), dtype=np.float32)
        v[:, VL["attn"]:VL["attn"] + 16] = _pvec(f(P["attn_norm_g"][l]))
        v[:, VL["ffn"]:VL["ffn"] + 16] = _pvec(f(P["ffn_norm_g"][l]))
        v[:, VL["grp"]:VL["grp"] + 16] = _pvec(f(P["grp_norm_g"][l]))
        v[:, VL["ps"]:VL["ps"] + 4] = _pvec(f(P["pool_scale"][l]))
        v[:, VL["qn"]:VL["qn"] + 4] = _pvec(f(P["mla_q_norm_g"][l]))
        v[:, VL["kvn"]:VL["kvn"] + 2] = _pvec(f(P["mla_kv_norm_g"][l]))
        qh = f(P["mla_q_head_norm_g"][l])
        kh = f(P["mla_k_head_norm_g"][l])
        swg = lambda g: np.concatenate([g[128:192], g[160:192], g[128:160]])
        v[:, VL["qhA"]] = qh[:128]
        v[:, VL["qhB"]] = swg(qh)
        v[:, VL["khA"]] = kh[:128]
        v[:, VL["khB"]] = swg(kh)
        hw = f(P["hy_conv_w"][l])
        for tap in range(3):
            v[:, VL["hyw"] + tap * 12:VL["hyw"] + (tap + 1) * 12] = _pvec(hw[tap])
        v[:, VL["hyb"]:VL["hyb"] + 12] = _pvec(f(P["hy_conv_b"][l]))
        v[:, VL["hybias"]:VL["hybias"] + 4] = _pvec(f(P["hy_bias"][l]))
        fw = f(P["ffn_conv_w"][l])
        for tap in range(3):
            v[:, VL["fcw"] + tap * FC:VL["fcw"] + (tap + 1) * FC] = _pvec(fw[tap])
        v[:, VL["fcb"]:VL["fcb"] + FC] = _pvec(f(P["ffn_conv_b"][l]))
        vecs.append(v)
        hb3.append(np.broadcast_to(f(P["hy_filt_b3"][l])[None, :], (128, 2 * HY)).copy())
        hv = np.zeros((FH, 4), dtype=np.float32)
        hv[:, 0] = f(P["hy_filt_b1"][l])
        hv[:, 1] = f(P["hy_filt_freq"][l])
        hv[:, 2] = f(P["hy_filt_b2"][l])
        hvec.append(hv)
    st_ = lambda lst: np.ascontiguousarray(np.stack(lst, 0))
    out.update(w_in=st_(w_in), w_uq=st_(w_uq), w_ukn=st_(w_ukn), w_ukv=st_(w_ukv), pool_w=st_(pw), w_out=st_(w_out),
               w_upg=st_(w_upg), w_upv=st_(w_upv), w_dn=st_(w_dn), vecs=st_(vecs), hf_b3=st_(hb3), hvec=st_(hvec),
               hf_w1=np.ascontiguousarray(f(P["hy_filt_w1"][:DEPTH])), hf_w2=np.ascontiguousarray(f(P["hy_filt_w2"][:DEPTH])),
               hf_w3=np.ascontiguousarray(f(P["hy_filt_w3"][:DEPTH])))
    return out


def run(cfg, x_all, P, n_cores, trace=False):
    L, NSEQ, DEPTH, DFF = cfg["L"], cfg["NSEQ"], cfg["DEPTH"], cfg["DFF"]
    nc = build(cfg)
    shared = prep_weights(P, DEPTH, DFF)
    shared.update(make_consts(L))
    in_maps = []
    for i in range(n_cores):
        m = dict(shared)
        m["xT"] = np.ascontiguousarray(np.asarray(x_all[i * NSEQ:(i + 1) * NSEQ], dtype=np.float32).transpose(0, 2, 1))
        in_maps.append(m)
    res = run_bass_kernel_spmd(nc, in_maps, core_ids=list(range(n_cores)), trace=trace)
    ys = [np.asarray(r["yT"]).transpose(0, 2, 1) for r in res.results]
    return np.concatenate(ys, 0), res


def kernel(**inputs):
    xp = np.asarray(inputs["x_prompt"], dtype=np.float32)
    xsm = np.asarray(inputs["x_sample"], dtype=np.float32)
    nb = xp.shape[0]
    x_all = np.concatenate([xp, xsm], 0)
    nseq = x_all.shape[0] // N_CORES
    cfg = {"L": x_all.shape[1], "NSEQ": nseq, "DEPTH": inputs["w_in"].shape[0], "DFF": inputs["ffn_conv_b"].shape[1]}
    P = {k: v for k, v in inputs.items() if k not in ("x_prompt", "x_sample")}
    y_all, _ = run(cfg, x_all, P, N_CORES)
    y_all = np.ascontiguousarray(y_all, dtype=np.float32)
    return (np.ascontiguousarray(y_all[:nb]), np.ascontiguousarray(y_all[nb:]))
```
